# Optimizing a Trainium2 kernel written in Bass

```python
import math
import jax
import jax.numpy as jnp
from jax import lax
import numpy as np

D_MODEL = 1024
BATCH = 16
SEQ = 4096
DEPTH = 4

CHUNK = 64
Q_BLOCK = 128
ROPE_THETA = 10000.0
NORM_EPS = 1e-6
F32 = jnp.float32

SSM_HEAD_DIM = 64
SSM_HEADS = D_MODEL // SSM_HEAD_DIM
SSM_INNER = SSM_HEADS * SSM_HEAD_DIM
SSM_STATE = 128
SSM_GROUPS = 2
CONV_WIDTH = 4
SSM_CONV_CH = SSM_INNER + 2 * SSM_GROUPS * SSM_STATE

DIFF_HEAD_DIM = 64
DIFF_HEADS = D_MODEL // (2 * DIFF_HEAD_DIM)
DIFF_WIDTH = DIFF_HEADS * 2 * DIFF_HEAD_DIM

EVEN_SIZES = (SSM_INNER, SSM_CONV_CH, SSM_HEADS, DIFF_WIDTH, DIFF_WIDTH, DIFF_WIDTH)
EVEN_IN = SSM_INNER + SSM_CONV_CH + SSM_HEADS + 3 * DIFF_WIDTH
EVEN_MIX = SSM_INNER + DIFF_WIDTH

S5_WIDTH = D_MODEL // 2
S5_GROUP = 16
S5_GROUPS = S5_WIDTH // S5_GROUP
S5_STATE = 64

MLA_HEADS = 8
MLA_NOPE = 64
MLA_ROPE = 32
MLA_V = 64
MLA_Q_RANK = 384
MLA_KV_RANK = 256
MLA_WIDTH = MLA_HEADS * MLA_V

ODD_SIZES = (S5_WIDTH, MLA_Q_RANK, MLA_KV_RANK, MLA_ROPE)
ODD_IN = S5_WIDTH + MLA_Q_RANK + MLA_KV_RANK + MLA_ROPE
ODD_MIX = S5_WIDTH + MLA_WIDTH

FFN_HIDDEN = ((8 * D_MODEL + 3 * 256 - 1) // (3 * 256)) * 256
N_EVEN = (DEPTH + 1) // 2
N_ODD = DEPTH // 2

kernel_name = 'hybrid_ssd_diffattn_s5_mla_encoder'


def rms_norm(x, g):
    xf = x.astype(F32)
    y = xf * lax.rsqrt(jnp.mean(xf * xf, axis=-1, keepdims=True) + NORM_EPS)
    return (y * g.astype(F32)).astype(x.dtype)


def split_cols(x, sizes):
    cuts, acc = [], 0
    for s in sizes[:-1]:
        acc += s
        cuts.append(acc)
    return jnp.split(x, cuts, axis=-1)


def rope(x, pos):
    d = x.shape[-1]
    half = d // 2
    inv_freq = ROPE_THETA ** (-jnp.arange(half, dtype=F32) * (2.0 / d))
    ang = pos.astype(F32)[:, :, None] * inv_freq
    cos = jnp.cos(ang)[:, :, None, :]
    sin = jnp.sin(ang)[:, :, None, :]
    xf = x.astype(F32)
    x1, x2 = xf[..., :half], xf[..., half:]
    return jnp.concatenate([x1 * cos - x2 * sin, x2 * cos + x1 * sin], axis=-1).astype(x.dtype)


def chunk_causal_mask(start, seq_len):
    qi = start + jnp.arange(Q_BLOCK, dtype=jnp.int32)
    kj = jnp.arange(seq_len, dtype=jnp.int32)
    return (kj[None, :] // CHUNK) <= (qi[:, None] // CHUNK)


def masked_softmax(s, mask):
    return jax.nn.softmax(jnp.where(mask, s.astype(F32), -jnp.inf), axis=-1)


def sweep_query_blocks(fn, *qs):
    bsz, s = qs[0].shape[:2]
    nb = s // Q_BLOCK
    blocks = tuple(jnp.swapaxes(q.reshape(bsz, nb, Q_BLOCK, *q.shape[2:]), 0, 1) for q in qs)
    starts = jnp.arange(nb, dtype=jnp.int32) * Q_BLOCK
    out = lax.map(lambda a: fn(*a), (starts, *blocks))
    out = jnp.swapaxes(out, 0, 1)
    return out.reshape(bsz, s, *out.shape[3:])


def causal_depthwise_conv(x, w, b):
    k, ch = w.shape
    y = lax.conv_general_dilated(x, w[:, None, :].astype(x.dtype), window_strides=(1,),
                                 padding=[(k - 1, 0)], dimension_numbers=('NWC', 'WIO', 'NWC'),
                                 feature_group_count=ch)
    return y + b.astype(x.dtype)


def ssd_chunked_scan(x, dt, a, b, c):
    bsz, s, h, p = x.shape
    g, n = b.shape[2], b.shape[3]
    r = h // g
    nc = s // CHUNK
    x = x.reshape(bsz, nc, CHUNK, g, r, p)
    dt = dt.reshape(bsz, nc, CHUNK, g, r)
    b = b.reshape(bsz, nc, CHUNK, g, n)
    c = c.reshape(bsz, nc, CHUNK, g, n)
    xdt = x * dt[..., None]
    a_cs = jnp.cumsum(dt * a.reshape(g, r), axis=2)
    seg = a_cs[:, :, :, None] - a_cs[:, :, None, :]
    causal = jnp.tril(jnp.ones((CHUNK, CHUNK), dtype=bool))[:, :, None, None]
    decay = jnp.exp(jnp.where(causal, seg, -jnp.inf))
    cb = jnp.einsum('bclgn,bcsgn->bclsg', c, b)
    y_diag = jnp.einsum('bclsgr,bcsgrp->bclgrp', cb[..., None] * decay, xdt)
    decay_states = jnp.exp(a_cs[:, :, -1:] - a_cs)
    states = jnp.einsum('bclgn,bclgrp->bcgrpn', b, decay_states[..., None] * xdt)
    chunk_decay = jnp.exp(a_cs[:, :, -1])

    def step(hs, inp):
        s_c, d_c = inp
        return hs * d_c[..., None, None] + s_c, hs

    _, prev = lax.scan(step, jnp.zeros_like(states[:, 0]),
                       (jnp.moveaxis(states, 1, 0), jnp.moveaxis(chunk_decay, 1, 0)))
    prev = jnp.moveaxis(prev, 0, 1)
    y_off = jnp.einsum('bclgn,bcgrpn->bclgrp', c, prev) * jnp.exp(a_cs)[..., None]
    return (y_diag + y_off).reshape(bsz, s, h, p)


def mamba2_group(z, xbc, dt_raw, conv_w, conv_b, dt_bias, a_log, d_skip, gate_norm):
    bsz, s, _ = z.shape
    xbc = jax.nn.silu(causal_depthwise_conv(xbc, conv_w, conv_b))
    xs, bs, cs = split_cols(xbc, (SSM_INNER, SSM_GROUPS * SSM_STATE, SSM_GROUPS * SSM_STATE))
    xs = xs.reshape(bsz, s, SSM_HEADS, SSM_HEAD_DIM).astype(F32)
    dt = jax.nn.softplus(dt_raw.astype(F32) + dt_bias.astype(F32))
    a = -jnp.exp(a_log.astype(F32))
    y = ssd_chunked_scan(xs, dt, a,
                         bs.reshape(bsz, s, SSM_GROUPS, SSM_STATE).astype(F32),
                         cs.reshape(bsz, s, SSM_GROUPS, SSM_STATE).astype(F32))
    y = y + d_skip.astype(F32)[:, None] * xs
    y = y.reshape(bsz, s, SSM_INNER) * jax.nn.silu(z.astype(F32))
    return rms_norm(y, gate_norm).astype(z.dtype)


def diff_attention_group(q, k, v, pos, layer, lambdas, subln):
    bsz, s, _ = q.shape
    hh, d = DIFF_HEADS, DIFF_HEAD_DIM
    q = rope(q.reshape(bsz, s, 2 * hh, d), pos).reshape(bsz, s, hh, 2, d)
    k = rope(k.reshape(bsz, s, 2 * hh, d), pos).reshape(bsz, s, hh, 2, d)
    vf = v.reshape(bsz, s, hh, 2 * d).astype(F32)
    k1, k2 = k[:, :, :, 0], k[:, :, :, 1]
    lam_init = 0.8 - 0.6 * math.exp(-0.3 * layer)
    lf = lambdas.astype(F32)
    lam = jnp.exp(jnp.sum(lf[0] * lf[1])) - jnp.exp(jnp.sum(lf[2] * lf[3])) + lam_init
    scale = d ** -0.5

    def block(start, q1b, q2b):
        mask = chunk_causal_mask(start, s)
        p1 = masked_softmax(jnp.einsum('bqhd,bkhd->bhqk', q1b, k1, preferred_element_type=F32) * scale, mask)
        p2 = masked_softmax(jnp.einsum('bqhd,bkhd->bhqk', q2b, k2, preferred_element_type=F32) * scale, mask)
        return jnp.einsum('bhqk,bkhe->bqhe', p1 - lam * p2, vf)

    o = sweep_query_blocks(block, q[:, :, :, 0], q[:, :, :, 1])
    o = rms_norm(o, subln) * (1.0 - lam_init)
    return o.reshape(bsz, s, DIFF_WIDTH).astype(q.dtype)


def s5_combine(e1, e2):
    a1, b1 = e1
    a2, b2 = e2
    return a1 * a2, a2 * b1 + b2


def s5_group(u, lam_re, lam_im, log_step, b_re, b_im, c_re, c_im, d_skip, w_glu, b_glu):
    bsz, s, _ = u.shape
    ug = u.reshape(bsz, s, S5_GROUPS, S5_GROUP).astype(F32)
    lam = lax.complex(lam_re.astype(F32), lam_im.astype(F32))
    delta = jnp.exp(log_step.astype(F32))[:, None]
    a_bar = jnp.exp(lam * delta)
    b_bar = ((a_bar - 1.0) / lam)[..., None] * lax.complex(b_re.astype(F32), b_im.astype(F32))
    bu = jnp.einsum('bsgh,gph->bsgp', ug.astype(jnp.complex64), b_bar)
    a_el = jnp.broadcast_to(a_bar, bu.shape)
    _, state = lax.associative_scan(s5_combine, (a_el, bu), axis=1)
    cm = lax.complex(c_re.astype(F32), c_im.astype(F32))
    y = jnp.real(jnp.einsum('bsgp,ghp->bsgh', state, cm)) + d_skip.astype(F32).reshape(S5_GROUPS, S5_GROUP) * ug
    y = jax.nn.gelu(y.reshape(bsz, s, S5_WIDTH))
    y = y * jax.nn.sigmoid(y @ w_glu.astype(F32) + b_glu.astype(F32))
    return y.astype(u.dtype)


def mla_group(c_q, c_kv, k_rope, pos, q_norm, w_uq, kv_norm, w_ukv):
    bsz, s, _ = c_q.shape
    q = (rms_norm(c_q, q_norm) @ w_uq).reshape(bsz, s, MLA_HEADS, MLA_NOPE + MLA_ROPE)
    q_nope, q_rot = q[..., :MLA_NOPE], rope(q[..., MLA_NOPE:], pos)
    kv = (rms_norm(c_kv, kv_norm) @ w_ukv).reshape(bsz, s, MLA_HEADS, MLA_NOPE + MLA_V)
    k_nope, vf = kv[..., :MLA_NOPE], kv[..., MLA_NOPE:].astype(F32)
    k_rot = rope(k_rope[:, :, None, :], pos)[:, :, 0, :]
    scale = (MLA_NOPE + MLA_ROPE) ** -0.5

    def block(start, qn, qr):
        sc = (jnp.einsum('bqhd,bkhd->bhqk', qn, k_nope, preferred_element_type=F32)
              + jnp.einsum('bqhr,bkr->bhqk', qr, k_rot, preferred_element_type=F32))
        p = masked_softmax(sc * scale, chunk_causal_mask(start, s))
        return jnp.einsum('bhqk,bkhe->bqhe', p, vf)

    o = sweep_query_blocks(block, q_nope, q_rot)
    return o.reshape(bsz, s, MLA_WIDTH).astype(c_q.dtype)


def even_mixer(h, pos, layer, w_in, conv_w, conv_b, dt_bias, a_log, d_skip, gate_norm, lambdas, subln, w_out):
    z, xbc, dt_raw, q, k, v = split_cols(h @ w_in, EVEN_SIZES)
    y_a = mamba2_group(z, xbc, dt_raw, conv_w, conv_b, dt_bias, a_log, d_skip, gate_norm)
    y_b = diff_attention_group(q, k, v, pos, layer, lambdas, subln)
    return jnp.concatenate([y_a, y_b], axis=-1) @ w_out


def odd_mixer(h, pos, w_in, lam_re, lam_im, log_step, b_re, b_im, c_re, c_im, d_skip, w_glu, b_glu,
              q_norm, w_uq, kv_norm, w_ukv, w_out):
    u, c_q, c_kv, k_rope = split_cols(h @ w_in, ODD_SIZES)
    y_c = s5_group(u, lam_re, lam_im, log_step, b_re, b_im, c_re, c_im, d_skip, w_glu, b_glu)
    y_d = mla_group(c_q, c_kv, k_rope, pos, q_norm, w_uq, kv_norm, w_ukv)
    return jnp.concatenate([y_c, y_d], axis=-1) @ w_out


def swiglu(x, wg, wu, wd):
    return (jax.nn.silu(x @ wg) * (x @ wu)) @ wd


def setup_inputs(seed: int = 0) -> dict:
    key = jax.random.key(seed)
    ks = iter(jax.random.split(key, 48))

    def nrm(shape, scale):
        return jax.random.normal(next(ks), shape, F32) * scale

    def gain(shape):
        return 1.0 + nrm(shape, 0.01)

    def log_uniform(shape, lo, hi):
        return jax.random.uniform(next(ks), shape, F32, minval=math.log(lo), maxval=math.log(hi))

    x = nrm((BATCH, SEQ, D_MODEL), 1.0)
    start = jax.random.randint(next(ks), (BATCH,), 0, 1024, dtype=jnp.int32) * CHUNK
    positions = start[:, None] + jnp.arange(SEQ, dtype=jnp.int32)[None, :]
    norm_mix = gain((DEPTH, D_MODEL))
    ev_w_in = nrm((N_EVEN, D_MODEL, EVEN_IN), D_MODEL ** -0.5)
    ev_conv_w = nrm((N_EVEN, CONV_WIDTH, SSM_CONV_CH), CONV_WIDTH ** -0.5)
    ev_conv_b = nrm((N_EVEN, SSM_CONV_CH), 0.01)
    dt0 = jnp.exp(log_uniform((N_EVEN, SSM_HEADS), 1e-3, 1e-1))
    ev_dt_bias = dt0 + jnp.log(-jnp.expm1(-dt0))
    ev_a_log = jnp.log(jax.random.uniform(next(ks), (N_EVEN, SSM_HEADS), F32, minval=1.0, maxval=16.0))
    ev_d_skip = gain((N_EVEN, SSM_HEADS))
    ev_gate_norm = gain((N_EVEN, SSM_INNER))
    ev_lambdas = nrm((N_EVEN, 4, DIFF_HEAD_DIM), 0.1)
    ev_subln = gain((N_EVEN, 2 * DIFF_HEAD_DIM))
    ev_w_out = nrm((N_EVEN, EVEN_MIX, D_MODEL), EVEN_MIX ** -0.5)
    od_w_in = nrm((N_ODD, D_MODEL, ODD_IN), D_MODEL ** -0.5)
    od_lam_re = -0.5 + nrm((N_ODD, S5_GROUPS, S5_STATE), 0.01)
    od_lam_im = math.pi * jnp.arange(S5_STATE, dtype=F32) + nrm((N_ODD, S5_GROUPS, S5_STATE), 0.01)
    od_log_step = log_uniform((N_ODD, S5_GROUPS), 1e-3, 1e-1)
    od_b_re = nrm((N_ODD, S5_GROUPS, S5_STATE, S5_GROUP), (2 * S5_GROUP) ** -0.5)
    od_b_im = nrm((N_ODD, S5_GROUPS, S5_STATE, S5_GROUP), (2 * S5_GROUP) ** -0.5)
    od_c_re = nrm((N_ODD, S5_GROUPS, S5_GROUP, S5_STATE), (2 * S5_STATE) ** -0.5)
    od_c_im = nrm((N_ODD, S5_GROUPS, S5_GROUP, S5_STATE), (2 * S5_STATE) ** -0.5)
    od_d_skip = gain((N_ODD, S5_WIDTH))
    od_w_glu = nrm((N_ODD, S5_WIDTH, S5_WIDTH), S5_WIDTH ** -0.5)
    od_b_glu = nrm((N_ODD, S5_WIDTH), 0.01)
    od_q_norm = gain((N_ODD, MLA_Q_RANK))
    od_w_uq = nrm((N_ODD, MLA_Q_RANK, MLA_HEADS * (MLA_NOPE + MLA_ROPE)), MLA_Q_RANK ** -0.5)
    od_kv_norm = gain((N_ODD, MLA_KV_RANK))
    od_w_ukv = nrm((N_ODD, MLA_KV_RANK, MLA_HEADS * (MLA_NOPE + MLA_V)), MLA_KV_RANK ** -0.5)
    od_w_out = nrm((N_ODD, ODD_MIX, D_MODEL), ODD_MIX ** -0.5)
    norm_ffn = gain((DEPTH, D_MODEL))
    ffn_gate = nrm((DEPTH, D_MODEL, FFN_HIDDEN), D_MODEL ** -0.5)
    ffn_up = nrm((DEPTH, D_MODEL, FFN_HIDDEN), D_MODEL ** -0.5)
    ffn_down = nrm((DEPTH, FFN_HIDDEN, D_MODEL), FFN_HIDDEN ** -0.5)
    norm_final = gain((D_MODEL,))
    return {'x': x, 'positions': positions, 'norm_mix': norm_mix,
            'ev_w_in': ev_w_in, 'ev_conv_w': ev_conv_w, 'ev_conv_b': ev_conv_b,
            'ev_dt_bias': ev_dt_bias, 'ev_a_log': ev_a_log, 'ev_d_skip': ev_d_skip,
            'ev_gate_norm': ev_gate_norm, 'ev_lambdas': ev_lambdas, 'ev_subln': ev_subln,
            'ev_w_out': ev_w_out,
            'od_w_in': od_w_in, 'od_lam_re': od_lam_re, 'od_lam_im': od_lam_im,
            'od_log_step': od_log_step, 'od_b_re': od_b_re, 'od_b_im': od_b_im,
            'od_c_re': od_c_re, 'od_c_im': od_c_im, 'od_d_skip': od_d_skip,
            'od_w_glu': od_w_glu, 'od_b_glu': od_b_glu, 'od_q_norm': od_q_norm,
            'od_w_uq': od_w_uq, 'od_kv_norm': od_kv_norm, 'od_w_ukv': od_w_ukv,
            'od_w_out': od_w_out,
            'norm_ffn': norm_ffn, 'ffn_gate': ffn_gate, 'ffn_up': ffn_up, 'ffn_down': ffn_down,
            'norm_final': norm_final}


def reference(x, positions, norm_mix,
              ev_w_in, ev_conv_w, ev_conv_b, ev_dt_bias, ev_a_log, ev_d_skip,
              ev_gate_norm, ev_lambdas, ev_subln, ev_w_out,
              od_w_in, od_lam_re, od_lam_im, od_log_step, od_b_re, od_b_im,
              od_c_re, od_c_im, od_d_skip, od_w_glu, od_b_glu, od_q_norm,
              od_w_uq, od_kv_norm, od_w_ukv, od_w_out,
              norm_ffn, ffn_gate, ffn_up, ffn_down, norm_final):
    h = x
    for i in range(DEPTH):
        j = i // 2
        hn = rms_norm(h, norm_mix[i])
        if i % 2 == 0:
            mix = even_mixer(hn, positions, i, ev_w_in[j], ev_conv_w[j], ev_conv_b[j], ev_dt_bias[j],
                             ev_a_log[j], ev_d_skip[j], ev_gate_norm[j], ev_lambdas[j], ev_subln[j],
                             ev_w_out[j])
        else:
            mix = odd_mixer(hn, positions, od_w_in[j], od_lam_re[j], od_lam_im[j], od_log_step[j],
                            od_b_re[j], od_b_im[j], od_c_re[j], od_c_im[j], od_d_skip[j],
                            od_w_glu[j], od_b_glu[j], od_q_norm[j], od_w_uq[j], od_kv_norm[j],
                            od_w_ukv[j], od_w_out[j])
        h = h + mix
        h = h + swiglu(rms_norm(h, norm_ffn[i]), ffn_gate[i], ffn_up[i], ffn_down[i])
    return rms_norm(h, norm_final)
```

```python
from contextlib import ExitStack
import numpy as np
import concourse.bass as bass
import concourse.mybir as mybir
from concourse.bass_utils import run_bass_kernel_spmd

F32 = mybir.dt.float32
BF16 = mybir.dt.bfloat16
I32 = mybir.dt.int32
AF = mybir.ActivationFunctionType
ALU = mybir.AluOpType
AX = mybir.AxisListType

D = 1024
KC = 8
FH = 2816
HC = 22
EPS = 1e-6
EVROW = 48 + 1024 + 256 + 128
NCORES = 8


class Buf:
    __slots__ = ("t", "name", "excl", "st")

    def __init__(self, t, name="", excl=False, st=None):
        self.t = t
        self.name = name
        self.excl = excl
        self.st = st if st is not None else [None, {}]

    @property
    def w(self):
        return self.st[0]

    @w.setter
    def w(self, v):
        self.st[0] = v

    @property
    def r(self):
        return self.st[1]

    @r.setter
    def r(self, v):
        self.st[1] = v

    def view(self, ap, name=""):
        return Buf(ap, name or self.name, self.excl, self.st)

    def __getitem__(self, idx):
        return self.t[idx]


class Eng:
    def __init__(self, name, e, sem):
        self.name = name
        self.e = e
        self.sem = sem
        self.count = 0
        self.seen = {}


class KB:
    def __init__(self, nc, es):
        self.nc = nc
        self.es = es
        mk = lambda n: es.enter_context(nc.semaphore(n))
        self.pe = Eng("pe", nc.tensor, mk("s_pe"))
        self.act = Eng("act", nc.scalar, mk("s_act"))
        self.dve = Eng("dve", nc.vector, mk("s_dve"))
        self.pool = Eng("pool", nc.gpsimd, mk("s_pool"))
        self.sp = Eng("sp", nc.sync, mk("s_sp"))
        self.engs = [self.pe, self.act, self.dve, self.pool, self.sp]
        self.dsem = {}
        for q, n in ((self.sp, 24), (self.pool, 16), (self.act, 8)):
            self.dsem[q.name] = [[mk(f"d_{q.name}{i}"), 0] for i in range(n)]
        self.dptr = {q: 0 for q in self.dsem}
        self.nwait = 0
        self.ninst = 0

    def _wait(self, eng, deps):
        for (s, v) in deps:
            if s is eng.sem and eng is self.pe:
                continue
            if eng.seen.get(s, 0) < v:
                eng.e.wait_ge(s, v)
                eng.seen[s] = v
                self.nwait += 1

    @staticmethod
    def _deps(reads, writes):
        deps = {}
        for b in reads:
            if b.w is not None:
                s, v = b.w
                if deps.get(s, 0) < v:
                    deps[s] = v
            if b.excl:
                for s, v in b.r.items():
                    if deps.get(s, 0) < v:
                        deps[s] = v
        for b in writes:
            if b.w is not None:
                s, v = b.w
                if deps.get(s, 0) < v:
                    deps[s] = v
            for s, v in b.r.items():
                if deps.get(s, 0) < v:
                    deps[s] = v
        return list(deps.items())

    @staticmethod
    def _mark(tok, reads, writes):
        s, v = tok
        for b in reads:
            b.r[s] = v
        for b in writes:
            b.w = tok
            b.r = {}

    def op(self, eng, fn, reads=(), writes=()):
        self._wait(eng, self._deps(reads, writes))
        ins = fn()
        eng.count += 1
        ins.then_inc(eng.sem, 1)
        self._mark((eng.sem, eng.count), reads, writes)
        self.ninst += 1
        return ins

    def dma(self, q, out_ap, in_ap, reads=(), writes=()):
        self._wait(q, self._deps(reads, writes))
        pool = self.dsem[q.name]
        i = self.dptr[q.name]
        self.dptr[q.name] = (i + 1) % len(pool)
        ent = pool[i]
        if ent[1] > 0:
            self._wait(q, [(ent[0], ent[1])])
        q.e.dma_start(out=out_ap, in_=in_ap).then_inc(ent[0], 16)
        ent[1] += 16
        self._mark((ent[0], ent[1]), reads, writes)
        self.ninst += 1

    def barrier(self):
        toks = [(e.sem, e.count) for e in self.engs if e.count > 0]
        for pool in self.dsem.values():
            toks += [(s, v) for s, v in pool if v > 0]
        for e in self.engs:
            for (s, v) in toks:
                if e.seen.get(s, 0) < v:
                    e.e.wait_ge(s, v)
                    e.seen[s] = v
                    self.nwait += 1

    def sb(self, es, name, shape, dt):
        return Buf(es.enter_context(self.nc.sbuf_tensor(name, list(shape), dt)), name)

    def ps(self, es, name, shape, dt=F32):
        return Buf(es.enter_context(self.nc.psum_tensor(name, list(shape), dt)), name, excl=True)


class PsPool:
    def __init__(self, kb, es, n, prefix):
        self.banks = [kb.ps(es, f"{prefix}{i}", [128, 512], F32) for i in range(n)]
        self.free = list(self.banks)

    def get(self):
        assert self.free, "PSUM pool exhausted"
        return self.free.pop(0)

    def put(self, b):
        self.free.append(b)


def bf(ap):
    return ap.bitcast(BF16)


def build(cfg):
    T = cfg["T"]
    SEQ = cfg["SEQ"]
    NSEQ = T // SEQ
    NB = T // 512
    layers = cfg["layers"]
    nc = bass.Bass("TRN2", target_bir_lowering=False)
    dram_in = lambda n, s, dt=F32: nc.dram_tensor(n, list(s), dt, kind="ExternalInput").ap()
    x_in = dram_in("x", [T, D])
    consts = dram_in("consts", [128, 1024])
    norm_ffn = dram_in("norm_ffn", [128, 4 * KC])
    norm_fin = dram_in("norm_final", [128, KC])
    ffn_g = dram_in("ffn_gate", [4, D, FH])
    ffn_u = dram_in("ffn_up", [4, D, FH])
    ffn_d = dram_in("ffn_down", [4, FH, D])
    posb = dram_in("posb", [128, T], I32)
    norm_mix = dram_in("norm_mix", [128, 4 * KC])
    ev_wqk = dram_in("ev_wqk", [2, D, 2048])
    ev_wv = dram_in("ev_wv", [2, D, 1024])
    ev_row = dram_in("ev_row", [2, 128, EVROW])
    ev_wxz = dram_in("ev_wxz", [2, D, 2576])
    ev_conv = dram_in("ev_conv", [2, 128, 60])
    iota_in = dram_in("iota", [128, 512])
    od_wu = dram_in("od_wu", [2, D, 512])
    od_wf = dram_in("od_wf", [2, D, 768])
    od_pt = dram_in("od_pt", [2, 128, 64])
    od_rowp = dram_in("od_rowp", [2, 128, 3 * 2048])
    od_bt = dram_in("od_bt", [2, 2, 128, 16, 128])
    od_ct = dram_in("od_ct", [2, 2, 128, 16, 128])
    od_wglu = dram_in("od_wglu", [2, 512, 512])
    od_wuq = dram_in("od_wuq", [2, 384, 1024])
    od_wuk = dram_in("od_wuk", [2, 256, 1024])
    od_wuv = dram_in("od_wuv", [2, 256, 512])
    od_nrm = dram_in("od_nrm", [2, 128, 5])
    od_wout = dram_in("od_wout", [2, D, D])
    ev_wout = dram_in("ev_wout", [2, 2048, D])
    out = nc.dram_tensor("out", [T, D], F32, kind="ExternalOutput").ap()
    hT_t = nc.dram_tensor("hT", [KC, 128, T], F32, kind="Internal").ap()
    mixT = nc.dram_tensor("mixT", [16, 128, T], BF16, kind="Internal").ap()
    qT = nc.dram_tensor("qT", [8, 128, T], BF16, kind="Internal").ap()
    kT = nc.dram_tensor("kT", [8, 128, T], BF16, kind="Internal").ap()
    vtm = nc.dram_tensor("vtm", [T, 1024], BF16, kind="Internal").ap()
    tabs = nc.dram_tensor("tabs", [4, 128, T], F32, kind="Internal").ap()

    with ExitStack() as es:
        kb = KB(nc, es)
        hT = [Buf(hT_t, f"hT{b}") for b in range(NB)]
        outb = Buf(out, "out")
        cst = kb.sb(es, "cst", [128, 1024], F32)
        kb.dma(kb.sp, cst[:, :], consts[:, :], writes=[cst])
        ident32 = cst[:, 0:128]
        identb = kb.sb(es, "identb", [128, 128], BF16)
        onesb = kb.sb(es, "onesb", [128, 128], BF16)
        kb.op(kb.dve, lambda: nc.vector.tensor_copy(out=identb[:, :], in_=cst[:, 0:128]), reads=[cst], writes=[identb])
        kb.op(kb.dve, lambda: nc.vector.memset(onesb[:, :], 1.0), writes=[onesb])
        epsc = kb.sb(es, "epsc", [128, 1], F32)
        kb.op(kb.dve, lambda: nc.vector.memset(epsc[:, :], EPS), writes=[epsc])
        nf = kb.sb(es, "nf", [128, 9 * KC], F32)
        kb.dma(kb.sp, nf[:, 0:4 * KC], norm_ffn[:, :], writes=[nf])
        kb.dma(kb.sp, nf[:, 4 * KC:5 * KC], norm_fin[:, :], writes=[nf])
        kb.dma(kb.sp, nf[:, 5 * KC:9 * KC], norm_mix[:, :], writes=[nf])
        halfpi = kb.sb(es, "halfpi", [128, 1], F32)
        kb.op(kb.dve, lambda: nc.vector.memset(halfpi[:, :], float(np.pi / 2)), writes=[halfpi])
        psw64 = kb.sb(es, "psw64", [128, 128], BF16)
        psw32 = kb.sb(es, "psw32", [128, 128], BF16)
        kb.op(kb.dve, lambda: nc.vector.tensor_copy(out=psw64[:, :], in_=cst[:, 256:384]), reads=[cst], writes=[psw64])
        kb.op(kb.dve, lambda: nc.vector.tensor_copy(out=psw32[:, :], in_=cst[:, 384:512]), reads=[cst], writes=[psw32])

        def copy_on(e, out_ap, in_ap, reads, writes):
            if e is kb.act:
                kb.op(e, lambda: nc.scalar.copy(out=out_ap, in_=in_ap), reads=reads, writes=writes)
            else:
                kb.op(e, lambda: e.e.tensor_copy(out=out_ap, in_=in_ap), reads=reads, writes=writes)

        def cast_load(w, src3, kcn, n, stg, rot=[0]):
            per = max(1, 4096 // n)
            for k0 in range(0, kcn, per):
                k1 = min(kcn, k0 + per)
                st = stg[rot[0] % len(stg)]
                sap = st.t[:, :, :].rearrange("p c t -> p (c t)")[:, 0:(k1 - k0) * n].rearrange("p (k n) -> p k n", n=n)
                kb.dma(kb.sp if rot[0] % 2 == 0 else kb.pool, sap, src3[:, k0:k1, :], writes=[st])
                copy_on(kb.act if rot[0] % 2 else kb.dve, w[:, k0:k1, :], sap, [st], [w])
                rot[0] += 1

        def rmsnorm_fm(pp, h32, gcol, hn, sq, rstd):
            for c in range(KC):
                kb.op(kb.act, lambda: nc.scalar.activation(out=sq[:, c, :], in_=h32[:, c, :], func=AF.Square),
                      reads=[h32], writes=[sq])
            p = pp.get()
            for c in range(KC):
                kb.op(kb.pe, lambda: nc.tensor.matmul(p[:, :], lhsT=onesb[:, :], rhs=sq[:, c, :], start=(c == 0), stop=(c == KC - 1)),
                      reads=[onesb, sq], writes=[p])
            kb.op(kb.act, lambda: nc.scalar.activation(out=rstd[:, :], in_=p[:, :], func=AF.Sqrt, scale=1.0 / D, bias=epsc[:, 0:1]),
                  reads=[p, epsc], writes=[rstd])
            pp.put(p)
            kb.op(kb.dve, lambda: nc.vector.reciprocal(out=rstd[:, :], in_=rstd[:, :]), reads=[rstd], writes=[rstd])
            for c in range(KC):
                kb.op(kb.dve, lambda: nc.vector.scalar_tensor_tensor(out=hn[:, c, :], in0=h32[:, c, :], scalar=gcol[:, c:c + 1], in1=rstd[:, :],
                                                                     op0=ALU.mult, op1=ALU.mult),
                      reads=[h32, rstd, nf], writes=[hn])

        with ExitStack() as ph:
            pp = PsPool(kb, ph, 4, "pre")
            xt = [kb.sb(ph, f"xt{i}", [128, 4, D], F32) for i in range(2)]
            ht = [kb.sb(ph, f"ht{i}", [128, KC, 512], F32) for i in range(2)]
            for b in range(NB):
                xb, hb = xt[b % 2], ht[b % 2]
                kb.dma(kb.sp, xb[:, :, :], x_in[b * 512:(b + 1) * 512, :].rearrange("(i p) d -> p i d", p=128), writes=[xb])
                for c in range(KC):
                    p = pp.get()
                    for i in range(4):
                        kb.op(kb.pe, lambda: nc.tensor.transpose(out=p[:, i * 128:(i + 1) * 128], in_=xb[:, i, c * 128:(c + 1) * 128], identity=ident32),
                              reads=[xb, cst], writes=[p])
                    e = kb.act if c % 2 else kb.dve
                    if e is kb.act:
                        kb.op(e, lambda: nc.scalar.copy(out=hb[:, c, :], in_=p[:, :]), reads=[p], writes=[hb])
                    else:
                        kb.op(e, lambda: nc.vector.tensor_copy(out=hb[:, c, :], in_=p[:, :]), reads=[p], writes=[hb])
                    pp.put(p)
                kb.dma(kb.pool, hT_t[:, :, b * 512:(b + 1) * 512].rearrange("c p t -> p c t"), hb[:, :, :], reads=[hb], writes=[hT[b]])
        kb.barrier()


        TWO_PI_HI = 6.28125
        TWO_PI_LO = float(2 * np.pi - 6.28125)
        with ExitStack() as ph:
            posi = [kb.sb(ph, f"posi{i}", [128, 512], I32) for i in range(2)]
            posf = [kb.sb(ph, f"posf{i}", [128, 512], F32) for i in range(2)]
            ang = kb.sb(ph, "ang", [128, 512], F32)
            ki = kb.sb(ph, "ki", [128, 512], I32)
            kf = kb.sb(ph, "kf", [128, 512], F32)
            rr = kb.sb(ph, "rr", [128, 512], F32)
            ab = kb.sb(ph, "ab", [128, 512], F32)
            s2 = kb.sb(ph, "s2", [128, 512], F32)
            c2 = kb.sb(ph, "c2", [128, 512], F32)
            sno = [kb.sb(ph, f"sno{i}", [128, 512], F32) for i in range(2)]
            cso = [kb.sb(ph, f"cso{i}", [128, 512], F32) for i in range(2)]
            n = 0
            for b in range(NB):
                pi_, pf = posi[b % 2], posf[b % 2]
                kb.dma(kb.sp, pi_[:, :], posb[:, b * 512:(b + 1) * 512], writes=[pi_])
                kb.op(kb.dve, lambda: nc.vector.tensor_copy(out=pf[:, :], in_=pi_[:, :]), reads=[pi_], writes=[pf])
                for ti in range(2):
                    sn_, cs_ = sno[n % 2], cso[n % 2]
                    n += 1
                    kb.op(kb.dve, lambda: nc.vector.tensor_scalar(out=ang[:, :], in0=pf[:, :], scalar1=cst[:, 128 + ti:129 + ti], scalar2=0.0, op0=ALU.mult, op1=ALU.add),
                          reads=[pf, cst], writes=[ang])
                    kb.op(kb.dve, lambda: nc.vector.tensor_scalar(out=ki[:, :], in0=ang[:, :], scalar1=float(1 / (2 * np.pi)), scalar2=0.0, op0=ALU.mult, op1=ALU.add),
                          reads=[ang], writes=[ki])
                    kb.op(kb.dve, lambda: nc.vector.tensor_copy(out=kf[:, :], in_=ki[:, :]), reads=[ki], writes=[kf])
                    kb.op(kb.dve, lambda: nc.vector.scalar_tensor_tensor(out=rr[:, :], in0=kf[:, :], scalar=-TWO_PI_HI, in1=ang[:, :], op0=ALU.mult, op1=ALU.add),
                          reads=[kf, ang], writes=[rr])
                    kb.op(kb.dve, lambda: nc.vector.scalar_tensor_tensor(out=rr[:, :], in0=kf[:, :], scalar=-TWO_PI_LO, in1=rr[:, :], op0=ALU.mult, op1=ALU.add),
                          reads=[kf, rr], writes=[rr])
                    kb.op(kb.dve, lambda: nc.vector.scalar_tensor_tensor(out=ab[:, :], in0=rr[:, :], scalar=-1.0, in1=rr[:, :], op0=ALU.mult, op1=ALU.max), reads=[rr], writes=[ab])
                    kb.op(kb.act, lambda: nc.scalar.activation(out=s2[:, :], in_=rr[:, :], func=AF.Sin, scale=0.5), reads=[rr], writes=[s2])
                    kb.op(kb.act, lambda: nc.scalar.activation(out=c2[:, :], in_=ab[:, :], func=AF.Sin, scale=-0.5, bias=halfpi[:, 0:1]),
                          reads=[ab, halfpi], writes=[c2])
                    kb.op(kb.dve, lambda: nc.vector.scalar_tensor_tensor(out=sn_[:, :], in0=s2[:, :], scalar=2.0, in1=c2[:, :], op0=ALU.mult, op1=ALU.mult),
                          reads=[s2, c2], writes=[sn_])
                    kb.op(kb.dve, lambda: nc.vector.tensor_tensor(out=cs_[:, :], in0=s2[:, :], in1=s2[:, :], op=ALU.mult), reads=[s2], writes=[cs_])
                    kb.op(kb.dve, lambda: nc.vector.tensor_scalar(out=cs_[:, :], in0=cs_[:, :], scalar1=-2.0, scalar2=1.0, op0=ALU.mult, op1=ALU.add),
                          reads=[cs_], writes=[cs_])
                    kb.dma(kb.pool, tabs[2 * ti][:, b * 512:(b + 1) * 512], cs_[:, :], reads=[cs_])
                    kb.dma(kb.pool, tabs[2 * ti + 1][:, b * 512:(b + 1) * 512], sn_[:, :], reads=[sn_])
        kb.barrier()

        def load_h(h32, b):
            kb.dma(kb.sp, h32[:, :, :], hT_t[:, :, b * 512:(b + 1) * 512].rearrange("c p t -> p c t"), reads=[hT[b]], writes=[h32])

        def rope_chunk(pp, p, psw, cs, sn, qbt, tA, tB, outap, outbuf):
            kb.op(kb.act, lambda: nc.scalar.copy(out=qbt[:, :], in_=p[:, :]), reads=[p], writes=[qbt])
            kb.op(kb.dve, lambda: nc.vector.tensor_tensor(out=tA[:, :], in0=cs[:, :], in1=p[:, :], op=ALU.mult), reads=[p, cs], writes=[tA])
            pp.put(p)
            p2 = pp.get()
            kb.op(kb.pe, lambda: nc.tensor.matmul(p2[:, :], lhsT=psw[:, :], rhs=qbt[:, :], start=True, stop=True), reads=[psw, qbt], writes=[p2])
            kb.op(kb.dve, lambda: nc.vector.tensor_tensor(out=tB[:, :], in0=sn[:, :], in1=p2[:, :], op=ALU.mult), reads=[p2, sn], writes=[tB])
            pp.put(p2)
            kb.op(kb.dve, lambda: nc.vector.tensor_tensor(out=outap, in0=tA[:, :], in1=tB[:, :], op=ALU.add), reads=[tA, tB], writes=[outbuf])


        def phase_e1a(li, j):
            with ExitStack() as ph:
                pp = PsPool(kb, ph, 8, f"e1a{li}_")
                wx = kb.sb(ph, f"wx{li}", [128, KC, 1536], BF16)
                wz = kb.sb(ph, f"wz{li}", [128, KC, 1040], BF16)
                h32s = [kb.sb(ph, f"sh32_{li}_{i}", [128, KC, 512], F32) for i in range(2)]
                hn = kb.sb(ph, f"shn{li}", [128, KC, 512], BF16)
                sq = kb.sb(ph, f"ssq{li}", [128, KC, 512], BF16)
                rstd = kb.sb(ph, f"srstd{li}", [128, 512], F32)
                cv = kb.sb(ph, f"scv{li}", [128, 60], F32)
                row = kb.sb(ph, f"srow{li}", [128, EVROW], F32)
                diag = kb.sb(ph, f"sdiag{li}", [128, 48, 128], BF16)
                aneg = kb.sb(ph, f"saneg{li}", [128, 16], F32)
                ones32 = kb.sb(ph, f"sones{li}", [128, 128], F32)
                onec = kb.sb(ph, f"sonec{li}", [128, 1], F32)
                nmb = kb.sb(ph, f"snmb{li}", [128, 4, 128], BF16)
                hs = kb.sb(ph, f"shs{li}", [128, 2, 512], F32)
                hsb = kb.sb(ph, f"shsb{li}", [128, 2, 512], BF16)
                xbcT = kb.sb(ph, f"sxbc{li}", [128, 12, 515], BF16)
                halo = kb.sb(ph, f"shalo{li}", [128, 12, 3], BF16)
                xcT = kb.sb(ph, f"sxc{li}", [128, 12, 512], BF16)
                zs = kb.sb(ph, f"szs{li}", [128, 1024], F32)
                sm = kb.sb(ph, f"ssm{li}", [128, 16, 16], F32)
                xdt = kb.sb(ph, f"sxdt{li}", [128, 1024], BF16)
                xD = kb.sb(ph, f"sxD{li}", [128, 1024], F32)
                xp = kb.sb(ph, f"sxp{li}", [128, 1024], BF16)
                btm = kb.sb(ph, f"sbtm{li}", [128, 256], BF16)
                daU = kb.sb(ph, f"sdaU{li}", [128, 16, 128], F32)
                Wd = kb.sb(ph, f"sWd{li}", [128, 16, 128], BF16)
                Mt = kb.sb(ph, f"sMt{li}", [128, 16, 128], BF16)
                t1 = kb.sb(ph, f"st1{li}", [128, 512], F32)
                yb = kb.sb(ph, f"syb{li}", [128, 1024], F32)
                yg = kb.sb(ph, f"syg{li}", [128, 1024], F32)
                ya = kb.sb(ph, f"sya{li}", [128, 1024], BF16)
                yaT = [kb.sb(ph, f"syaT{li}_{i}", [128, KC, 512], BF16) for i in range(2)]
                s1 = kb.sb(ph, f"ss1{li}", [128, 8], F32)
                wsrc = ev_wxz[j].rearrange("(k p) n -> p k n", p=128)
                cast_load(wx, wsrc[:, :, 0:1536], KC, 1536, h32s)
                cast_load(wz, wsrc[:, :, 1536:2576], KC, 1040, h32s)
                kb.dma(kb.sp, cv[:, :], ev_conv[j], writes=[cv])
                kb.dma(kb.sp, row[:, :], ev_row[j], writes=[row])
                for ck in range(48):
                    kb.op(kb.dve, lambda: nc.vector.tensor_scalar(out=diag[:, ck, :], in0=identb[:, :], scalar1=cv[:, ck:ck + 1], scalar2=0.0, op0=ALU.mult, op1=ALU.add),
                          reads=[identb, cv], writes=[diag])
                kb.op(kb.act, lambda: nc.scalar.activation(out=aneg[:, :], in_=row[:, 16:32], func=AF.Exp), reads=[row], writes=[aneg])
                kb.op(kb.dve, lambda: nc.vector.tensor_scalar(out=aneg[:, :], in0=aneg[:, :], scalar1=-1.0, scalar2=0.0, op0=ALU.mult, op1=ALU.add), reads=[aneg], writes=[aneg])
                kb.op(kb.dve, lambda: nc.vector.memset(ones32[:, :], 1.0), writes=[ones32])
                kb.op(kb.dve, lambda: nc.vector.memset(onec[:, :], 1.0), writes=[onec])
                for q in range(4):
                    kb.op(kb.dve, lambda: nc.vector.tensor_copy(out=nmb[:, q, :], in_=cst[:, 640:768]), reads=[cst], writes=[nmb])
                U32 = cst[:, 512:640]
                dtb, dsk, gn = row[:, 0:16], row[:, 32:48], row[:, 48:48 + 1024]
                SL = lambda k: sm[:, k, :]
                bc64 = lambda ap, nh: ap.unsqueeze(2).broadcast_to([128, nh, 64])
                load_h(h32s[0], 0)
                for b in range(NB):
                    h32 = h32s[b % 2]
                    if b + 1 < NB:
                        load_h(h32s[(b + 1) % 2], b + 1)
                    first = (b * 512) % SEQ == 0
                    if first:
                        kb.op(kb.dve, lambda: nc.vector.memset(hs[:, :, :], 0.0), writes=[hs])
                        kb.op(kb.dve, lambda: nc.vector.memset(hsb[:, :, :], 0.0), writes=[hsb])
                        kb.op(kb.dve, lambda: nc.vector.memset(xbcT[:, :, 0:3], 0.0), writes=[xbcT])
                    else:
                        kb.op(kb.dve, lambda: nc.vector.tensor_copy(out=xbcT[:, :, 0:3], in_=halo[:, :, :]), reads=[halo], writes=[xbcT])
                    rmsnorm_fm(pp, h32, nf[:, (5 + li) * KC:(6 + li) * KC], hn, sq, rstd)
                    for c in range(12):
                        p = pp.get()
                        for k in range(KC):
                            kb.op(kb.pe, lambda: nc.tensor.matmul(p[:, :], lhsT=wx[:, k, c * 128:(c + 1) * 128], rhs=hn[:, k, :], start=(k == 0), stop=(k == KC - 1)),
                                  reads=[wx, hn], writes=[p])
                        copy_on(kb.act if c % 2 else kb.dve, xbcT[:, c, 3:515], p[:, :], [p], [xbcT])
                        pp.put(p)
                    kb.op(kb.dve, lambda: nc.vector.tensor_copy(out=halo[:, :, :], in_=xbcT[:, :, 512:515]), reads=[xbcT], writes=[halo])
                    for c in range(12):
                        p = pp.get()
                        for k in range(4):
                            kb.op(kb.pe, lambda: nc.tensor.matmul(p[:, :], lhsT=diag[:, c * 4 + k, :], rhs=xbcT[:, c, k:k + 512], start=(k == 0), stop=(k == 3)),
                                  reads=[diag, xbcT], writes=[p])
                        kb.op(kb.act, lambda: nc.scalar.activation(out=xcT[:, c, :], in_=p[:, :], func=AF.Silu, bias=cv[:, 48 + c:49 + c]), reads=[p, cv], writes=[xcT])
                        pp.put(p)
                    yT = yaT[b % 2]
                    for i in range(4):
                        tc_ = slice(i * 128, (i + 1) * 128)
                        pz0, pz1, pdt = pp.get(), pp.get(), pp.get()
                        for (pz, c0, c1) in ((pz0, 0, 512), (pz1, 512, 1024), (pdt, 1024, 1040)):
                            for k in range(KC):
                                kb.op(kb.pe, lambda: nc.tensor.matmul(pz[:, 0:c1 - c0], lhsT=hn[:, k, tc_], rhs=wz[:, k, c0:c1], start=(k == 0), stop=(k == KC - 1)),
                                      reads=[wz, hn], writes=[pz])
                        kb.op(kb.act, lambda: nc.scalar.activation(out=zs[:, 0:512], in_=pz0[:, :], func=AF.Silu), reads=[pz0], writes=[zs])
                        kb.op(kb.act, lambda: nc.scalar.activation(out=zs[:, 512:1024], in_=pz1[:, :], func=AF.Silu), reads=[pz1], writes=[zs])
                        pp.put(pz0); pp.put(pz1)
                        kb.op(kb.dve, lambda: nc.vector.tensor_tensor(out=SL(0), in0=dtb, in1=pdt[:, 0:16], op=ALU.add), reads=[row, pdt], writes=[sm])
                        pp.put(pdt)
                        kb.op(kb.dve, lambda: nc.vector.scalar_tensor_tensor(out=SL(1), in0=SL(0), scalar=-1.0, in1=SL(0), op0=ALU.mult, op1=ALU.min), reads=[sm], writes=[sm])
                        kb.op(kb.act, lambda: nc.scalar.activation(out=SL(1), in_=SL(1), func=AF.Exp), reads=[sm], writes=[sm])
                        kb.op(kb.act, lambda: nc.scalar.activation(out=SL(1), in_=SL(1), func=AF.Ln, bias=onec[:, 0:1]), reads=[sm, onec], writes=[sm])
                        kb.op(kb.dve, lambda: nc.vector.scalar_tensor_tensor(out=SL(2), in0=SL(0), scalar=0.0, in1=SL(1), op0=ALU.max, op1=ALU.add), reads=[sm], writes=[sm])
                        kb.op(kb.dve, lambda: nc.vector.tensor_tensor(out=SL(3), in0=SL(2), in1=aneg[:, :], op=ALU.mult), reads=[sm, aneg], writes=[sm])
                        pa = pp.get()
                        kb.op(kb.pe, lambda: nc.tensor.matmul(pa[:, 0:16], lhsT=U32, rhs=SL(3), start=True, stop=True), reads=[cst, sm], writes=[pa])
                        kb.op(kb.pe, lambda: nc.tensor.matmul(pa[:, 16:32], lhsT=ones32[:, :], rhs=SL(3), start=True, stop=True), reads=[ones32, sm], writes=[pa])
                        kb.op(kb.dve, lambda: nc.vector.tensor_scalar(out=SL(4), in0=pa[:, 0:16], scalar1=-1.0, scalar2=0.0, op0=ALU.mult, op1=ALU.add), reads=[pa], writes=[sm])
                        kb.op(kb.act, lambda: nc.scalar.activation(out=SL(5), in_=pa[:, 0:16], func=AF.Exp), reads=[pa], writes=[sm])
                        kb.op(kb.dve, lambda: nc.vector.tensor_tensor(out=SL(6), in0=SL(4), in1=pa[:, 16:32], op=ALU.add), reads=[sm, pa], writes=[sm])
                        kb.op(kb.act, lambda: nc.scalar.activation(out=SL(6), in_=SL(6), func=AF.Exp), reads=[sm], writes=[sm])
                        kb.op(kb.act, lambda: nc.scalar.activation(out=SL(7), in_=pa[:, 16:32], func=AF.Exp), reads=[pa], writes=[sm])
                        pp.put(pa)
                        kb.op(kb.dve, lambda: nc.vector.tensor_tensor(out=SL(8), in0=SL(2), in1=SL(6), op=ALU.mult), reads=[sm], writes=[sm])
                        kb.op(kb.dve, lambda: nc.vector.tensor_tensor(out=daU[:, :, :], in0=U32.unsqueeze(1).broadcast_to([128, 16, 128]),
                                                                      in1=SL(3).unsqueeze(2).broadcast_to([128, 16, 128]), op=ALU.mult), reads=[cst, sm], writes=[daU])
                        px = pp.get()
                        pxv = px[:, :].bitcast(BF16)
                        for c in range(8):
                            kb.op(kb.pe, lambda: nc.tensor.transpose(out=pxv[:, c * 128:(c + 1) * 128], in_=xcT[:, c, tc_], identity=identb[:, :]), reads=[xcT, identb], writes=[px])
                        x3 = pxv[:, 0:1024].rearrange("p (h e) -> p h e", e=64)
                        v3 = lambda t: t[:, :].rearrange("p (h e) -> p h e", e=64)
                        kb.op(kb.dve, lambda: nc.vector.tensor_tensor(out=v3(xdt), in0=bc64(SL(2), 16), in1=x3, op=ALU.mult), reads=[sm, px], writes=[xdt])
                        kb.op(kb.dve, lambda: nc.vector.tensor_tensor(out=v3(xD), in0=bc64(dsk, 16), in1=x3, op=ALU.mult), reads=[row, px], writes=[xD])
                        kb.op(kb.dve, lambda: nc.vector.tensor_tensor(out=v3(xp), in0=bc64(SL(8), 16), in1=x3, op=ALU.mult), reads=[sm, px], writes=[xp])
                        pp.put(px)
                        pb = pp.get()
                        pbv = pb[:, :].bitcast(BF16)
                        for g in range(2):
                            kb.op(kb.pe, lambda: nc.tensor.transpose(out=pbv[:, g * 128:(g + 1) * 128], in_=xcT[:, 8 + g, tc_], identity=identb[:, :]), reads=[xcT, identb], writes=[pb])
                        copy_on(kb.act, btm[:, :], pbv[:, 0:256], [pb], [btm])
                        pp.put(pb)
                        for g in range(2):
                            hsl = slice(g * 8, (g + 1) * 8)
                            sg_ = [pp.get(), pp.get()]
                            for q in range(2):
                                kb.op(kb.pe, lambda: nc.tensor.matmul(sg_[q][:, :], lhsT=ones32[:, :], rhs=daU[:, g * 8 + q * 4:g * 8 + q * 4 + 4, :].rearrange("p h l -> p (h l)"), start=True, stop=False),
                                      reads=[ones32, daU], writes=[sg_[q]])
                                kb.op(kb.pe, lambda: nc.tensor.matmul(sg_[q][:, :], lhsT=identb[:, :], rhs=nmb[:, :, :].rearrange("p h l -> p (h l)"), start=False, stop=True),
                                      reads=[identb, nmb], writes=[sg_[q]])
                                for hh in range(4):
                                    h = g * 8 + q * 4 + hh
                                    kb.op(kb.act, lambda: nc.scalar.activation(out=Wd[:, h, :], in_=sg_[q][:, hh * 128:(hh + 1) * 128], func=AF.Exp, bias=sm[:, 4, h:h + 1]),
                                          reads=[sg_[q], sm], writes=[Wd])
                                pp.put(sg_[q])
                            pcb = pp.get()
                            kb.op(kb.pe, lambda: nc.tensor.matmul(pcb[:, 0:128], lhsT=xcT[:, 8 + g, tc_], rhs=xcT[:, 10 + g, tc_], start=True, stop=True), reads=[xcT], writes=[pcb])
                            kb.op(kb.dve, lambda: nc.vector.tensor_tensor(out=Mt[:, hsl, :], in0=Wd[:, hsl, :], in1=pcb[:, 0:128].unsqueeze(1).broadcast_to([128, 8, 128]), op=ALU.mult),
                                  reads=[Wd, pcb], writes=[Mt])
                            pp.put(pcb)
                            py, po = pp.get(), pp.get()
                            for hh in range(8):
                                h = g * 8 + hh
                                kb.op(kb.pe, lambda: nc.tensor.matmul(py[:, hh * 64:(hh + 1) * 64], lhsT=Mt[:, h, :], rhs=xdt[:, h * 64:(h + 1) * 64], start=True, stop=True),
                                      reads=[Mt, xdt], writes=[py])
                            kb.op(kb.pe, lambda: nc.tensor.matmul(po[:, :], lhsT=xcT[:, 10 + g, tc_], rhs=hsb[:, g, :], start=True, stop=True), reads=[xcT, hsb], writes=[po])
                            kb.op(kb.dve, lambda: nc.vector.tensor_tensor(out=t1[:, :].rearrange("p (h e) -> p h e", e=64), in0=bc64(sm[:, 5, hsl], 8),
                                                                          in1=po[:, :].rearrange("p (h e) -> p h e", e=64), op=ALU.mult), reads=[sm, po], writes=[t1])
                            pp.put(po)
                            kb.op(kb.pool, lambda: nc.gpsimd.tensor_tensor(out=t1[:, :], in0=t1[:, :], in1=xD[:, g * 512:(g + 1) * 512], op=ALU.add), reads=[t1, xD], writes=[t1])
                            kb.op(kb.dve, lambda: nc.vector.tensor_tensor(out=yb[:, g * 512:(g + 1) * 512], in0=t1[:, :], in1=py[:, :], op=ALU.add), reads=[t1, py], writes=[yb])
                            pp.put(py)
                            pst = pp.get()
                            kb.op(kb.pe, lambda: nc.tensor.matmul(pst[:, :], lhsT=btm[:, g * 128:(g + 1) * 128], rhs=xp[:, g * 512:(g + 1) * 512], start=True, stop=True),
                                  reads=[btm, xp], writes=[pst])
                            hs3 = hs[:, g, :].rearrange("p (h e) -> p h e", e=64)
                            kb.op(kb.dve, lambda: nc.vector.tensor_tensor(out=hs3, in0=hs3, in1=bc64(sm[:, 7, hsl], 8), op=ALU.mult), reads=[hs, sm], writes=[hs])
                            kb.op(kb.dve, lambda: nc.vector.tensor_tensor(out=hs[:, g, :], in0=hs[:, g, :], in1=pst[:, :], op=ALU.add), reads=[hs, pst], writes=[hs])
                            pp.put(pst)
                            copy_on(kb.act, hsb[:, g, :], hs[:, g, :], [hs], [hsb])
                        kb.op(kb.pool, lambda: nc.gpsimd.tensor_tensor(out=yg[:, :], in0=yb[:, :], in1=zs[:, :], op=ALU.mult), reads=[yb, zs], writes=[yg])
                        kb.op(kb.act, lambda: nc.scalar.activation(out=yb[:, :], in_=yg[:, :], func=AF.Square, accum_out=s1[:, 0:1]), reads=[yg], writes=[yb, s1])
                        kb.op(kb.act, lambda: nc.scalar.activation(out=s1[:, 1:2], in_=s1[:, 0:1], func=AF.Sqrt, scale=1.0 / 1024, bias=epsc[:, 0:1]), reads=[s1, epsc], writes=[s1])
                        kb.op(kb.dve, lambda: nc.vector.reciprocal(out=s1[:, 2:3], in_=s1[:, 1:2]), reads=[s1], writes=[s1])
                        kb.op(kb.dve, lambda: nc.vector.scalar_tensor_tensor(out=ya[:, :], in0=yg[:, :], scalar=s1[:, 2:3], in1=gn, op0=ALU.mult, op1=ALU.mult),
                              reads=[yg, s1, row], writes=[ya])
                        pt = pp.get()
                        ptv = pt[:, :].bitcast(BF16)
                        for c in range(8):
                            kb.op(kb.pe, lambda: nc.tensor.transpose(out=ptv[:, c * 128:(c + 1) * 128], in_=ya[:, c * 128:(c + 1) * 128], identity=identb[:, :]), reads=[ya, identb], writes=[pt])
                        copy_on(kb.act, yT[:, :, tc_], ptv[:, 0:1024].rearrange("p (c t) -> p c t", t=128), [pt], [yT])
                        pp.put(pt)
                    kb.dma(kb.pool, mixT[0:8, :, b * 512:(b + 1) * 512].rearrange("c p t -> p c t"), yT[:, :, :], reads=[yT])
            kb.barrier()


        def sincos(N, ang, angbuf, sn_ap, cs_ap, outs, tb):
            ki, kf, rr, ab, s2, c2 = tb
            kb.op(kb.dve, lambda: nc.vector.tensor_scalar(out=ki[:, 0:N], in0=ang, scalar1=float(1 / (2 * np.pi)), scalar2=0.0, op0=ALU.mult, op1=ALU.add), reads=[angbuf], writes=[ki])
            kb.op(kb.dve, lambda: nc.vector.tensor_copy(out=kf[:, 0:N], in_=ki[:, 0:N]), reads=[ki], writes=[kf])
            kb.op(kb.dve, lambda: nc.vector.scalar_tensor_tensor(out=rr[:, 0:N], in0=kf[:, 0:N], scalar=-TWO_PI_HI, in1=ang, op0=ALU.mult, op1=ALU.add), reads=[kf, angbuf], writes=[rr])
            kb.op(kb.dve, lambda: nc.vector.scalar_tensor_tensor(out=rr[:, 0:N], in0=kf[:, 0:N], scalar=-TWO_PI_LO, in1=rr[:, 0:N], op0=ALU.mult, op1=ALU.add), reads=[kf, rr], writes=[rr])
            kb.op(kb.dve, lambda: nc.vector.scalar_tensor_tensor(out=ab[:, 0:N], in0=rr[:, 0:N], scalar=-1.0, in1=rr[:, 0:N], op0=ALU.mult, op1=ALU.max), reads=[rr], writes=[ab])
            kb.op(kb.act, lambda: nc.scalar.activation(out=s2[:, 0:N], in_=rr[:, 0:N], func=AF.Sin, scale=0.5), reads=[rr], writes=[s2])
            kb.op(kb.act, lambda: nc.scalar.activation(out=c2[:, 0:N], in_=ab[:, 0:N], func=AF.Sin, scale=-0.5, bias=halfpi[:, 0:1]), reads=[ab, halfpi], writes=[c2])
            kb.op(kb.dve, lambda: nc.vector.scalar_tensor_tensor(out=sn_ap, in0=s2[:, 0:N], scalar=2.0, in1=c2[:, 0:N], op0=ALU.mult, op1=ALU.mult), reads=[s2, c2], writes=outs)
            kb.op(kb.dve, lambda: nc.vector.tensor_tensor(out=c2[:, 0:N], in0=s2[:, 0:N], in1=s2[:, 0:N], op=ALU.mult), reads=[s2], writes=[c2])
            kb.op(kb.dve, lambda: nc.vector.tensor_scalar(out=cs_ap, in0=c2[:, 0:N], scalar1=-2.0, scalar2=1.0, op0=ALU.mult, op1=ALU.add), reads=[c2], writes=outs)

        def tt(e, out, a, b, op, reads, writes):
            kb.op(e, lambda: e.e.tensor_tensor(out=out, in0=a, in1=b, op=op), reads=reads, writes=writes)

        def phase_o1a(li, j):
            with ExitStack() as ph:
                pp = PsPool(kb, ph, 8, f"o1a{li}_")
                wu = kb.sb(ph, f"wu5{li}", [128, KC, 512], BF16)
                h32s = [kb.sb(ph, f"fh32_{li}_{i}", [128, KC, 512], F32) for i in range(2)]
                hn = kb.sb(ph, f"fhn{li}", [128, KC, 512], BF16)
                rstd = kb.sb(ph, f"frstd{li}", [128, 512], F32)
                cosT = kb.sb(ph, f"fcos{li}", [128, 16, 512], F32)
                sinT = kb.sb(ph, f"fsin{li}", [128, 16, 512], F32)
                Btre = kb.sb(ph, f"fBre{li}", [128, 16, 128], BF16)
                Btim = kb.sb(ph, f"fBim{li}", [128, 16, 128], BF16)
                Cre = kb.sb(ph, f"fCre{li}", [128, 16, 128], BF16)
                nCre = kb.sb(ph, f"fnCre{li}", [128, 16, 128], BF16)
                nCim = kb.sb(ph, f"fnCim{li}", [128, 16, 128], BF16)
                wglu = kb.sb(ph, f"fwglu{li}", [128, 4, 512], BF16)
                pt = kb.sb(ph, f"fpt{li}", [128, 64], F32)
                sc = kb.sb(ph, f"fsc{li}", [128, 12, 16], F32)
                uT32 = kb.sb(ph, f"fu32{li}", [128, 4, 512], F32)
                uTb = kb.sb(ph, f"fub{li}", [128, 4, 512], BF16)
                g32 = kb.sb(ph, f"fg32{li}", [128, 4, 512], F32)
                gb = kb.sb(ph, f"fgb{li}", [128, 4, 512], BF16)
                sq = g32.view(g32.t[:, :, :].rearrange("p c t -> p (c t)").bitcast(BF16).rearrange("p (c t) -> p c t", t=512), "sq_alias")
                tmp = [kb.sb(ph, f"ftmp{li}_{i}", [128, 512], F32) for i in range(8)]
                cre = [kb.sb(ph, f"fcre{li}_{i}", [128, 512], F32) for i in range(1)] * 2
                cim = [kb.sb(ph, f"fcim{li}_{i}", [128, 512], F32) for i in range(1)] * 2
                wre = [kb.sb(ph, f"fwre{li}_{i}", [128, 512], F32) for i in range(2)]
                wim = [kb.sb(ph, f"fwim{li}_{i}", [128, 512], F32) for i in range(2)]
                pr = [kb.sb(ph, f"fpr{li}_{i}", [128, 4, 512], BF16) for i in range(2)]
                yco = [kb.sb(ph, f"fyc{li}_{i}", [128, 512], BF16) for i in range(1)] * 2
                iot = tmp[7].view(tmp[7].t[:, :], "iot_alias")
                kib = tmp[6].view(tmp[6].t[:, :].bitcast(I32), "kib_alias")
                cast_load(wu, od_wu[j].rearrange("(k p) n -> p k n", p=128), KC, 512, h32s)
                kb.dma(kb.sp, pt[:, :], od_pt[j], writes=[pt])
                kb.dma(kb.sp, iot[:, :], iota_in[:, :], writes=[iot])
                tb = (kib, tmp[0], tmp[1], tmp[2], tmp[3], tmp[4])
                S = lambda k: sc[:, k, :]
                kb.op(kb.act, lambda: nc.scalar.activation(out=S(0), in_=pt[:, 32:48], func=AF.Exp), reads=[pt], writes=[sc])
                tt(kb.dve, S(1), pt[:, 0:16], S(0), ALU.mult, [pt, sc], [sc])
                kb.op(kb.act, lambda: nc.scalar.activation(out=S(1), in_=S(1), func=AF.Exp), reads=[sc], writes=[sc])
                tt(kb.dve, S(2), pt[:, 16:32], S(0), ALU.mult, [pt, sc], [sc])
                for k in range(16):
                    kb.op(kb.dve, lambda: nc.vector.tensor_scalar(out=tmp[5][:, :], in0=iot[:, :], scalar1=sc[:, 2, k:k + 1], scalar2=0.0, op0=ALU.mult, op1=ALU.add),
                          reads=[iot, sc], writes=[tmp[5]])
                    sincos(512, tmp[5][:, :], tmp[5], sinT[:, k, :], cosT[:, k, :], [sinT, cosT], tb)
                kb.op(kb.dve, lambda: nc.vector.tensor_copy(out=S(3), in_=cosT[:, :, 511]), reads=[cosT], writes=[sc])
                kb.op(kb.dve, lambda: nc.vector.tensor_copy(out=S(4), in_=sinT[:, :, 511]), reads=[sinT], writes=[sc])
                st0 = h32s[0].t[:, :, :].rearrange("p c t -> p (c t)")
                st1 = h32s[1].t[:, :, :].rearrange("p c t -> p (c t)")
                R = lambda i: st0[:, i * 512:(i + 1) * 512]
                for q in range(4):
                    cols = slice(q * 512, (q + 1) * 512)
                    for w_ in range(3):
                        kb.dma(kb.sp, R(w_), od_rowp[j][:, w_ * 2048 + q * 512: w_ * 2048 + (q + 1) * 512], writes=[h32s[0]])
                    kb.dma(kb.sp, st1[:, 0:512].rearrange("p (k m) -> p k m", m=128), od_bt[j][0][:, q * 4:(q + 1) * 4, :], writes=[h32s[1]])
                    kb.dma(kb.sp, st1[:, 512:1024].rearrange("p (k m) -> p k m", m=128), od_bt[j][1][:, q * 4:(q + 1) * 4, :], writes=[h32s[1]])
                    H0, H1 = [h32s[0]], [h32s[1]]
                    kb.op(kb.act, lambda: nc.scalar.activation(out=R(2), in_=R(2), func=AF.Exp), reads=H0, writes=H0)
                    tt(kb.dve, R(3), R(0), R(2), ALU.mult, H0, H0)
                    kb.op(kb.act, lambda: nc.scalar.activation(out=R(3), in_=R(3), func=AF.Exp), reads=H0, writes=H0)
                    tt(kb.dve, R(4), R(1), R(2), ALU.mult, H0, H0)
                    sincos(512, R(4), h32s[0], R(5), R(6), H0, tb)
                    tt(kb.dve, R(5), R(5), R(3), ALU.mult, H0, H0)
                    tt(kb.dve, R(6), R(6), R(3), ALU.mult, H0, H0)
                    kb.op(kb.dve, lambda: nc.vector.tensor_scalar(out=R(6), in0=R(6), scalar1=1.0, scalar2=-1.0, op0=ALU.mult, op1=ALU.add), reads=H0, writes=H0)
                    tt(kb.dve, R(2), R(6), R(0), ALU.mult, H0, H0)
                    tt(kb.dve, R(3), R(5), R(1), ALU.mult, H0, H0)
                    tt(kb.dve, R(2), R(2), R(3), ALU.add, H0, H0)
                    tt(kb.dve, R(3), R(5), R(0), ALU.mult, H0, H0)
                    tt(kb.dve, R(4), R(6), R(1), ALU.mult, H0, H0)
                    tt(kb.dve, R(3), R(3), R(4), ALU.subtract, H0, H0)
                    tt(kb.dve, R(4), R(0), R(0), ALU.mult, H0, H0)
                    tt(kb.dve, R(5), R(1), R(1), ALU.mult, H0, H0)
                    tt(kb.dve, R(4), R(4), R(5), ALU.add, H0, H0)
                    kb.op(kb.dve, lambda: nc.vector.reciprocal(out=R(4), in_=R(4)), reads=H0, writes=H0)
                    tt(kb.dve, R(2), R(2), R(4), ALU.mult, H0, H0)
                    tt(kb.dve, R(3), R(3), R(4), ALU.mult, H0, H0)
                    bre, bim = st1[:, 0:512], st1[:, 512:1024]
                    T_ = lambda i: st1[:, 1024 + i * 512:1024 + (i + 1) * 512]
                    tt(kb.dve, T_(0), R(2), bre, ALU.mult, H0 + H1, H1)
                    tt(kb.dve, T_(1), R(3), bim, ALU.mult, H0 + H1, H1)
                    tt(kb.dve, Btre[:, q * 4:(q + 1) * 4, :].rearrange("p k m -> p (k m)"), T_(0), T_(1), ALU.subtract, H1, [Btre])
                    tt(kb.dve, T_(0), R(2), bim, ALU.mult, H0 + H1, H1)
                    tt(kb.dve, T_(1), R(3), bre, ALU.mult, H0 + H1, H1)
                    tt(kb.dve, Btim[:, q * 4:(q + 1) * 4, :].rearrange("p k m -> p (k m)"), T_(0), T_(1), ALU.add, H1, [Btim])
                for w_, dsts in ((0, (Cre, nCre)), (1, (None, nCim))):
                    for q in range(2):
                        kb.dma(kb.sp, st0[:, 0:1024].rearrange("p (k m) -> p k m", m=128), od_ct[j][w_][:, q * 8:(q + 1) * 8, :], writes=[h32s[0]])
                        src = st0[:, 0:1024].rearrange("p (k m) -> p k m", m=128)
                        if dsts[0] is not None:
                            copy_on(kb.act, dsts[0][:, q * 8:(q + 1) * 8, :], src, [h32s[0]], [dsts[0]])
                        kb.op(kb.dve, lambda: nc.vector.tensor_scalar(out=dsts[1][:, q * 8:(q + 1) * 8, :], in0=src, scalar1=-1.0, scalar2=0.0, op0=ALU.mult, op1=ALU.add),
                              reads=[h32s[0]], writes=[dsts[1]])
                cast_load(wglu, od_wglu[j].rearrange("(k p) n -> p k n", p=128), 4, 512, h32s)
                dskp, bglu = pt[:, 48:52], pt[:, 52:56]
                load_h(h32s[0], 0)
                nt = 0
                for b in range(NB):
                    h32 = h32s[b % 2]
                    if b + 1 < NB:
                        load_h(h32s[(b + 1) % 2], b + 1)
                    first = (b * 512) % SEQ == 0
                    rmsnorm_fm(pp, h32, nf[:, (5 + li) * KC:(6 + li) * KC], hn, sq, rstd)
                    for c in range(4):
                        p = pp.get()
                        for k in range(KC):
                            kb.op(kb.pe, lambda: nc.tensor.matmul(p[:, :], lhsT=wu[:, k, c * 128:(c + 1) * 128], rhs=hn[:, k, :], start=(k == 0), stop=(k == KC - 1)),
                                  reads=[wu, hn], writes=[p])
                        copy_on(kb.act, uT32[:, c, :], p[:, :], [p], [uT32])
                        copy_on(kb.dve, uTb[:, c, :], p[:, :], [p], [uTb])
                        pp.put(p)
                    py = None
                    for k in range(16):
                        cc, r = k // 4, k % 4
                        rs = slice(32 * r, 32 * r + 32) if r < 3 else slice(64, 128)
                        pre, pim = pp.get(), pp.get()
                        kb.op(kb.pe, lambda: nc.tensor.matmul(pre[:, :], lhsT=Btre[rs, k, :], rhs=uTb[rs, cc, :], start=True, stop=True), reads=[Btre, uTb], writes=[pre])
                        kb.op(kb.pe, lambda: nc.tensor.matmul(pim[:, :], lhsT=Btim[rs, k, :], rhs=uTb[rs, cc, :], start=True, stop=True), reads=[Btim, uTb], writes=[pim])
                        i2 = nt % 2
                        nt += 1
                        t1, t2, t3, t4 = tmp[4 * i2:4 * i2 + 4]
                        ck, sk = cosT[:, k, :], sinT[:, k, :]
                        tt(kb.dve, t1[:, :], ck, pre[:, :], ALU.mult, [cosT, pre], [t1])
                        tt(kb.dve, t2[:, :], sk, pim[:, :], ALU.mult, [sinT, pim], [t2])
                        tt(kb.dve, t3[:, :], ck, pim[:, :], ALU.mult, [cosT, pim], [t3])
                        tt(kb.dve, t4[:, :], sk, pre[:, :], ALU.mult, [sinT, pre], [t4])
                        pp.put(pre); pp.put(pim)
                        tt(kb.pool, cre[i2][:, :], t1[:, :], t2[:, :], ALU.add, [t1, t2], [cre[i2]])
                        tt(kb.pool, cim[i2][:, :], t3[:, :], t4[:, :], ALU.subtract, [t3, t4], [cim[i2]])
                        rho_b = sc[:, 1, k:k + 1].broadcast_to([128, 512])
                        ini_re = 0.0 if first else sc[:, 7, k:k + 1]
                        ini_im = 0.0 if first else sc[:, 8, k:k + 1]
                        kb.op(kb.dve, lambda: nc.vector.tensor_tensor_scan(out=wre[i2][:, :], data0=rho_b, data1=cre[i2][:, :], initial=ini_re, op0=ALU.mult, op1=ALU.add),
                              reads=[sc, cre[i2]], writes=[wre[i2]])
                        kb.op(kb.dve, lambda: nc.vector.tensor_tensor_scan(out=wim[i2][:, :], data0=rho_b, data1=cim[i2][:, :], initial=ini_im, op0=ALU.mult, op1=ALU.add),
                              reads=[sc, cim[i2]], writes=[wim[i2]])
                        copy_on(kb.act, sc[:, 5, k:k + 1], wre[i2][:, 511:512], [wre[i2]], [sc])
                        copy_on(kb.act, sc[:, 6, k:k + 1], wim[i2][:, 511:512], [wim[i2]], [sc])
                        prb = pr[i2]
                        tt(kb.pool, prb[:, 0, :], ck, wre[i2][:, :], ALU.mult, [cosT, wre[i2]], [prb])
                        tt(kb.pool, prb[:, 1, :], sk, wim[i2][:, :], ALU.mult, [sinT, wim[i2]], [prb])
                        tt(kb.dve, prb[:, 2, :], sk, wre[i2][:, :], ALU.mult, [sinT, wre[i2]], [prb])
                        tt(kb.dve, prb[:, 3, :], ck, wim[i2][:, :], ALU.mult, [cosT, wim[i2]], [prb])
                        if r == 0:
                            py = pp.get()
                        for m_, lh in enumerate((Cre, nCre, nCim, nCim)):
                            kb.op(kb.pe, lambda: nc.tensor.matmul(py[:, :], lhsT=lh[:, k, :], rhs=prb[:, m_, :], start=(r == 0 and m_ == 0), stop=(r == 3 and m_ == 3)),
                                  reads=[lh, prb], writes=[py])
                        if r == 3:
                            yv, x2, inn = tmp[0], tmp[1], tmp[2]
                            kb.op(kb.dve, lambda: nc.vector.scalar_tensor_tensor(out=yv[:, :], in0=uT32[:, cc, :], scalar=dskp[:, cc:cc + 1], in1=py[:, :], op0=ALU.mult, op1=ALU.add),
                                  reads=[uT32, pt, py], writes=[yv])
                            pp.put(py)
                            tt(kb.pool, x2[:, :], yv[:, :], yv[:, :], ALU.mult, [yv], [x2])
                            kb.op(kb.pool, lambda: nc.gpsimd.tensor_scalar(out=x2[:, :], in0=x2[:, :], scalar1=0.044715, scalar2=1.0, op0=ALU.mult, op1=ALU.add), reads=[x2], writes=[x2])
                            tt(kb.pool, inn[:, :], x2[:, :], yv[:, :], ALU.mult, [x2, yv], [inn])
                            kb.op(kb.act, lambda: nc.scalar.activation(out=inn[:, :], in_=inn[:, :], func=AF.Sigmoid, scale=float(2.0 * np.sqrt(2.0 / np.pi))), reads=[inn], writes=[inn])
                            tt(kb.pool, g32[:, cc, :], inn[:, :], yv[:, :], ALU.mult, [inn, yv], [g32])
                            copy_on(kb.act, gb[:, cc, :], g32[:, cc, :], [g32], [gb])
                    tt(kb.dve, S(9), S(3), S(5), ALU.mult, [sc], [sc])
                    tt(kb.dve, S(10), S(4), S(6), ALU.mult, [sc], [sc])
                    tt(kb.dve, S(11), S(4), S(5), ALU.mult, [sc], [sc])
                    tt(kb.dve, S(7), S(9), S(10), ALU.subtract, [sc], [sc])
                    tt(kb.dve, S(9), S(3), S(6), ALU.mult, [sc], [sc])
                    tt(kb.dve, S(8), S(11), S(9), ALU.add, [sc], [sc])
                    for c2 in range(4):
                        p = pp.get()
                        for k in range(4):
                            kb.op(kb.pe, lambda: nc.tensor.matmul(p[:, :], lhsT=wglu[:, k, c2 * 128:(c2 + 1) * 128], rhs=gb[:, k, :], start=(k == 0), stop=(k == 3)),
                                  reads=[wglu, gb], writes=[p])
                        sgt = tmp[3]
                        kb.op(kb.act, lambda: nc.scalar.activation(out=sgt[:, :], in_=p[:, :], func=AF.Sigmoid, bias=bglu[:, c2:c2 + 1]), reads=[p, pt], writes=[sgt])
                        pp.put(p)
                        yo = yco[c2 % 2]
                        tt(kb.pool, yo[:, :], sgt[:, :], g32[:, c2, :], ALU.mult, [sgt, g32], [yo])
                        kb.dma(kb.pool, mixT[c2][:, b * 512:(b + 1) * 512], yo[:, :], reads=[yo])
            kb.barrier()

        def phase_o1b(li, j):
            with ExitStack() as ph:
                pp = PsPool(kb, ph, 8, f"o1b{li}_")
                wf = kb.sb(ph, f"gwf{li}", [128, KC, 768], BF16)
                wuq = kb.sb(ph, f"gwuq{li}", [128, 3, 1024], BF16)
                wuk = kb.sb(ph, f"gwuk{li}", [128, 2, 1024], BF16)
                wuv = kb.sb(ph, f"gwuv{li}", [128, 2, 512], BF16)
                h32s = [kb.sb(ph, f"gh32_{li}_{i}", [128, KC, 512], F32) for i in range(2)]
                hn = kb.sb(ph, f"ghn{li}", [128, KC, 512], BF16)
                sq = kb.sb(ph, f"gsq{li}", [128, KC, 512], BF16)
                rstd = kb.sb(ph, f"grstd{li}", [128, 512], F32)
                rs2 = kb.sb(ph, f"grs2{li}", [128, 512], F32)
                nrm = kb.sb(ph, f"gnrm{li}", [128, 8], F32)
                c32 = kb.sb(ph, f"gc32{li}", [128, 5, 512], F32)
                cn = kb.sb(ph, f"gcn{li}", [128, 5, 512], BF16)
                css = [kb.sb(ph, f"gcs{li}_{i}", [128, 512], F32) for i in range(2)]
                sns = [kb.sb(ph, f"gsn{li}_{i}", [128, 512], F32) for i in range(2)]
                qbt = [kb.sb(ph, f"gqb{li}_{i}", [128, 512], BF16) for i in range(2)]
                tA = [kb.sb(ph, f"gtA{li}_{i}", [128, 512], F32) for i in range(2)]
                tB = [kb.sb(ph, f"gtB{li}_{i}", [128, 512], F32) for i in range(2)]
                qr = [kb.sb(ph, f"gqr{li}_{i}", [128, 512], BF16) for i in range(3)]
                krot = kb.sb(ph, f"gkr{li}", [128, 512], BF16)
                vt = [kb.sb(ph, f"gvt{li}_{i}", [128, 512], BF16) for i in range(2)]
                cast_load(wf, od_wf[j].rearrange("(k p) n -> p k n", p=128), KC, 768, h32s)
                cast_load(wuq, od_wuq[j].rearrange("(k p) n -> p k n", p=128), 3, 1024, h32s)
                cast_load(wuk, od_wuk[j].rearrange("(k p) n -> p k n", p=128), 2, 1024, h32s)
                cast_load(wuv, od_wuv[j].rearrange("(k p) n -> p k n", p=128), 2, 512, h32s)
                kb.dma(kb.sp, nrm[:, 0:5], od_nrm[j], writes=[nrm])
                load_h(h32s[0], 0)
                n = 0
                for b in range(NB):
                    h32 = h32s[b % 2]
                    cs, sn = css[b % 2], sns[b % 2]
                    kb.dma(kb.sp, cs[:, :], tabs[2][:, b * 512:(b + 1) * 512], writes=[cs])
                    kb.dma(kb.sp, sn[:, :], tabs[3][:, b * 512:(b + 1) * 512], writes=[sn])
                    if b + 1 < NB:
                        load_h(h32s[(b + 1) % 2], b + 1)
                    rmsnorm_fm(pp, h32, nf[:, (5 + li) * KC:(6 + li) * KC], hn, sq, rstd)
                    for c in range(5):
                        p = pp.get()
                        for k in range(KC):
                            kb.op(kb.pe, lambda: nc.tensor.matmul(p[:, :], lhsT=wf[:, k, c * 128:(c + 1) * 128], rhs=hn[:, k, :], start=(k == 0), stop=(k == KC - 1)),
                                  reads=[wf, hn], writes=[p])
                        copy_on(kb.dve, c32[:, c, :], p[:, :], [p], [c32])
                        kb.op(kb.act, lambda: nc.scalar.activation(out=sq[:, c, :], in_=p[:, :], func=AF.Square), reads=[p], writes=[sq])
                        pp.put(p)
                    for (c0, c1, rs_) in ((0, 3, rstd), (3, 5, rs2)):
                        p = pp.get()
                        for c in range(c0, c1):
                            kb.op(kb.pe, lambda: nc.tensor.matmul(p[:, :], lhsT=onesb[:, :], rhs=sq[:, c, :], start=(c == c0), stop=(c == c1 - 1)), reads=[onesb, sq], writes=[p])
                        kb.op(kb.act, lambda: nc.scalar.activation(out=rs_[:, :], in_=p[:, :], func=AF.Sqrt, scale=1.0 / (128 * (c1 - c0)), bias=epsc[:, 0:1]), reads=[p, epsc], writes=[rs_])
                        pp.put(p)
                        kb.op(kb.dve, lambda: nc.vector.reciprocal(out=rs_[:, :], in_=rs_[:, :]), reads=[rs_], writes=[rs_])
                        for c in range(c0, c1):
                            kb.op(kb.dve, lambda: nc.vector.scalar_tensor_tensor(out=cn[:, c, :], in0=c32[:, c, :], scalar=nrm[:, c:c + 1], in1=rs_[:, :], op0=ALU.mult, op1=ALU.mult),
                                  reads=[c32, nrm, rs_], writes=[cn])
                    p = pp.get()
                    for k in range(KC):
                        kb.op(kb.pe, lambda: nc.tensor.matmul(p[:, :], lhsT=wf[:, k, 640:768], rhs=hn[:, k, :], start=(k == 0), stop=(k == KC - 1)), reads=[wf, hn], writes=[p])
                    rope_chunk(pp, p, psw32, cs, sn, qbt[n % 2], tA[n % 2], tB[n % 2], krot[:, :], krot)
                    n += 1
                    for h in range(8):
                        p = pp.get()
                        for c in range(3):
                            kb.op(kb.pe, lambda: nc.tensor.matmul(p[:, :], lhsT=wuq[:, c, h * 128:(h + 1) * 128], rhs=cn[:, c, :], start=(c == 0), stop=(c == 2)), reads=[wuq, cn], writes=[p])
                        o = qr[n % 3]
                        rope_chunk(pp, p, psw32, cs, sn, qbt[n % 2], tA[n % 2], tB[n % 2], o[:, :], o)
                        n += 1
                        kb.dma(kb.pool, qT[h][:, b * 512:(b + 1) * 512], o[:, :], reads=[o])
                    for h in range(8):
                        p = pp.get()
                        for c in range(2):
                            kb.op(kb.pe, lambda: nc.tensor.matmul(p[:, :], lhsT=wuk[:, c, h * 128:(h + 1) * 128], rhs=cn[:, 3 + c, :], start=(c == 0), stop=(c == 1)), reads=[wuk, cn], writes=[p])
                        o = qr[n % 3]
                        n += 1
                        copy_on(kb.act, o[0:64, :], p[0:64, :], [p], [o])
                        pp.put(p)
                        copy_on(kb.dve, o[64:128, :], krot[64:128, :], [krot], [o])
                        kb.dma(kb.pool, kT[h][:, b * 512:(b + 1) * 512], o[:, :], reads=[o])
                    for i in range(4):
                        p = pp.get()
                        for c in range(2):
                            kb.op(kb.pe, lambda: nc.tensor.matmul(p[:, :], lhsT=cn[:, 3 + c, i * 128:(i + 1) * 128], rhs=wuv[:, c, :], start=(c == 0), stop=(c == 1)), reads=[wuv, cn], writes=[p])
                        v = vt[i % 2]
                        copy_on(kb.act if i % 2 else kb.dve, v[:, :], p[:, :], [p], [v])
                        pp.put(p)
                        kb.dma(kb.pool, vtm[b * 512 + i * 128:b * 512 + (i + 1) * 128, 0:512], v[:, :], reads=[v])
            kb.barrier()

        def phase_e1b(li, j):
            with ExitStack() as ph:
                pp = PsPool(kb, ph, 8, f"e1b{li}_")
                wqk = kb.sb(ph, f"wqk{li}", [128, KC, 2048], BF16)
                wv = kb.sb(ph, f"wv{li}", [128, KC, 1024], BF16)
                h32s = [kb.sb(ph, f"eh32_{li}_{i}", [128, KC, 512], F32) for i in range(2)]
                hn = kb.sb(ph, f"ehn{li}", [128, KC, 512], BF16)
                sq = kb.sb(ph, f"esq{li}", [128, KC, 512], BF16)
                rstd = kb.sb(ph, f"erstd{li}", [128, 512], F32)
                css = [kb.sb(ph, f"ecs{li}_{i}", [128, 512], F32) for i in range(2)]
                sns = [kb.sb(ph, f"esn{li}_{i}", [128, 512], F32) for i in range(2)]
                qbt = [kb.sb(ph, f"eqb{li}_{i}", [128, 512], BF16) for i in range(2)]
                tA = [kb.sb(ph, f"etA{li}_{i}", [128, 512], F32) for i in range(2)]
                tB = [kb.sb(ph, f"etB{li}_{i}", [128, 512], F32) for i in range(2)]
                qr = [kb.sb(ph, f"eqr{li}_{i}", [128, 512], BF16) for i in range(3)]
                vt = [kb.sb(ph, f"evt{li}_{i}", [128, 1024], BF16) for i in range(2)]
                cast_load(wqk, ev_wqk[j].rearrange("(k p) n -> p k n", p=128), KC, 2048, h32s)
                cast_load(wv, ev_wv[j].rearrange("(k p) n -> p k n", p=128), KC, 1024, h32s)
                load_h(h32s[0], 0)
                n = 0
                for b in range(NB):
                    h32 = h32s[b % 2]
                    cs, sn = css[b % 2], sns[b % 2]
                    kb.dma(kb.sp, cs[:, :], tabs[0][:, b * 512:(b + 1) * 512], writes=[cs])
                    kb.dma(kb.sp, sn[:, :], tabs[1][:, b * 512:(b + 1) * 512], writes=[sn])
                    if b + 1 < NB:
                        load_h(h32s[(b + 1) % 2], b + 1)
                    rmsnorm_fm(pp, h32, nf[:, (5 + li) * KC:(6 + li) * KC], hn, sq, rstd)
                    for c in range(16 if cfg.get('dbg', 3) >= 2 else 0):
                        p = pp.get()
                        for k in range(KC):
                            kb.op(kb.pe, lambda: nc.tensor.matmul(p[:, :], lhsT=wqk[:, k, c * 128:(c + 1) * 128], rhs=hn[:, k, :],
                                                                  start=(k == 0), stop=(k == KC - 1)), reads=[wqk, hn], writes=[p])
                        o = qr[n % 3]
                        if cfg.get('dbg2', 0) == 1:
                            copy_on(kb.act, o[:, :], p[:, :], [p], [o])
                            pp.put(p)
                        else:
                            rope_chunk(pp, p, psw64, cs, sn, qbt[n % 2], tA[n % 2], tB[n % 2], o[:, :], o)
                        n += 1
                        dst = (qT if c < 8 else kT)[c % 8]
                        kb.dma(kb.pool, dst[:, b * 512:(b + 1) * 512], o[:, :], reads=[o])
                    for i in range(4 if cfg.get('dbg', 3) >= 3 else 0):
                        v = vt[i % 2]
                        for half in range(2):
                            p = pp.get()
                            for k in range(KC):
                                kb.op(kb.pe, lambda: nc.tensor.matmul(p[:, :], lhsT=hn[:, k, i * 128:(i + 1) * 128], rhs=wv[:, k, half * 512:(half + 1) * 512],
                                                                      start=(k == 0), stop=(k == KC - 1)), reads=[wv, hn], writes=[p])
                            copy_on(kb.act if half else kb.dve, v[:, half * 512:(half + 1) * 512], p[:, :], [p], [v])
                            pp.put(p)
                        kb.dma(kb.pool, vtm[b * 512 + i * 128:b * 512 + (i + 1) * 128, :], v[:, :], reads=[v])
            kb.barrier()

        def phase_att(li, j, kind):
            diff = kind == "diff"
            dv = 128 if diff else 64
            nmap = 2 if diff else 1
            rows = 64 if diff else 128
            scale = 64 ** -0.5 if diff else 96 ** -0.5
            NKT = SEQ // 128
            NQB = SEQ // 512
            lam_init = 0.8 - 0.6 * float(np.exp(-0.3 * li))
            with ExitStack() as ph:
                sp_ = [kb.ps(ph, f"as{li}_{i}", [128, 512], F32) for i in range(2)]
                ob = [kb.ps(ph, f"ao{li}_{i}", [128, 512], F32) for i in range(4)]
                tp = [kb.ps(ph, f"at{li}_{i}", [128, 512], F32) for i in range(2)]
                qts = [kb.sb(ph, f"aq{li}_{i}", [128, SEQ], BF16) for i in range(2)]
                kts = [kb.sb(ph, f"ak{li}_{i}", [128, SEQ], BF16) for i in range(2)]
                vts = [kb.sb(ph, f"av{li}_{i}", [128, NKT, dv + 1], BF16) for i in range(2)]
                pts = [kb.sb(ph, f"ap{li}_{i}", [128, 512], BF16) for i in range(3)]
                o1 = [kb.sb(ph, f"ao1{li}_{i}", [128, 128], F32) for i in range(4)]
                of = [kb.sb(ph, f"aof{li}_{i}", [128, 128], F32) for i in range(2)]
                obf = [kb.sb(ph, f"aob{li}_{i}", [128, 128], BF16) for i in range(2)]
                junk = kb.sb(ph, f"ajk{li}", [128, 128], F32)
                junk2 = kb.sb(ph, f"ajk2{li}", [128, 128], F32)
                sm = [kb.sb(ph, f"asm{li}_{i}", [128, 8], F32) for i in range(4)]
                ybt = [kb.sb(ph, f"ayb{li}_{i}", [128, 512], BF16) for i in range(2)]
                for v in vts:
                    kb.op(kb.dve, lambda: nc.vector.memset(v[:, :, dv:dv + 1], 1.0), writes=[v])
                if diff:
                    row = kb.sb(ph, f"arow{li}", [128, EVROW], F32)
                    kb.dma(kb.sp, row[:, :], ev_row[j], writes=[row])
                    lw = kb.sb(ph, f"alw{li}", [128, 128], F32)
                    lam = kb.sb(ph, f"alam{li}", [128, 8], F32)
                    srow = kb.sb(ph, f"asrow{li}", [128, 128], F32)
                    L0 = 48 + 1024
                    kb.op(kb.dve, lambda: nc.vector.tensor_tensor(out=lw[:, 0:64], in0=row[:, L0:L0 + 64], in1=row[:, L0 + 64:L0 + 128], op=ALU.mult), reads=[row], writes=[lw])
                    kb.op(kb.dve, lambda: nc.vector.tensor_tensor(out=lw[:, 64:128], in0=row[:, L0 + 128:L0 + 192], in1=row[:, L0 + 192:L0 + 256], op=ALU.mult), reads=[row, lw], writes=[lw])
                    kb.op(kb.dve, lambda: nc.vector.reduce_sum(out=lam[:, 0:1], in_=lw[:, 0:64], axis=AX.X), reads=[lw], writes=[lam])
                    kb.op(kb.dve, lambda: nc.vector.reduce_sum(out=lam[:, 1:2], in_=lw[:, 64:128], axis=AX.X), reads=[lw, lam], writes=[lam])
                    kb.op(kb.act, lambda: nc.scalar.activation(out=lam[:, 2:4], in_=lam[:, 0:2], func=AF.Exp), reads=[lam], writes=[lam])
                    kb.op(kb.dve, lambda: nc.vector.scalar_tensor_tensor(out=lam[:, 4:5], in0=lam[:, 3:4], scalar=-lam_init, in1=lam[:, 2:3], op0=ALU.add, op1=ALU.subtract),
                          reads=[lam], writes=[lam])
                    nlam = lam[:, 4:5]
                    kb.op(kb.dve, lambda: nc.vector.tensor_scalar(out=srow[:, :], in0=row[:, L0 + 256:L0 + 384], scalar1=float(1.0 - lam_init), scalar2=0.0, op0=ALU.mult, op1=ALU.add),
                          reads=[row], writes=[srow])
                nh = 8
                it = 0
                npt = 0
                for s_ in range(NSEQ):
                    base = s_ * SEQ
                    for h in range(nh):
                        qt, kt_, vt = qts[it % 2], kts[it % 2], vts[it % 2]
                        it += 1
                        kb.dma(kb.sp, qt[:, :], qT[h][:, base:base + SEQ], writes=[qt])
                        kb.dma(kb.sp, kt_[:, :], kT[h][:, base:base + SEQ], writes=[kt_])
                        vsrc = vtm[base:base + SEQ, h * dv:(h + 1) * dv] if diff else vtm[base:base + SEQ, h * dv:(h + 1) * dv]
                        vsrc3 = vsrc.rearrange("(kt p) e -> p kt e", p=128)
                        for k0 in range(0, NKT, 8):
                            k1 = min(NKT, k0 + 8)
                            kb.dma(kb.sp, vt[:, k0:k1, 0:dv], vsrc3[:, k0:k1, :], writes=[vt])
                        for qb in range(NQB):
                            for mp in range(nmap):
                                r0 = mp * rows
                                nkt = 4 * qb + 4
                                def s_mm(kt):
                                    c0 = max(0, kt - 4 * qb)
                                    sp = sp_[kt % 2]
                                    kb.op(kb.pe, lambda: nc.tensor.matmul(sp[:, c0 * 128:512], lhsT=kt_[r0:r0 + rows, kt * 128:(kt + 1) * 128],
                                                                          rhs=qt[r0:r0 + rows, qb * 512 + c0 * 128:(qb + 1) * 512], start=True, stop=True),
                                          reads=[kt_, qt], writes=[sp])
                                s_mm(0)
                                for kt in range(nkt):
                                    if kt + 1 < nkt:
                                        s_mm(kt + 1)
                                    c0 = max(0, kt - 4 * qb)
                                    sp = sp_[kt % 2]
                                    pt = pts[npt % 3]
                                    npt += 1
                                    kb.op(kb.act, lambda: nc.scalar.activation(out=pt[:, c0 * 128:512], in_=sp[:, c0 * 128:512], func=AF.Exp, scale=float(scale)),
                                          reads=[sp], writes=[pt])
                                    if kt >= 4 * qb:
                                        ii = kt - 4 * qb
                                        kb.op(kb.pool, lambda: nc.gpsimd.memset(pt[64:128, ii * 128:ii * 128 + 64], 0.0), writes=[pt])
                                    for ii in range(c0, 4):
                                        kb.op(kb.pe, lambda: nc.tensor.matmul(ob[ii][:, 0:dv + 1], lhsT=pt[:, ii * 128:(ii + 1) * 128], rhs=vt[:, kt, :],
                                                                              start=(kt == 0), stop=(kt == 4 * qb + ii)), reads=[pt, vt], writes=[ob[ii]])
                                for ii in range(4):
                                    O = ob[ii]
                                    smi = sm[ii]
                                    kb.op(kb.dve, lambda: nc.vector.reciprocal(out=smi[:, mp:mp + 1], in_=O[:, dv:dv + 1]), reads=[O], writes=[smi])
                                    if diff and mp == 0:
                                        kb.op(kb.dve, lambda: nc.vector.tensor_scalar(out=o1[ii][:, :], in0=O[:, 0:dv], scalar1=smi[:, 0:1], scalar2=0.0, op0=ALU.mult, op1=ALU.add),
                                              reads=[O, smi], writes=[o1[ii]])
                                        continue
                                    tpb = tp[(it + qb) % 2]
                                    tpv = tpb[:, :].bitcast(BF16)
                                    if diff:
                                        f, fb = of[ii % 2], obf[ii % 2]
                                        kb.op(kb.dve, lambda: nc.vector.tensor_tensor(out=smi[:, 2:3], in0=smi[:, 1:2], in1=nlam, op=ALU.mult), reads=[smi, lam], writes=[smi])
                                        kb.op(kb.dve, lambda: nc.vector.tensor_scalar(out=junk[:, :], in0=O[:, 0:dv], scalar1=smi[:, 2:3], scalar2=0.0, op0=ALU.mult, op1=ALU.add),
                                              reads=[O, smi], writes=[junk])
                                        kb.op(kb.dve, lambda: nc.vector.tensor_tensor(out=f[:, :], in0=junk[:, :], in1=o1[ii][:, :], op=ALU.add),
                                              reads=[junk, o1[ii]], writes=[f])
                                        kb.op(kb.act, lambda: nc.scalar.activation(out=junk2[:, :], in_=f[:, :], func=AF.Square, accum_out=smi[:, 3:4]),
                                              reads=[f], writes=[junk2, smi])
                                        kb.op(kb.act, lambda: nc.scalar.activation(out=smi[:, 4:5], in_=smi[:, 3:4], func=AF.Sqrt, scale=1.0 / 128, bias=epsc[:, 0:1]),
                                              reads=[smi, epsc], writes=[smi])
                                        kb.op(kb.dve, lambda: nc.vector.reciprocal(out=smi[:, 5:6], in_=smi[:, 4:5]), reads=[smi], writes=[smi])
                                        kb.op(kb.dve, lambda: nc.vector.scalar_tensor_tensor(out=fb[:, :], in0=f[:, :], scalar=smi[:, 5:6], in1=srow[:, :],
                                                                                             op0=ALU.mult, op1=ALU.mult), reads=[f, smi, srow], writes=[fb])
                                        kb.op(kb.pe, lambda: nc.tensor.transpose(out=tpv[:, ii * 128:(ii + 1) * 128], in_=fb[:, :], identity=identb[:, :]),
                                              reads=[fb, identb], writes=[tpb])
                                    else:
                                        fb = obf[ii % 2]
                                        kb.op(kb.dve, lambda: nc.vector.tensor_scalar(out=fb[:, 0:dv], in0=O[:, 0:dv], scalar1=smi[:, 0:1], scalar2=0.0, op0=ALU.mult, op1=ALU.add),
                                              reads=[O, smi], writes=[fb])
                                        po = (h % 2) * 64
                                        kb.op(kb.pe, lambda: nc.tensor.transpose(out=tpv[po:po + 64, ii * 128:(ii + 1) * 128], in_=fb[:, 0:dv], identity=identb[:, :]),
                                              reads=[fb, identb], writes=[tpb])
                                if mp == nmap - 1:
                                    tpb = tp[(it + qb) % 2]
                                    tpv = tpb[:, :].bitcast(BF16)
                                    y = ybt[qb % 2]
                                    if diff:
                                        copy_on(kb.act, y[:, :], tpv[:, 0:512], [tpb], [y])
                                        kb.dma(kb.pool, mixT[8 + h][:, base + qb * 512:base + (qb + 1) * 512], y[:, :], reads=[y])
                                    else:
                                        po = (h % 2) * 64
                                        copy_on(kb.act, y[po:po + 64, :], tpv[po:po + 64, 0:512], [tpb], [y])
                                        kb.dma(kb.pool, mixT[4 + h // 2][po:po + 64, base + qb * 512:base + (qb + 1) * 512], y[po:po + 64, :], reads=[y])
            kb.barrier()

        def phase_op(li, w_src, lo, hi):
            nk = hi - lo
            with ExitStack() as ph:
                pp = PsPool(kb, ph, 4, f"op{li}_")
                wo = kb.sb(ph, f"wo{li}", [128, nk, D], BF16)
                h32s = [kb.sb(ph, f"oh32_{li}_{i}", [128, KC, 512], F32) for i in range(2)]
                mxs = [kb.sb(ph, f"omx{li}_{i}", [128, nk, 512], BF16) for i in range(2)]
                cast_load(wo, w_src.rearrange("(k p) n -> p k n", p=128)[:, lo:hi, :], nk, D, h32s)

                def loads(b):
                    load_h(h32s[b % 2], b)
                    kb.dma(kb.sp, mxs[b % 2][:, :, :], mixT[lo:hi, :, b * 512:(b + 1) * 512].rearrange("c p t -> p c t"), writes=[mxs[b % 2]])
                loads(0)
                for b in range(NB):
                    h32, mx = h32s[b % 2], mxs[b % 2]
                    if b + 1 < NB:
                        loads(b + 1)
                    for oc in range(KC):
                        p = pp.get()
                        for k in range(nk):
                            kb.op(kb.pe, lambda: nc.tensor.matmul(p[:, :], lhsT=wo[:, k, oc * 128:(oc + 1) * 128], rhs=mx[:, k, :],
                                                                  start=(k == 0), stop=(k == nk - 1)), reads=[wo, mx], writes=[p])
                        kb.op(kb.dve, lambda: nc.vector.tensor_tensor(out=h32[:, oc, :], in0=h32[:, oc, :], in1=p[:, :], op=ALU.add),
                              reads=[h32, p], writes=[h32])
                        pp.put(p)
                    kb.dma(kb.pool, hT_t[:, :, b * 512:(b + 1) * 512].rearrange("c p t -> p c t"), h32[:, :, :], reads=[h32], writes=[hT[b]])
            kb.barrier()

        def phase_ff(li, last):
            with ExitStack() as ph:
                pp = PsPool(kb, ph, 8, f"ff{li}_")
                wg = kb.sb(ph, f"wg{li}", [128, KC, FH], BF16)
                wu = kb.sb(ph, f"wu{li}", [128, KC, FH], BF16)
                wd = kb.sb(ph, f"wd{li}", [128, HC, D], BF16)
                actb = kb.sb(ph, f"actb{li}", [128, HC, 512], BF16)
                h32s = [kb.sb(ph, f"h32_{li}_{i}", [128, KC, 512], F32) for i in range(2)]
                hn = kb.sb(ph, f"hn{li}", [128, KC, 512], BF16)
                rstd = kb.sb(ph, f"rstd{li}", [128, 512], F32)
                sg = [kb.sb(ph, f"sg{li}_{i}", [128, 512], F32) for i in range(2)]
                stg = [Buf(h32s[i].t[:, :, :].rearrange("p c t -> p (c t)"), f"stg{i}") for i in range(2)]
                stg = [h32s[0], h32s[1]]

                def stage_ap(b, ncol):
                    return b.t[:, :, :].rearrange("p c t -> p (c t)")[:, 0:ncol]

                rot = 0
                for (w, src, kcn, n) in ((wg, ffn_g[li].rearrange("(k p) n -> p k n", p=128), KC, FH),
                                         (wu, ffn_u[li].rearrange("(k p) n -> p k n", p=128), KC, FH),
                                         (wd, ffn_d[li].rearrange("(k p) n -> p k n", p=128), HC, D)):
                    per = max(1, 4096 // n)
                    for k0 in range(0, kcn, per):
                        k1 = min(kcn, k0 + per)
                        st = stg[rot % 2]
                        sap = stage_ap(st, (k1 - k0) * n).rearrange("p (k n) -> p k n", n=n)
                        kb.dma(kb.sp if rot % 2 == 0 else kb.pool, sap, src[:, k0:k1, :], writes=[st])
                        if rot % 2:
                            kb.op(kb.act, lambda: nc.scalar.copy(out=w[:, k0:k1, :], in_=sap), reads=[st], writes=[w])
                        else:
                            kb.op(kb.dve, lambda: nc.vector.tensor_copy(out=w[:, k0:k1, :], in_=sap), reads=[st], writes=[w])
                        rot += 1

                load_h(h32s[0], 0)
                for b in range(NB):
                    h32 = h32s[b % 2]
                    if b + 1 < NB:
                        load_h(h32s[(b + 1) % 2], b + 1)
                    rmsnorm_fm(pp, h32, nf[:, li * KC:(li + 1) * KC], hn, actb, rstd)
                    for hc in range(HC):
                        pg, pu = pp.get(), pp.get()
                        for k in range(KC):
                            kb.op(kb.pe, lambda: nc.tensor.matmul(pg[:, :], lhsT=wg[:, k, hc * 128:(hc + 1) * 128], rhs=hn[:, k, :],
                                                                  start=(k == 0), stop=(k == KC - 1)), reads=[wg, hn], writes=[pg])
                        for k in range(KC):
                            kb.op(kb.pe, lambda: nc.tensor.matmul(pu[:, :], lhsT=wu[:, k, hc * 128:(hc + 1) * 128], rhs=hn[:, k, :],
                                                                  start=(k == 0), stop=(k == KC - 1)), reads=[wu, hn], writes=[pu])
                        s = sg[hc % 2]
                        kb.op(kb.act, lambda: nc.scalar.activation(out=s[:, :], in_=pg[:, :], func=AF.Silu), reads=[pg], writes=[s])
                        pp.put(pg)
                        kb.op(kb.dve, lambda: nc.vector.tensor_tensor(out=actb[:, hc, :], in0=s[:, :], in1=pu[:, :], op=ALU.mult),
                              reads=[s, pu], writes=[actb])
                        pp.put(pu)
                    for oc in range(KC):
                        p = pp.get()
                        for k in range(HC):
                            kb.op(kb.pe, lambda: nc.tensor.matmul(p[:, :], lhsT=wd[:, k, oc * 128:(oc + 1) * 128], rhs=actb[:, k, :],
                                                                  start=(k == 0), stop=(k == HC - 1)), reads=[wd, actb], writes=[p])
                        kb.op(kb.dve, lambda: nc.vector.tensor_tensor(out=h32[:, oc, :], in0=h32[:, oc, :], in1=p[:, :], op=ALU.add),
                              reads=[h32, p], writes=[h32])
                        pp.put(p)
                    if not last:
                        kb.dma(kb.pool, hT_t[:, :, b * 512:(b + 1) * 512].rearrange("c p t -> p c t"), h32[:, :, :],
                               reads=[h32], writes=[hT[b]])
                    else:
                        rmsnorm_fm(pp, h32, nf[:, 4 * KC:5 * KC], hn, actb, rstd)
                        for c in range(KC):
                            kb.op(kb.dve, lambda: nc.vector.scalar_tensor_tensor(out=h32[:, c, :], in0=h32[:, c, :], scalar=nf[:, 4 * KC + c:4 * KC + c + 1],
                                                                                 in1=rstd[:, :], op0=ALU.mult, op1=ALU.mult),
                                  reads=[h32, rstd, nf], writes=[h32])
                        for i in range(4):
                            for half in range(2):
                                p = pp.get()
                                for cc in range(4):
                                    c = half * 4 + cc
                                    kb.op(kb.pe, lambda: nc.tensor.transpose(out=p[:, cc * 128:(cc + 1) * 128], in_=h32[:, c, i * 128:(i + 1) * 128],
                                                                             identity=ident32), reads=[h32, cst], writes=[p])
                                o = sg[half]
                                kb.op(kb.act if half else kb.dve,
                                      (lambda: nc.scalar.copy(out=o[:, :], in_=p[:, :])) if half else (lambda: nc.vector.tensor_copy(out=o[:, :], in_=p[:, :])),
                                      reads=[p], writes=[o])
                                pp.put(p)
                                kb.dma(kb.pool, out[b * 512 + i * 128: b * 512 + (i + 1) * 128, half * 512:(half + 1) * 512], o[:, :],
                                       reads=[o], writes=[outb])
            kb.barrier()

        for li in layers:
            j = li // 2
            upto = cfg.get("upto", "all")
            if li % 2 == 0:
                if cfg.get("ssd", True):
                    phase_e1a(li, j)
                    if not cfg.get("att", True):
                        phase_op(li, ev_wout[j], 0, 8)
                if cfg.get("att", True):
                    if upto in ("e1b", "att", "op", "all"):
                        phase_e1b(li, j)
                    if upto in ("att", "op", "all"):
                        phase_att(li, j, "diff")
                    lo = 0 if cfg.get("ssd", True) else 8
                    if upto in ("op", "all"):
                        phase_op(li, ev_wout[j], lo, 16)
            if li % 2 == 1:
                s5, mla = cfg.get("s5", True), cfg.get("mla", True)
                if s5:
                    phase_o1a(li, j)
                if mla:
                    phase_o1b(li, j)
                    phase_att(li, j, "mla")
                if s5 or mla:
                    phase_op(li, od_wout[j], 0 if s5 else 4, 8 if mla else 4)
            if cfg.get("ffn", True):
                phase_ff(li, li == layers[-1])

        kb.barrier()
    return nc, kb


def host_consts():
    c = np.zeros((128, 1024), np.float32)
    c[:, 0:128] = np.eye(128, dtype=np.float32)
    p = np.arange(128)
    th = np.float32(10000.0)
    c[:, 128] = th ** (-(p % 32).astype(np.float32) * np.float32(2.0 / 64))
    f32 = th ** (-((p - 64) % 16).astype(np.float32) * np.float32(2.0 / 32))
    c[:, 129] = np.where((p >= 64) & (p < 96), f32, 0.0)
    for m in range(128):
        if (m % 64) < 32:
            c[m + 32, 256 + m] = -1.0
        else:
            c[m - 32, 256 + m] = 1.0
    for m in range(64, 80):
        c[m + 16, 384 + m] = -1.0
    for m in range(80, 96):
        c[m - 16, 384 + m] = 1.0
    t = np.arange(128)
    c[:, 512:640] = (t[:, None] <= t[None, :]).astype(np.float32)
    c[:, 640:768] = np.where(t[:, None] > t[None, :], -30000.0, 0.0)
    return c


def fm_cols(v, nch):
    return np.ascontiguousarray(np.asarray(v, np.float32).reshape(nch, 128).T)


def rep(v):
    v = np.asarray(v, np.float32).reshape(1, -1)
    return np.ascontiguousarray(np.broadcast_to(v, (128, v.shape[1])))


CFG = dict(T=8192, SEQ=4096, layers=[0, 1, 2, 3])


def make_in_maps(inputs, cfg, ncores):
    T = cfg["T"]
    f = lambda k: np.asarray(inputs[k], np.float32)
    x = f("x").reshape(-1, D)
    pos = np.asarray(inputs["positions"], np.int32).reshape(-1)
    nfn = np.concatenate([fm_cols(inputs["norm_ffn"][i], KC) for i in range(4)], axis=1)
    nmx = np.concatenate([fm_cols(inputs["norm_mix"][i], KC) for i in range(4)], axis=1)
    wi = f("ev_w_in")
    shared = {
        "consts": host_consts(),
        "norm_ffn": np.ascontiguousarray(nfn),
        "norm_mix": np.ascontiguousarray(nmx),
        "norm_final": fm_cols(inputs["norm_final"], KC),
        "ffn_gate": f("ffn_gate"), "ffn_up": f("ffn_up"), "ffn_down": f("ffn_down"),
        "ev_wqk": np.ascontiguousarray(wi[:, :, 2576:4624]),
        "ev_wv": np.ascontiguousarray(wi[:, :, 4624:5648]),
        "ev_row": np.stack([np.concatenate([rep(inputs["ev_dt_bias"][j]), rep(inputs["ev_a_log"][j]), rep(inputs["ev_d_skip"][j]),
                                            rep(inputs["ev_gate_norm"][j]), rep(np.asarray(inputs["ev_lambdas"][j]).reshape(-1)),
                                            rep(inputs["ev_subln"][j])], axis=1) for j in range(2)]),
        "ev_wout": f("ev_w_out"),
        "ev_wxz": np.ascontiguousarray(np.concatenate([wi[:, :, 1024:2560], wi[:, :, 0:1024], wi[:, :, 2560:2576]], axis=2)),
        "ev_conv": np.stack([np.concatenate([np.ascontiguousarray(np.asarray(inputs["ev_conv_w"][j], np.float32).T).reshape(12, 128, 4).transpose(1, 0, 2).reshape(128, 48),
                                             fm_cols(inputs["ev_conv_b"][j], 12)], axis=1) for j in range(2)]),
    }
    wo = f("od_w_in")
    krpad = np.zeros((2, D, 128), np.float32)
    krpad[:, :, 64:96] = wo[:, :, 1152:1184]
    shared["iota"] = np.ascontiguousarray(np.broadcast_to(np.arange(1, 513, dtype=np.float32)[None, :], (128, 512)))
    shared["od_wu"] = np.ascontiguousarray(wo[:, :, 0:512])
    shared["od_wf"] = np.ascontiguousarray(np.concatenate([wo[:, :, 512:1152], krpad], axis=2))
    def pt_layout(v):
        return np.ascontiguousarray(np.asarray(v, np.float32).reshape(16, 128).T)
    od_pt, od_rowp, od_bt, od_ct, od_nrm, wuq_p, wuk_p, wuv_p = [], [], [], [], [], [], [], []
    for j in range(2):
        lre, lim = f("od_lam_re")[j], f("od_lam_im")[j]
        stp = np.repeat(f("od_log_step")[j][:, None], 64, axis=1)
        od_pt.append(np.concatenate([pt_layout(lre), pt_layout(lim), pt_layout(stp), fm_cols(inputs["od_d_skip"][j], 4), fm_cols(inputs["od_b_glu"][j], 4),
                                     np.zeros((128, 8), np.float32)], axis=1))
        od_rowp.append(np.concatenate([rep(lre.reshape(-1)), rep(lim.reshape(-1)), rep(stp.reshape(-1))], axis=1))
        bts, cts = [], []
        for src in (f("od_b_re")[j], f("od_b_im")[j]):
            bt = np.zeros((128, 16, 128), np.float32)
            for g in range(32):
                k, r = g // 2, (g // 2) % 4
                rows = slice(32 * r + (g % 2) * 16, 32 * r + (g % 2) * 16 + 16)
                bt[rows, k, (g % 2) * 64:(g % 2) * 64 + 64] = src[g].T
            bts.append(bt)
        for src in (f("od_c_re")[j], f("od_c_im")[j]):
            ct = np.zeros((128, 16, 128), np.float32)
            for g in range(32):
                k = g // 2
                ct[(g % 2) * 64:(g % 2) * 64 + 64, k, (g % 8) * 16:(g % 8) * 16 + 16] = src[g].T
            cts.append(ct)
        od_bt.append(np.stack(bts)); od_ct.append(np.stack(cts))
        od_nrm.append(np.concatenate([fm_cols(inputs["od_q_norm"][j], 3), fm_cols(inputs["od_kv_norm"][j], 2)], axis=1))
        uq = f("od_w_uq")[j].reshape(384, 8, 96)
        uqp = np.zeros((384, 8, 128), np.float32); uqp[:, :, 0:96] = uq
        ukv = f("od_w_ukv")[j].reshape(256, 8, 128)
        ukp = np.zeros((256, 8, 128), np.float32); ukp[:, :, 0:64] = ukv[:, :, 0:64]
        wuq_p.append(uqp.reshape(384, 1024)); wuk_p.append(ukp.reshape(256, 1024))
        wuv_p.append(np.ascontiguousarray(ukv[:, :, 64:128]).reshape(256, 512))
    shared.update({"od_pt": np.stack(od_pt), "od_rowp": np.stack(od_rowp), "od_bt": np.stack(od_bt), "od_ct": np.stack(od_ct),
                   "od_nrm": np.stack(od_nrm), "od_wuq": np.stack(wuq_p), "od_wuk": np.stack(wuk_p), "od_wuv": np.stack(wuv_p),
                   "od_wglu": f("od_w_glu"), "od_wout": f("od_w_out")})
    maps = []
    for c in range(ncores):
        m = dict(shared)
        m["x"] = np.ascontiguousarray(x[c * T:(c + 1) * T])
        m["posb"] = np.ascontiguousarray(np.broadcast_to(pos[None, c * T:(c + 1) * T], (128, T)))
        maps.append(m)
    return maps


def kernel(**inputs):
    cfg = CFG
    nc, kb = build(cfg)
    maps = make_in_maps(inputs, cfg, NCORES)
    res = run_bass_kernel_spmd(nc, maps, core_ids=list(range(NCORES)))
    outs = [np.asarray(r["out"], np.float32) for r in res.results]
    B, S = inputs["x"].shape[0], inputs["x"].shape[1]
    return np.concatenate(outs, axis=0).reshape(B, S, D)
```

```python
from contextlib import ExitStack
import numpy as np
import concourse.bass as bass
import concourse.mybir as mybir
from concourse.bass_utils import run_bass_kernel_spmd

F32 = mybir.dt.float32
BF16 = mybir.dt.bfloat16
I32 = mybir.dt.int32
AF = mybir.ActivationFunctionType
ALU = mybir.AluOpType
AX = mybir.AxisListType

D = 1024
KC = 8
FH = 2816
HC = 22
EPS = 1e-6
EVROW = 48 + 1024 + 256 + 128
NCORES = 8


class Buf:
    __slots__ = ("t", "name", "excl", "st")

    def __init__(self, t, name="", excl=False, st=None):
        self.t = t
        self.name = name
        self.excl = excl
        self.st = st if st is not None else [None, {}]

    @property
    def w(self):
        return self.st[0]

    @w.setter
    def w(self, v):
        self.st[0] = v

    @property
    def r(self):
        return self.st[1]

    @r.setter
    def r(self, v):
        self.st[1] = v

    def view(self, ap, name=""):
        return Buf(ap, name or self.name, self.excl, self.st)

    def __getitem__(self, idx):
        return self.t[idx]


class Eng:
    def __init__(self, name, e, sem):
        self.name = name
        self.e = e
        self.sem = sem
        self.count = 0
        self.seen = {}


class KB:
    def __init__(self, nc, es):
        self.nc = nc
        self.es = es
        mk = lambda n: es.enter_context(nc.semaphore(n))
        self.pe = Eng("pe", nc.tensor, mk("s_pe"))
        self.act = Eng("act", nc.scalar, mk("s_act"))
        self.dve = Eng("dve", nc.vector, mk("s_dve"))
        self.pool = Eng("pool", nc.gpsimd, mk("s_pool"))
        self.sp = Eng("sp", nc.sync, mk("s_sp"))
        self.engs = [self.pe, self.act, self.dve, self.pool, self.sp]
        self.dsem = {}
        for q, n in ((self.sp, 24), (self.pool, 16), (self.act, 8)):
            self.dsem[q.name] = [[mk(f"d_{q.name}{i}"), 0] for i in range(n)]
        self.dptr = {q: 0 for q in self.dsem}
        self.nwait = 0
        self.ninst = 0

    def _wait(self, eng, deps, attach=False):
        need = []
        for (s, v) in deps:
            if s is eng.sem and eng is self.pe:
                continue
            if eng.seen.get(s, 0) < v:
                need.append((s, v))
                eng.seen[s] = v
        last = need.pop() if (attach and need) else None
        for (s, v) in need:
            eng.e.wait_ge(s, v)
            self.nwait += 1
        return last

    @staticmethod
    def _deps(reads, writes):
        deps = {}
        for b in reads:
            if b.w is not None:
                s, v = b.w
                if deps.get(s, 0) < v:
                    deps[s] = v
            if b.excl:
                for s, v in b.r.items():
                    if deps.get(s, 0) < v:
                        deps[s] = v
        for b in writes:
            if b.w is not None:
                s, v = b.w
                if deps.get(s, 0) < v:
                    deps[s] = v
            for s, v in b.r.items():
                if deps.get(s, 0) < v:
                    deps[s] = v
        return list(deps.items())

    @staticmethod
    def _mark(tok, reads, writes):
        s, v = tok
        for b in reads:
            b.r[s] = v
        for b in writes:
            b.w = tok
            b.r = {}

    def op(self, eng, fn, reads=(), writes=()):
        last = self._wait(eng, self._deps(reads, writes), attach=True)
        ins = fn()
        if last is not None:
            ins._wait_ge(last[0], last[1])
        eng.count += 1
        ins.then_inc(eng.sem, 1)
        self._mark((eng.sem, eng.count), reads, writes)
        self.ninst += 1
        return ins

    def dma(self, q, out_ap, in_ap, reads=(), writes=()):
        self._wait(q, self._deps(reads, writes))
        pool = self.dsem[q.name]
        i = self.dptr[q.name]
        self.dptr[q.name] = (i + 1) % len(pool)
        ent = pool[i]
        if ent[1] > 0:
            self._wait(q, [(ent[0], ent[1])])
        q.e.dma_start(out=out_ap, in_=in_ap).then_inc(ent[0], 16)
        ent[1] += 16
        self._mark((ent[0], ent[1]), reads, writes)
        self.ninst += 1

    def barrier(self):
        toks = [(e.sem, e.count) for e in self.engs if e.count > 0]
        for pool in self.dsem.values():
            toks += [(s, v) for s, v in pool if v > 0]
        for e in self.engs:
            for (s, v) in toks:
                if e.seen.get(s, 0) < v:
                    e.e.wait_ge(s, v)
                    e.seen[s] = v
                    self.nwait += 1

    def sb(self, es, name, shape, dt):
        return Buf(es.enter_context(self.nc.sbuf_tensor(name, list(shape), dt)), name)

    def ps(self, es, name, shape, dt=F32):
        return Buf(es.enter_context(self.nc.psum_tensor(name, list(shape), dt)), name, excl=True)


class PsPool:
    def __init__(self, kb, es, n, prefix):
        self.banks = [kb.ps(es, f"{prefix}{i}", [128, 512], F32) for i in range(n)]
        self.free = list(self.banks)

    def get(self):
        assert self.free, "PSUM pool exhausted"
        return self.free.pop(0)

    def put(self, b):
        self.free.append(b)


def bf(ap):
    return ap.bitcast(BF16)


def build(cfg):
    T = cfg["T"]
    SEQ = cfg["SEQ"]
    NSEQ = T // SEQ
    NB = T // 512
    layers = cfg["layers"]
    nc = bass.Bass("TRN2", target_bir_lowering=False)
    dram_in = lambda n, s, dt=F32: nc.dram_tensor(n, list(s), dt, kind="ExternalInput").ap()
    x_in = dram_in("x", [T, D])
    consts = dram_in("consts", [128, 1024])
    norm_ffn = dram_in("norm_ffn", [128, 4 * KC])
    norm_fin = dram_in("norm_final", [128, KC])
    ffn_g = dram_in("ffn_gate", [4, D, FH])
    ffn_u = dram_in("ffn_up", [4, D, FH])
    ffn_d = dram_in("ffn_down", [4, FH, D])
    posb = dram_in("posb", [128, T], I32)
    norm_mix = dram_in("norm_mix", [128, 4 * KC])
    ev_wqk = dram_in("ev_wqk", [2, D, 2048])
    ev_wv = dram_in("ev_wv", [2, D, 1024])
    ev_row = dram_in("ev_row", [2, 128, EVROW])
    ev_wxz = dram_in("ev_wxz", [2, D, 2576])
    ev_conv = dram_in("ev_conv", [2, 128, 60])
    iota_in = dram_in("iota", [128, 512])
    od_wu = dram_in("od_wu", [2, D, 512])
    od_wf = dram_in("od_wf", [2, D, 768])
    od_pt = dram_in("od_pt", [2, 128, 64])
    od_rowp = dram_in("od_rowp", [2, 128, 3 * 2048])
    od_bt = dram_in("od_bt", [2, 2, 128, 16, 128])
    od_ct = dram_in("od_ct", [2, 2, 128, 16, 128])
    od_wglu = dram_in("od_wglu", [2, 512, 512])
    od_wuq = dram_in("od_wuq", [2, 384, 1024])
    od_wuk = dram_in("od_wuk", [2, 256, 1024])
    od_wuv = dram_in("od_wuv", [2, 256, 512])
    od_nrm = dram_in("od_nrm", [2, 128, 5])
    od_wout = dram_in("od_wout", [2, D, D])
    ev_wout = dram_in("ev_wout", [2, 2048, D])
    out = nc.dram_tensor("out", [T, D], F32, kind="ExternalOutput").ap()
    hT_t = nc.dram_tensor("hT", [KC, 128, T], F32, kind="Internal").ap()
    mixT = nc.dram_tensor("mixT", [16, 128, T], BF16, kind="Internal").ap()
    qT = nc.dram_tensor("qT", [8, 128, T], BF16, kind="Internal").ap()
    kT = nc.dram_tensor("kT", [8, 128, T], BF16, kind="Internal").ap()
    vtm = nc.dram_tensor("vtm", [T, 1024], BF16, kind="Internal").ap()
    tabs = nc.dram_tensor("tabs", [4, 128, T], F32, kind="Internal").ap()

    with ExitStack() as es:
        kb = KB(nc, es)
        hT = [Buf(hT_t, f"hT{b}") for b in range(NB)]
        outb = Buf(out, "out")
        cst = kb.sb(es, "cst", [128, 1024], F32)
        kb.dma(kb.sp, cst[:, :], consts[:, :], writes=[cst])
        ident32 = cst[:, 0:128]
        identb = kb.sb(es, "identb", [128, 128], BF16)
        onesb = kb.sb(es, "onesb", [128, 128], BF16)
        kb.op(kb.dve, lambda: nc.vector.tensor_copy(out=identb[:, :], in_=cst[:, 0:128]), reads=[cst], writes=[identb])
        kb.op(kb.dve, lambda: nc.vector.memset(onesb[:, :], 1.0), writes=[onesb])
        epsc = kb.sb(es, "epsc", [128, 1], F32)
        kb.op(kb.dve, lambda: nc.vector.memset(epsc[:, :], EPS), writes=[epsc])
        nf = kb.sb(es, "nf", [128, 9 * KC], F32)
        kb.dma(kb.sp, nf[:, 0:4 * KC], norm_ffn[:, :], writes=[nf])
        kb.dma(kb.sp, nf[:, 4 * KC:5 * KC], norm_fin[:, :], writes=[nf])
        kb.dma(kb.sp, nf[:, 5 * KC:9 * KC], norm_mix[:, :], writes=[nf])
        halfpi = kb.sb(es, "halfpi", [128, 1], F32)
        kb.op(kb.dve, lambda: nc.vector.memset(halfpi[:, :], float(np.pi / 2)), writes=[halfpi])
        psw64 = kb.sb(es, "psw64", [128, 128], BF16)
        psw32 = kb.sb(es, "psw32", [128, 128], BF16)
        kb.op(kb.dve, lambda: nc.vector.tensor_copy(out=psw64[:, :], in_=cst[:, 256:384]), reads=[cst], writes=[psw64])
        kb.op(kb.dve, lambda: nc.vector.tensor_copy(out=psw32[:, :], in_=cst[:, 384:512]), reads=[cst], writes=[psw32])

        def copy_on(e, out_ap, in_ap, reads, writes):
            if e is kb.act:
                kb.op(e, lambda: nc.scalar.copy(out=out_ap, in_=in_ap), reads=reads, writes=writes)
            else:
                kb.op(e, lambda: e.e.tensor_copy(out=out_ap, in_=in_ap), reads=reads, writes=writes)

        def cast_load(w, src3, kcn, n, stg, rot=[0]):
            per = max(1, 4096 // n)
            for k0 in range(0, kcn, per):
                k1 = min(kcn, k0 + per)
                st = stg[rot[0] % len(stg)]
                sap = st.t[:, :, :].rearrange("p c t -> p (c t)")[:, 0:(k1 - k0) * n].rearrange("p (k n) -> p k n", n=n)
                kb.dma(kb.sp if rot[0] % 2 == 0 else kb.pool, sap, src3[:, k0:k1, :], writes=[st])
                copy_on(kb.act if rot[0] % 2 else kb.dve, w[:, k0:k1, :], sap, [st], [w])
                rot[0] += 1

        def rmsnorm_fm(pp, h32, gcol, hn, sq, rstd):
            for c in range(KC):
                kb.op(kb.act, lambda: nc.scalar.activation(out=sq[:, c, :], in_=h32[:, c, :], func=AF.Square),
                      reads=[h32], writes=[sq])
            p = pp.get()
            for c in range(KC):
                kb.op(kb.pe, lambda: nc.tensor.matmul(p[:, :], lhsT=onesb[:, :], rhs=sq[:, c, :], start=(c == 0), stop=(c == KC - 1)),
                      reads=[onesb, sq], writes=[p])
            kb.op(kb.act, lambda: nc.scalar.activation(out=rstd[:, :], in_=p[:, :], func=AF.Sqrt, scale=1.0 / D, bias=epsc[:, 0:1]),
                  reads=[p, epsc], writes=[rstd])
            pp.put(p)
            kb.op(kb.dve, lambda: nc.vector.reciprocal(out=rstd[:, :], in_=rstd[:, :]), reads=[rstd], writes=[rstd])
            for c in range(KC):
                kb.op(kb.dve, lambda: nc.vector.scalar_tensor_tensor(out=hn[:, c, :], in0=h32[:, c, :], scalar=gcol[:, c:c + 1], in1=rstd[:, :],
                                                                     op0=ALU.mult, op1=ALU.mult),
                      reads=[h32, rstd, nf], writes=[hn])

        with ExitStack() as ph:
            pp = PsPool(kb, ph, 4, "pre")
            xt = [kb.sb(ph, f"xt{i}", [128, 4, D], F32) for i in range(2)]
            ht = [kb.sb(ph, f"ht{i}", [128, KC, 512], F32) for i in range(2)]
            for b in range(NB):
                xb, hb = xt[b % 2], ht[b % 2]
                kb.dma(kb.sp, xb[:, :, :], x_in[b * 512:(b + 1) * 512, :].rearrange("(i p) d -> p i d", p=128), writes=[xb])
                for c in range(KC):
                    p = pp.get()
                    for i in range(4):
                        kb.op(kb.pe, lambda: nc.tensor.transpose(out=p[:, i * 128:(i + 1) * 128], in_=xb[:, i, c * 128:(c + 1) * 128], identity=ident32),
                              reads=[xb, cst], writes=[p])
                    e = kb.act if c % 2 else kb.dve
                    if e is kb.act:
                        kb.op(e, lambda: nc.scalar.copy(out=hb[:, c, :], in_=p[:, :]), reads=[p], writes=[hb])
                    else:
                        kb.op(e, lambda: nc.vector.tensor_copy(out=hb[:, c, :], in_=p[:, :]), reads=[p], writes=[hb])
                    pp.put(p)
                kb.dma(kb.pool, hT_t[:, :, b * 512:(b + 1) * 512].rearrange("c p t -> p c t"), hb[:, :, :], reads=[hb], writes=[hT[b]])
        kb.barrier()


        TWO_PI_HI = 6.28125
        TWO_PI_LO = float(2 * np.pi - 6.28125)
        with ExitStack() as ph:
            posi = [kb.sb(ph, f"posi{i}", [128, 512], I32) for i in range(2)]
            posf = [kb.sb(ph, f"posf{i}", [128, 512], F32) for i in range(2)]
            ang = kb.sb(ph, "ang", [128, 512], F32)
            ki = kb.sb(ph, "ki", [128, 512], I32)
            kf = kb.sb(ph, "kf", [128, 512], F32)
            rr = kb.sb(ph, "rr", [128, 512], F32)
            ab = kb.sb(ph, "ab", [128, 512], F32)
            s2 = kb.sb(ph, "s2", [128, 512], F32)
            c2 = kb.sb(ph, "c2", [128, 512], F32)
            sno = [kb.sb(ph, f"sno{i}", [128, 512], F32) for i in range(2)]
            cso = [kb.sb(ph, f"cso{i}", [128, 512], F32) for i in range(2)]
            n = 0
            for b in range(NB):
                pi_, pf = posi[b % 2], posf[b % 2]
                kb.dma(kb.sp, pi_[:, :], posb[:, b * 512:(b + 1) * 512], writes=[pi_])
                kb.op(kb.dve, lambda: nc.vector.tensor_copy(out=pf[:, :], in_=pi_[:, :]), reads=[pi_], writes=[pf])
                for ti in range(2):
                    sn_, cs_ = sno[n % 2], cso[n % 2]
                    n += 1
                    kb.op(kb.dve, lambda: nc.vector.tensor_scalar(out=ang[:, :], in0=pf[:, :], scalar1=cst[:, 128 + ti:129 + ti], scalar2=0.0, op0=ALU.mult, op1=ALU.add),
                          reads=[pf, cst], writes=[ang])
                    kb.op(kb.dve, lambda: nc.vector.tensor_scalar(out=ki[:, :], in0=ang[:, :], scalar1=float(1 / (2 * np.pi)), scalar2=0.0, op0=ALU.mult, op1=ALU.add),
                          reads=[ang], writes=[ki])
                    kb.op(kb.dve, lambda: nc.vector.tensor_copy(out=kf[:, :], in_=ki[:, :]), reads=[ki], writes=[kf])
                    kb.op(kb.dve, lambda: nc.vector.scalar_tensor_tensor(out=rr[:, :], in0=kf[:, :], scalar=-TWO_PI_HI, in1=ang[:, :], op0=ALU.mult, op1=ALU.add),
                          reads=[kf, ang], writes=[rr])
                    kb.op(kb.dve, lambda: nc.vector.scalar_tensor_tensor(out=rr[:, :], in0=kf[:, :], scalar=-TWO_PI_LO, in1=rr[:, :], op0=ALU.mult, op1=ALU.add),
                          reads=[kf, rr], writes=[rr])
                    kb.op(kb.dve, lambda: nc.vector.scalar_tensor_tensor(out=ab[:, :], in0=rr[:, :], scalar=-1.0, in1=rr[:, :], op0=ALU.mult, op1=ALU.max), reads=[rr], writes=[ab])
                    kb.op(kb.act, lambda: nc.scalar.activation(out=s2[:, :], in_=rr[:, :], func=AF.Sin, scale=0.5), reads=[rr], writes=[s2])
                    kb.op(kb.act, lambda: nc.scalar.activation(out=c2[:, :], in_=ab[:, :], func=AF.Sin, scale=-0.5, bias=halfpi[:, 0:1]),
                          reads=[ab, halfpi], writes=[c2])
                    kb.op(kb.dve, lambda: nc.vector.scalar_tensor_tensor(out=sn_[:, :], in0=s2[:, :], scalar=2.0, in1=c2[:, :], op0=ALU.mult, op1=ALU.mult),
                          reads=[s2, c2], writes=[sn_])
                    kb.op(kb.dve, lambda: nc.vector.tensor_tensor(out=cs_[:, :], in0=s2[:, :], in1=s2[:, :], op=ALU.mult), reads=[s2], writes=[cs_])
                    kb.op(kb.dve, lambda: nc.vector.tensor_scalar(out=cs_[:, :], in0=cs_[:, :], scalar1=-2.0, scalar2=1.0, op0=ALU.mult, op1=ALU.add),
                          reads=[cs_], writes=[cs_])
                    kb.dma(kb.pool, tabs[2 * ti][:, b * 512:(b + 1) * 512], cs_[:, :], reads=[cs_])
                    kb.dma(kb.pool, tabs[2 * ti + 1][:, b * 512:(b + 1) * 512], sn_[:, :], reads=[sn_])
        kb.barrier()

        def load_h(h32, b):
            kb.dma(kb.sp, h32[:, :, :], hT_t[:, :, b * 512:(b + 1) * 512].rearrange("c p t -> p c t"), reads=[hT[b]], writes=[h32])

        def rope_chunk(pp, p, psw, cs, sn, qbt, tA, tB, outap, outbuf):
            kb.op(kb.act, lambda: nc.scalar.copy(out=qbt[:, :], in_=p[:, :]), reads=[p], writes=[qbt])
            kb.op(kb.dve, lambda: nc.vector.tensor_tensor(out=tA[:, :], in0=cs[:, :], in1=p[:, :], op=ALU.mult), reads=[p, cs], writes=[tA])
            pp.put(p)
            p2 = pp.get()
            kb.op(kb.pe, lambda: nc.tensor.matmul(p2[:, :], lhsT=psw[:, :], rhs=qbt[:, :], start=True, stop=True), reads=[psw, qbt], writes=[p2])
            kb.op(kb.dve, lambda: nc.vector.tensor_tensor(out=tB[:, :], in0=sn[:, :], in1=p2[:, :], op=ALU.mult), reads=[p2, sn], writes=[tB])
            pp.put(p2)
            kb.op(kb.dve, lambda: nc.vector.tensor_tensor(out=outap, in0=tA[:, :], in1=tB[:, :], op=ALU.add), reads=[tA, tB], writes=[outbuf])


        def phase_e1a(li, j):
            with ExitStack() as ph:
                pp = PsPool(kb, ph, 8, f"e1a{li}_")
                wx = kb.sb(ph, f"wx{li}", [128, KC, 1536], BF16)
                wz = kb.sb(ph, f"wz{li}", [128, KC, 1040], BF16)
                h32s = [kb.sb(ph, f"sh32_{li}_{i}", [128, KC, 512], F32) for i in range(2)]
                hn = kb.sb(ph, f"shn{li}", [128, KC, 512], BF16)
                sq = kb.sb(ph, f"ssq{li}", [128, KC, 512], BF16)
                rstd = kb.sb(ph, f"srstd{li}", [128, 512], F32)
                cv = kb.sb(ph, f"scv{li}", [128, 60], F32)
                row = kb.sb(ph, f"srow{li}", [128, EVROW], F32)
                diag = kb.sb(ph, f"sdiag{li}", [128, 48, 128], BF16)
                aneg = kb.sb(ph, f"saneg{li}", [128, 16], F32)
                ones32 = kb.sb(ph, f"sones{li}", [128, 128], F32)
                onec = kb.sb(ph, f"sonec{li}", [128, 1], F32)
                nmb = kb.sb(ph, f"snmb{li}", [128, 4, 128], BF16)
                hs = kb.sb(ph, f"shs{li}", [128, 2, 512], F32)
                hsb = kb.sb(ph, f"shsb{li}", [128, 2, 512], BF16)
                xbcT = kb.sb(ph, f"sxbc{li}", [128, 12, 515], BF16)
                halo = kb.sb(ph, f"shalo{li}", [128, 12, 3], BF16)
                xcT = kb.sb(ph, f"sxc{li}", [128, 12, 512], BF16)
                zs = kb.sb(ph, f"szs{li}", [128, 1024], F32)
                sm = kb.sb(ph, f"ssm{li}", [128, 16, 16], F32)
                xdt = kb.sb(ph, f"sxdt{li}", [128, 1024], BF16)
                xD = kb.sb(ph, f"sxD{li}", [128, 1024], F32)
                xp = kb.sb(ph, f"sxp{li}", [128, 1024], BF16)
                btm = kb.sb(ph, f"sbtm{li}", [128, 256], BF16)
                daU = kb.sb(ph, f"sdaU{li}", [128, 16, 128], F32)
                Wd = kb.sb(ph, f"sWd{li}", [128, 16, 128], BF16)
                Mt = kb.sb(ph, f"sMt{li}", [128, 16, 128], BF16)
                t1 = kb.sb(ph, f"st1{li}", [128, 512], F32)
                yb = kb.sb(ph, f"syb{li}", [128, 1024], F32)
                yg = kb.sb(ph, f"syg{li}", [128, 1024], F32)
                ya = kb.sb(ph, f"sya{li}", [128, 1024], BF16)
                yaT = [kb.sb(ph, f"syaT{li}_{i}", [128, KC, 512], BF16) for i in range(2)]
                s1 = kb.sb(ph, f"ss1{li}", [128, 8], F32)
                wsrc = ev_wxz[j].rearrange("(k p) n -> p k n", p=128)
                cast_load(wx, wsrc[:, :, 0:1536], KC, 1536, h32s)
                cast_load(wz, wsrc[:, :, 1536:2576], KC, 1040, h32s)
                kb.dma(kb.sp, cv[:, :], ev_conv[j], writes=[cv])
                kb.dma(kb.sp, row[:, :], ev_row[j], writes=[row])
                for ck in range(48):
                    kb.op(kb.dve, lambda: nc.vector.tensor_scalar(out=diag[:, ck, :], in0=identb[:, :], scalar1=cv[:, ck:ck + 1], scalar2=0.0, op0=ALU.mult, op1=ALU.add),
                          reads=[identb, cv], writes=[diag])
                kb.op(kb.act, lambda: nc.scalar.activation(out=aneg[:, :], in_=row[:, 16:32], func=AF.Exp), reads=[row], writes=[aneg])
                kb.op(kb.dve, lambda: nc.vector.tensor_scalar(out=aneg[:, :], in0=aneg[:, :], scalar1=-1.0, scalar2=0.0, op0=ALU.mult, op1=ALU.add), reads=[aneg], writes=[aneg])
                kb.op(kb.dve, lambda: nc.vector.memset(ones32[:, :], 1.0), writes=[ones32])
                kb.op(kb.dve, lambda: nc.vector.memset(onec[:, :], 1.0), writes=[onec])
                for q in range(4):
                    kb.op(kb.dve, lambda: nc.vector.tensor_copy(out=nmb[:, q, :], in_=cst[:, 640:768]), reads=[cst], writes=[nmb])
                U32 = cst[:, 512:640]
                dtb, dsk, gn = row[:, 0:16], row[:, 32:48], row[:, 48:48 + 1024]
                SL = lambda k: sm[:, k, :]
                bc64 = lambda ap, nh: ap.unsqueeze(2).broadcast_to([128, nh, 64])
                load_h(h32s[0], 0)
                for b in range(NB):
                    h32 = h32s[b % 2]
                    if b + 1 < NB:
                        load_h(h32s[(b + 1) % 2], b + 1)
                    first = (b * 512) % SEQ == 0
                    if first:
                        kb.op(kb.dve, lambda: nc.vector.memset(hs[:, :, :], 0.0), writes=[hs])
                        kb.op(kb.dve, lambda: nc.vector.memset(hsb[:, :, :], 0.0), writes=[hsb])
                        kb.op(kb.dve, lambda: nc.vector.memset(xbcT[:, :, 0:3], 0.0), writes=[xbcT])
                    else:
                        kb.op(kb.dve, lambda: nc.vector.tensor_copy(out=xbcT[:, :, 0:3], in_=halo[:, :, :]), reads=[halo], writes=[xbcT])
                    rmsnorm_fm(pp, h32, nf[:, (5 + li) * KC:(6 + li) * KC], hn, sq, rstd)
                    for c in range(12):
                        p = pp.get()
                        for k in range(KC):
                            kb.op(kb.pe, lambda: nc.tensor.matmul(p[:, :], lhsT=wx[:, k, c * 128:(c + 1) * 128], rhs=hn[:, k, :], start=(k == 0), stop=(k == KC - 1)),
                                  reads=[wx, hn], writes=[p])
                        copy_on(kb.act if c % 2 else kb.dve, xbcT[:, c, 3:515], p[:, :], [p], [xbcT])
                        pp.put(p)
                    kb.op(kb.dve, lambda: nc.vector.tensor_copy(out=halo[:, :, :], in_=xbcT[:, :, 512:515]), reads=[xbcT], writes=[halo])
                    for c in range(12):
                        p = pp.get()
                        for k in range(4):
                            kb.op(kb.pe, lambda: nc.tensor.matmul(p[:, :], lhsT=diag[:, c * 4 + k, :], rhs=xbcT[:, c, k:k + 512], start=(k == 0), stop=(k == 3)),
                                  reads=[diag, xbcT], writes=[p])
                        kb.op(kb.act, lambda: nc.scalar.activation(out=xcT[:, c, :], in_=p[:, :], func=AF.Silu, bias=cv[:, 48 + c:49 + c]), reads=[p, cv], writes=[xcT])
                        pp.put(p)
                    yT = yaT[b % 2]
                    for i in range(4):
                        tc_ = slice(i * 128, (i + 1) * 128)
                        pz0, pz1, pdt = pp.get(), pp.get(), pp.get()
                        for (pz, c0, c1) in ((pz0, 0, 512), (pz1, 512, 1024), (pdt, 1024, 1040)):
                            for k in range(KC):
                                kb.op(kb.pe, lambda: nc.tensor.matmul(pz[:, 0:c1 - c0], lhsT=hn[:, k, tc_], rhs=wz[:, k, c0:c1], start=(k == 0), stop=(k == KC - 1)),
                                      reads=[wz, hn], writes=[pz])
                        kb.op(kb.act, lambda: nc.scalar.activation(out=zs[:, 0:512], in_=pz0[:, :], func=AF.Silu), reads=[pz0], writes=[zs])
                        kb.op(kb.act, lambda: nc.scalar.activation(out=zs[:, 512:1024], in_=pz1[:, :], func=AF.Silu), reads=[pz1], writes=[zs])
                        pp.put(pz0); pp.put(pz1)
                        kb.op(kb.dve, lambda: nc.vector.tensor_tensor(out=SL(0), in0=dtb, in1=pdt[:, 0:16], op=ALU.add), reads=[row, pdt], writes=[sm])
                        pp.put(pdt)
                        kb.op(kb.dve, lambda: nc.vector.scalar_tensor_tensor(out=SL(1), in0=SL(0), scalar=-1.0, in1=SL(0), op0=ALU.mult, op1=ALU.min), reads=[sm], writes=[sm])
                        kb.op(kb.act, lambda: nc.scalar.activation(out=SL(1), in_=SL(1), func=AF.Exp), reads=[sm], writes=[sm])
                        kb.op(kb.act, lambda: nc.scalar.activation(out=SL(1), in_=SL(1), func=AF.Ln, bias=onec[:, 0:1]), reads=[sm, onec], writes=[sm])
                        kb.op(kb.dve, lambda: nc.vector.scalar_tensor_tensor(out=SL(2), in0=SL(0), scalar=0.0, in1=SL(1), op0=ALU.max, op1=ALU.add), reads=[sm], writes=[sm])
                        kb.op(kb.dve, lambda: nc.vector.tensor_tensor(out=SL(3), in0=SL(2), in1=aneg[:, :], op=ALU.mult), reads=[sm, aneg], writes=[sm])
                        pa = pp.get()
                        kb.op(kb.pe, lambda: nc.tensor.matmul(pa[:, 0:16], lhsT=U32, rhs=SL(3), start=True, stop=True), reads=[cst, sm], writes=[pa])
                        kb.op(kb.pe, lambda: nc.tensor.matmul(pa[:, 16:32], lhsT=ones32[:, :], rhs=SL(3), start=True, stop=True), reads=[ones32, sm], writes=[pa])
                        kb.op(kb.dve, lambda: nc.vector.tensor_scalar(out=SL(4), in0=pa[:, 0:16], scalar1=-1.0, scalar2=0.0, op0=ALU.mult, op1=ALU.add), reads=[pa], writes=[sm])
                        kb.op(kb.act, lambda: nc.scalar.activation(out=SL(5), in_=pa[:, 0:16], func=AF.Exp), reads=[pa], writes=[sm])
                        kb.op(kb.dve, lambda: nc.vector.tensor_tensor(out=SL(6), in0=SL(4), in1=pa[:, 16:32], op=ALU.add), reads=[sm, pa], writes=[sm])
                        kb.op(kb.act, lambda: nc.scalar.activation(out=SL(6), in_=SL(6), func=AF.Exp), reads=[sm], writes=[sm])
                        kb.op(kb.act, lambda: nc.scalar.activation(out=SL(7), in_=pa[:, 16:32], func=AF.Exp), reads=[pa], writes=[sm])
                        pp.put(pa)
                        kb.op(kb.dve, lambda: nc.vector.tensor_tensor(out=SL(8), in0=SL(2), in1=SL(6), op=ALU.mult), reads=[sm], writes=[sm])
                        kb.op(kb.dve, lambda: nc.vector.tensor_tensor(out=daU[:, :, :], in0=U32.unsqueeze(1).broadcast_to([128, 16, 128]),
                                                                      in1=SL(3).unsqueeze(2).broadcast_to([128, 16, 128]), op=ALU.mult), reads=[cst, sm], writes=[daU])
                        px = pp.get()
                        pxv = px[:, :].bitcast(BF16)
                        for c in range(8):
                            kb.op(kb.pe, lambda: nc.tensor.transpose(out=pxv[:, c * 128:(c + 1) * 128], in_=xcT[:, c, tc_], identity=identb[:, :]), reads=[xcT, identb], writes=[px])
                        x3 = pxv[:, 0:1024].rearrange("p (h e) -> p h e", e=64)
                        v3 = lambda t: t[:, :].rearrange("p (h e) -> p h e", e=64)
                        kb.op(kb.dve, lambda: nc.vector.tensor_tensor(out=v3(xdt), in0=bc64(SL(2), 16), in1=x3, op=ALU.mult), reads=[sm, px], writes=[xdt])
                        kb.op(kb.dve, lambda: nc.vector.tensor_tensor(out=v3(xD), in0=bc64(dsk, 16), in1=x3, op=ALU.mult), reads=[row, px], writes=[xD])
                        kb.op(kb.dve, lambda: nc.vector.tensor_tensor(out=v3(xp), in0=bc64(SL(8), 16), in1=x3, op=ALU.mult), reads=[sm, px], writes=[xp])
                        pp.put(px)
                        pb = pp.get()
                        pbv = pb[:, :].bitcast(BF16)
                        for g in range(2):
                            kb.op(kb.pe, lambda: nc.tensor.transpose(out=pbv[:, g * 128:(g + 1) * 128], in_=xcT[:, 8 + g, tc_], identity=identb[:, :]), reads=[xcT, identb], writes=[pb])
                        copy_on(kb.act, btm[:, :], pbv[:, 0:256], [pb], [btm])
                        pp.put(pb)
                        for g in range(2):
                            hsl = slice(g * 8, (g + 1) * 8)
                            sg_ = [pp.get(), pp.get()]
                            for q in range(2):
                                kb.op(kb.pe, lambda: nc.tensor.matmul(sg_[q][:, :], lhsT=ones32[:, :], rhs=daU[:, g * 8 + q * 4:g * 8 + q * 4 + 4, :].rearrange("p h l -> p (h l)"), start=True, stop=False),
                                      reads=[ones32, daU], writes=[sg_[q]])
                                kb.op(kb.pe, lambda: nc.tensor.matmul(sg_[q][:, :], lhsT=identb[:, :], rhs=nmb[:, :, :].rearrange("p h l -> p (h l)"), start=False, stop=True),
                                      reads=[identb, nmb], writes=[sg_[q]])
                                for hh in range(4):
                                    h = g * 8 + q * 4 + hh
                                    kb.op(kb.act, lambda: nc.scalar.activation(out=Wd[:, h, :], in_=sg_[q][:, hh * 128:(hh + 1) * 128], func=AF.Exp, bias=sm[:, 4, h:h + 1]),
                                          reads=[sg_[q], sm], writes=[Wd])
                                pp.put(sg_[q])
                            pcb = pp.get()
                            kb.op(kb.pe, lambda: nc.tensor.matmul(pcb[:, 0:128], lhsT=xcT[:, 8 + g, tc_], rhs=xcT[:, 10 + g, tc_], start=True, stop=True), reads=[xcT], writes=[pcb])
                            kb.op(kb.dve, lambda: nc.vector.tensor_tensor(out=Mt[:, hsl, :], in0=Wd[:, hsl, :], in1=pcb[:, 0:128].unsqueeze(1).broadcast_to([128, 8, 128]), op=ALU.mult),
                                  reads=[Wd, pcb], writes=[Mt])
                            pp.put(pcb)
                            py, po = pp.get(), pp.get()
                            for hh in range(8):
                                h = g * 8 + hh
                                kb.op(kb.pe, lambda: nc.tensor.matmul(py[:, hh * 64:(hh + 1) * 64], lhsT=Mt[:, h, :], rhs=xdt[:, h * 64:(h + 1) * 64], start=True, stop=True),
                                      reads=[Mt, xdt], writes=[py])
                            kb.op(kb.pe, lambda: nc.tensor.matmul(po[:, :], lhsT=xcT[:, 10 + g, tc_], rhs=hsb[:, g, :], start=True, stop=True), reads=[xcT, hsb], writes=[po])
                            kb.op(kb.dve, lambda: nc.vector.tensor_tensor(out=t1[:, :].rearrange("p (h e) -> p h e", e=64), in0=bc64(sm[:, 5, hsl], 8),
                                                                          in1=po[:, :].rearrange("p (h e) -> p h e", e=64), op=ALU.mult), reads=[sm, po], writes=[t1])
                            pp.put(po)
                            kb.op(kb.pool, lambda: nc.gpsimd.tensor_tensor(out=t1[:, :], in0=t1[:, :], in1=xD[:, g * 512:(g + 1) * 512], op=ALU.add), reads=[t1, xD], writes=[t1])
                            kb.op(kb.dve, lambda: nc.vector.tensor_tensor(out=yb[:, g * 512:(g + 1) * 512], in0=t1[:, :], in1=py[:, :], op=ALU.add), reads=[t1, py], writes=[yb])
                            pp.put(py)
                            pst = pp.get()
                            kb.op(kb.pe, lambda: nc.tensor.matmul(pst[:, :], lhsT=btm[:, g * 128:(g + 1) * 128], rhs=xp[:, g * 512:(g + 1) * 512], start=True, stop=True),
                                  reads=[btm, xp], writes=[pst])
                            hs3 = hs[:, g, :].rearrange("p (h e) -> p h e", e=64)
                            kb.op(kb.dve, lambda: nc.vector.tensor_tensor(out=hs3, in0=hs3, in1=bc64(sm[:, 7, hsl], 8), op=ALU.mult), reads=[hs, sm], writes=[hs])
                            kb.op(kb.dve, lambda: nc.vector.tensor_tensor(out=hs[:, g, :], in0=hs[:, g, :], in1=pst[:, :], op=ALU.add), reads=[hs, pst], writes=[hs])
                            pp.put(pst)
                            copy_on(kb.act, hsb[:, g, :], hs[:, g, :], [hs], [hsb])
                        kb.op(kb.pool, lambda: nc.gpsimd.tensor_tensor(out=yg[:, :], in0=yb[:, :], in1=zs[:, :], op=ALU.mult), reads=[yb, zs], writes=[yg])
                        kb.op(kb.act, lambda: nc.scalar.activation(out=yb[:, :], in_=yg[:, :], func=AF.Square, accum_out=s1[:, 0:1]), reads=[yg], writes=[yb, s1])
                        kb.op(kb.act, lambda: nc.scalar.activation(out=s1[:, 1:2], in_=s1[:, 0:1], func=AF.Sqrt, scale=1.0 / 1024, bias=epsc[:, 0:1]), reads=[s1, epsc], writes=[s1])
                        kb.op(kb.dve, lambda: nc.vector.reciprocal(out=s1[:, 2:3], in_=s1[:, 1:2]), reads=[s1], writes=[s1])
                        kb.op(kb.dve, lambda: nc.vector.scalar_tensor_tensor(out=ya[:, :], in0=yg[:, :], scalar=s1[:, 2:3], in1=gn, op0=ALU.mult, op1=ALU.mult),
                              reads=[yg, s1, row], writes=[ya])
                        pt = pp.get()
                        ptv = pt[:, :].bitcast(BF16)
                        for c in range(8):
                            kb.op(kb.pe, lambda: nc.tensor.transpose(out=ptv[:, c * 128:(c + 1) * 128], in_=ya[:, c * 128:(c + 1) * 128], identity=identb[:, :]), reads=[ya, identb], writes=[pt])
                        copy_on(kb.act, yT[:, :, tc_], ptv[:, 0:1024].rearrange("p (c t) -> p c t", t=128), [pt], [yT])
                        pp.put(pt)
                    kb.dma(kb.pool, mixT[0:8, :, b * 512:(b + 1) * 512].rearrange("c p t -> p c t"), yT[:, :, :], reads=[yT])
            kb.barrier()


        def sincos(N, ang, angbuf, sn_ap, cs_ap, outs, tb):
            ki, kf, rr, ab, s2, c2 = tb
            kb.op(kb.dve, lambda: nc.vector.tensor_scalar(out=ki[:, 0:N], in0=ang, scalar1=float(1 / (2 * np.pi)), scalar2=0.0, op0=ALU.mult, op1=ALU.add), reads=[angbuf], writes=[ki])
            kb.op(kb.dve, lambda: nc.vector.tensor_copy(out=kf[:, 0:N], in_=ki[:, 0:N]), reads=[ki], writes=[kf])
            kb.op(kb.dve, lambda: nc.vector.scalar_tensor_tensor(out=rr[:, 0:N], in0=kf[:, 0:N], scalar=-TWO_PI_HI, in1=ang, op0=ALU.mult, op1=ALU.add), reads=[kf, angbuf], writes=[rr])
            kb.op(kb.dve, lambda: nc.vector.scalar_tensor_tensor(out=rr[:, 0:N], in0=kf[:, 0:N], scalar=-TWO_PI_LO, in1=rr[:, 0:N], op0=ALU.mult, op1=ALU.add), reads=[kf, rr], writes=[rr])
            kb.op(kb.dve, lambda: nc.vector.scalar_tensor_tensor(out=ab[:, 0:N], in0=rr[:, 0:N], scalar=-1.0, in1=rr[:, 0:N], op0=ALU.mult, op1=ALU.max), reads=[rr], writes=[ab])
            kb.op(kb.act, lambda: nc.scalar.activation(out=s2[:, 0:N], in_=rr[:, 0:N], func=AF.Sin, scale=0.5), reads=[rr], writes=[s2])
            kb.op(kb.act, lambda: nc.scalar.activation(out=c2[:, 0:N], in_=ab[:, 0:N], func=AF.Sin, scale=-0.5, bias=halfpi[:, 0:1]), reads=[ab, halfpi], writes=[c2])
            kb.op(kb.dve, lambda: nc.vector.scalar_tensor_tensor(out=sn_ap, in0=s2[:, 0:N], scalar=2.0, in1=c2[:, 0:N], op0=ALU.mult, op1=ALU.mult), reads=[s2, c2], writes=outs)
            kb.op(kb.dve, lambda: nc.vector.tensor_tensor(out=c2[:, 0:N], in0=s2[:, 0:N], in1=s2[:, 0:N], op=ALU.mult), reads=[s2], writes=[c2])
            kb.op(kb.dve, lambda: nc.vector.tensor_scalar(out=cs_ap, in0=c2[:, 0:N], scalar1=-2.0, scalar2=1.0, op0=ALU.mult, op1=ALU.add), reads=[c2], writes=outs)

        def tt(e, out, a, b, op, reads, writes):
            kb.op(e, lambda: e.e.tensor_tensor(out=out, in0=a, in1=b, op=op), reads=reads, writes=writes)

        def phase_o1a(li, j):
            with ExitStack() as ph:
                pp = PsPool(kb, ph, 8, f"o1a{li}_")
                wu = kb.sb(ph, f"wu5{li}", [128, KC, 512], BF16)
                h32s = [kb.sb(ph, f"fh32_{li}_{i}", [128, KC, 512], F32) for i in range(2)]
                hn = kb.sb(ph, f"fhn{li}", [128, KC, 512], BF16)
                rstd = kb.sb(ph, f"frstd{li}", [128, 512], F32)
                cosT = kb.sb(ph, f"fcos{li}", [128, 16, 512], F32)
                sinT = kb.sb(ph, f"fsin{li}", [128, 16, 512], F32)
                Btre = kb.sb(ph, f"fBre{li}", [128, 16, 128], BF16)
                Btim = kb.sb(ph, f"fBim{li}", [128, 16, 128], BF16)
                Cre = kb.sb(ph, f"fCre{li}", [128, 16, 128], BF16)
                nCre = kb.sb(ph, f"fnCre{li}", [128, 16, 128], BF16)
                nCim = kb.sb(ph, f"fnCim{li}", [128, 16, 128], BF16)
                wglu = kb.sb(ph, f"fwglu{li}", [128, 4, 512], BF16)
                pt = kb.sb(ph, f"fpt{li}", [128, 64], F32)
                sc = kb.sb(ph, f"fsc{li}", [128, 12, 16], F32)
                uT32 = kb.sb(ph, f"fu32{li}", [128, 4, 512], F32)
                uTb = kb.sb(ph, f"fub{li}", [128, 4, 512], BF16)
                g32 = kb.sb(ph, f"fg32{li}", [128, 4, 512], F32)
                gb = kb.sb(ph, f"fgb{li}", [128, 4, 512], BF16)
                sq = g32.view(g32.t[:, :, :].rearrange("p c t -> p (c t)").bitcast(BF16).rearrange("p (c t) -> p c t", t=512), "sq_alias")
                tmp = [kb.sb(ph, f"ftmp{li}_{i}", [128, 512], F32) for i in range(8)]
                cre = [kb.sb(ph, f"fcre{li}_{i}", [128, 512], F32) for i in range(1)] * 2
                cim = [kb.sb(ph, f"fcim{li}_{i}", [128, 512], F32) for i in range(1)] * 2
                wre = [kb.sb(ph, f"fwre{li}_{i}", [128, 512], F32) for i in range(2)]
                wim = [kb.sb(ph, f"fwim{li}_{i}", [128, 512], F32) for i in range(2)]
                pr = [kb.sb(ph, f"fpr{li}_{i}", [128, 4, 512], BF16) for i in range(2)]
                yco = [kb.sb(ph, f"fyc{li}_{i}", [128, 512], BF16) for i in range(1)] * 2
                iot = tmp[7].view(tmp[7].t[:, :], "iot_alias")
                kib = tmp[6].view(tmp[6].t[:, :].bitcast(I32), "kib_alias")
                cast_load(wu, od_wu[j].rearrange("(k p) n -> p k n", p=128), KC, 512, h32s)
                kb.dma(kb.sp, pt[:, :], od_pt[j], writes=[pt])
                kb.dma(kb.sp, iot[:, :], iota_in[:, :], writes=[iot])
                tb = (kib, tmp[0], tmp[1], tmp[2], tmp[3], tmp[4])
                S = lambda k: sc[:, k, :]
                kb.op(kb.act, lambda: nc.scalar.activation(out=S(0), in_=pt[:, 32:48], func=AF.Exp), reads=[pt], writes=[sc])
                tt(kb.dve, S(1), pt[:, 0:16], S(0), ALU.mult, [pt, sc], [sc])
                kb.op(kb.act, lambda: nc.scalar.activation(out=S(1), in_=S(1), func=AF.Exp), reads=[sc], writes=[sc])
                tt(kb.dve, S(2), pt[:, 16:32], S(0), ALU.mult, [pt, sc], [sc])
                for k in range(16):
                    kb.op(kb.dve, lambda: nc.vector.tensor_scalar(out=tmp[5][:, :], in0=iot[:, :], scalar1=sc[:, 2, k:k + 1], scalar2=0.0, op0=ALU.mult, op1=ALU.add),
                          reads=[iot, sc], writes=[tmp[5]])
                    sincos(512, tmp[5][:, :], tmp[5], sinT[:, k, :], cosT[:, k, :], [sinT, cosT], tb)
                kb.op(kb.dve, lambda: nc.vector.tensor_copy(out=S(3), in_=cosT[:, :, 511]), reads=[cosT], writes=[sc])
                kb.op(kb.dve, lambda: nc.vector.tensor_copy(out=S(4), in_=sinT[:, :, 511]), reads=[sinT], writes=[sc])
                st0 = h32s[0].t[:, :, :].rearrange("p c t -> p (c t)")
                st1 = h32s[1].t[:, :, :].rearrange("p c t -> p (c t)")
                R = lambda i: st0[:, i * 512:(i + 1) * 512]
                for q in range(4):
                    cols = slice(q * 512, (q + 1) * 512)
                    for w_ in range(3):
                        kb.dma(kb.sp, R(w_), od_rowp[j][:, w_ * 2048 + q * 512: w_ * 2048 + (q + 1) * 512], writes=[h32s[0]])
                    kb.dma(kb.sp, st1[:, 0:512].rearrange("p (k m) -> p k m", m=128), od_bt[j][0][:, q * 4:(q + 1) * 4, :], writes=[h32s[1]])
                    kb.dma(kb.sp, st1[:, 512:1024].rearrange("p (k m) -> p k m", m=128), od_bt[j][1][:, q * 4:(q + 1) * 4, :], writes=[h32s[1]])
                    H0, H1 = [h32s[0]], [h32s[1]]
                    kb.op(kb.act, lambda: nc.scalar.activation(out=R(2), in_=R(2), func=AF.Exp), reads=H0, writes=H0)
                    tt(kb.dve, R(3), R(0), R(2), ALU.mult, H0, H0)
                    kb.op(kb.act, lambda: nc.scalar.activation(out=R(3), in_=R(3), func=AF.Exp), reads=H0, writes=H0)
                    tt(kb.dve, R(4), R(1), R(2), ALU.mult, H0, H0)
                    sincos(512, R(4), h32s[0], R(5), R(6), H0, tb)
                    tt(kb.dve, R(5), R(5), R(3), ALU.mult, H0, H0)
                    tt(kb.dve, R(6), R(6), R(3), ALU.mult, H0, H0)
                    kb.op(kb.dve, lambda: nc.vector.tensor_scalar(out=R(6), in0=R(6), scalar1=1.0, scalar2=-1.0, op0=ALU.mult, op1=ALU.add), reads=H0, writes=H0)
                    tt(kb.dve, R(2), R(6), R(0), ALU.mult, H0, H0)
                    tt(kb.dve, R(3), R(5), R(1), ALU.mult, H0, H0)
                    tt(kb.dve, R(2), R(2), R(3), ALU.add, H0, H0)
                    tt(kb.dve, R(3), R(5), R(0), ALU.mult, H0, H0)
                    tt(kb.dve, R(4), R(6), R(1), ALU.mult, H0, H0)
                    tt(kb.dve, R(3), R(3), R(4), ALU.subtract, H0, H0)
                    tt(kb.dve, R(4), R(0), R(0), ALU.mult, H0, H0)
                    tt(kb.dve, R(5), R(1), R(1), ALU.mult, H0, H0)
                    tt(kb.dve, R(4), R(4), R(5), ALU.add, H0, H0)
                    kb.op(kb.dve, lambda: nc.vector.reciprocal(out=R(4), in_=R(4)), reads=H0, writes=H0)
                    tt(kb.dve, R(2), R(2), R(4), ALU.mult, H0, H0)
                    tt(kb.dve, R(3), R(3), R(4), ALU.mult, H0, H0)
                    bre, bim = st1[:, 0:512], st1[:, 512:1024]
                    T_ = lambda i: st1[:, 1024 + i * 512:1024 + (i + 1) * 512]
                    tt(kb.dve, T_(0), R(2), bre, ALU.mult, H0 + H1, H1)
                    tt(kb.dve, T_(1), R(3), bim, ALU.mult, H0 + H1, H1)
                    tt(kb.dve, Btre[:, q * 4:(q + 1) * 4, :].rearrange("p k m -> p (k m)"), T_(0), T_(1), ALU.subtract, H1, [Btre])
                    tt(kb.dve, T_(0), R(2), bim, ALU.mult, H0 + H1, H1)
                    tt(kb.dve, T_(1), R(3), bre, ALU.mult, H0 + H1, H1)
                    tt(kb.dve, Btim[:, q * 4:(q + 1) * 4, :].rearrange("p k m -> p (k m)"), T_(0), T_(1), ALU.add, H1, [Btim])
                for w_, dsts in ((0, (Cre, nCre)), (1, (None, nCim))):
                    for q in range(2):
                        kb.dma(kb.sp, st0[:, 0:1024].rearrange("p (k m) -> p k m", m=128), od_ct[j][w_][:, q * 8:(q + 1) * 8, :], writes=[h32s[0]])
                        src = st0[:, 0:1024].rearrange("p (k m) -> p k m", m=128)
                        if dsts[0] is not None:
                            copy_on(kb.act, dsts[0][:, q * 8:(q + 1) * 8, :], src, [h32s[0]], [dsts[0]])
                        kb.op(kb.dve, lambda: nc.vector.tensor_scalar(out=dsts[1][:, q * 8:(q + 1) * 8, :], in0=src, scalar1=-1.0, scalar2=0.0, op0=ALU.mult, op1=ALU.add),
                              reads=[h32s[0]], writes=[dsts[1]])
                cast_load(wglu, od_wglu[j].rearrange("(k p) n -> p k n", p=128), 4, 512, h32s)
                dskp, bglu = pt[:, 48:52], pt[:, 52:56]
                load_h(h32s[0], 0)
                nt = 0
                for b in range(NB):
                    h32 = h32s[b % 2]
                    if b + 1 < NB:
                        load_h(h32s[(b + 1) % 2], b + 1)
                    first = (b * 512) % SEQ == 0
                    rmsnorm_fm(pp, h32, nf[:, (5 + li) * KC:(6 + li) * KC], hn, sq, rstd)
                    for c in range(4):
                        p = pp.get()
                        for k in range(KC):
                            kb.op(kb.pe, lambda: nc.tensor.matmul(p[:, :], lhsT=wu[:, k, c * 128:(c + 1) * 128], rhs=hn[:, k, :], start=(k == 0), stop=(k == KC - 1)),
                                  reads=[wu, hn], writes=[p])
                        copy_on(kb.act, uT32[:, c, :], p[:, :], [p], [uT32])
                        copy_on(kb.dve, uTb[:, c, :], p[:, :], [p], [uTb])
                        pp.put(p)
                    py = None
                    for k in range(16):
                        cc, r = k // 4, k % 4
                        rs = slice(32 * r, 32 * r + 32) if r < 3 else slice(64, 128)
                        pre, pim = pp.get(), pp.get()
                        kb.op(kb.pe, lambda: nc.tensor.matmul(pre[:, :], lhsT=Btre[rs, k, :], rhs=uTb[rs, cc, :], start=True, stop=True), reads=[Btre, uTb], writes=[pre])
                        kb.op(kb.pe, lambda: nc.tensor.matmul(pim[:, :], lhsT=Btim[rs, k, :], rhs=uTb[rs, cc, :], start=True, stop=True), reads=[Btim, uTb], writes=[pim])
                        i2 = nt % 2
                        nt += 1
                        t1, t2, t3, t4 = tmp[4 * i2:4 * i2 + 4]
                        ck, sk = cosT[:, k, :], sinT[:, k, :]
                        tt(kb.dve, t1[:, :], ck, pre[:, :], ALU.mult, [cosT, pre], [t1])
                        tt(kb.dve, t2[:, :], sk, pim[:, :], ALU.mult, [sinT, pim], [t2])
                        tt(kb.dve, t3[:, :], ck, pim[:, :], ALU.mult, [cosT, pim], [t3])
                        tt(kb.dve, t4[:, :], sk, pre[:, :], ALU.mult, [sinT, pre], [t4])
                        pp.put(pre); pp.put(pim)
                        tt(kb.pool, cre[i2][:, :], t1[:, :], t2[:, :], ALU.add, [t1, t2], [cre[i2]])
                        tt(kb.pool, cim[i2][:, :], t3[:, :], t4[:, :], ALU.subtract, [t3, t4], [cim[i2]])
                        rho_b = sc[:, 1, k:k + 1].broadcast_to([128, 512])
                        ini_re = 0.0 if first else sc[:, 7, k:k + 1]
                        ini_im = 0.0 if first else sc[:, 8, k:k + 1]
                        kb.op(kb.dve, lambda: nc.vector.tensor_tensor_scan(out=wre[i2][:, :], data0=rho_b, data1=cre[i2][:, :], initial=ini_re, op0=ALU.mult, op1=ALU.add),
                              reads=[sc, cre[i2]], writes=[wre[i2]])
                        kb.op(kb.dve, lambda: nc.vector.tensor_tensor_scan(out=wim[i2][:, :], data0=rho_b, data1=cim[i2][:, :], initial=ini_im, op0=ALU.mult, op1=ALU.add),
                              reads=[sc, cim[i2]], writes=[wim[i2]])
                        copy_on(kb.act, sc[:, 5, k:k + 1], wre[i2][:, 511:512], [wre[i2]], [sc])
                        copy_on(kb.act, sc[:, 6, k:k + 1], wim[i2][:, 511:512], [wim[i2]], [sc])
                        prb = pr[i2]
                        tt(kb.pool, prb[:, 0, :], ck, wre[i2][:, :], ALU.mult, [cosT, wre[i2]], [prb])
                        tt(kb.pool, prb[:, 1, :], sk, wim[i2][:, :], ALU.mult, [sinT, wim[i2]], [prb])
                        tt(kb.dve, prb[:, 2, :], sk, wre[i2][:, :], ALU.mult, [sinT, wre[i2]], [prb])
                        tt(kb.dve, prb[:, 3, :], ck, wim[i2][:, :], ALU.mult, [cosT, wim[i2]], [prb])
                        if r == 0:
                            py = pp.get()
                        for m_, lh in enumerate((Cre, nCre, nCim, nCim)):
                            kb.op(kb.pe, lambda: nc.tensor.matmul(py[:, :], lhsT=lh[:, k, :], rhs=prb[:, m_, :], start=(r == 0 and m_ == 0), stop=(r == 3 and m_ == 3)),
                                  reads=[lh, prb], writes=[py])
                        if r == 3:
                            yv, x2, inn = tmp[0], tmp[1], tmp[2]
                            kb.op(kb.dve, lambda: nc.vector.scalar_tensor_tensor(out=yv[:, :], in0=uT32[:, cc, :], scalar=dskp[:, cc:cc + 1], in1=py[:, :], op0=ALU.mult, op1=ALU.add),
                                  reads=[uT32, pt, py], writes=[yv])
                            pp.put(py)
                            tt(kb.pool, x2[:, :], yv[:, :], yv[:, :], ALU.mult, [yv], [x2])
                            kb.op(kb.pool, lambda: nc.gpsimd.tensor_scalar(out=x2[:, :], in0=x2[:, :], scalar1=0.044715, scalar2=1.0, op0=ALU.mult, op1=ALU.add), reads=[x2], writes=[x2])
                            tt(kb.pool, inn[:, :], x2[:, :], yv[:, :], ALU.mult, [x2, yv], [inn])
                            kb.op(kb.act, lambda: nc.scalar.activation(out=inn[:, :], in_=inn[:, :], func=AF.Sigmoid, scale=float(2.0 * np.sqrt(2.0 / np.pi))), reads=[inn], writes=[inn])
                            tt(kb.pool, g32[:, cc, :], inn[:, :], yv[:, :], ALU.mult, [inn, yv], [g32])
                            copy_on(kb.act, gb[:, cc, :], g32[:, cc, :], [g32], [gb])
                    tt(kb.dve, S(9), S(3), S(5), ALU.mult, [sc], [sc])
                    tt(kb.dve, S(10), S(4), S(6), ALU.mult, [sc], [sc])
                    tt(kb.dve, S(11), S(4), S(5), ALU.mult, [sc], [sc])
                    tt(kb.dve, S(7), S(9), S(10), ALU.subtract, [sc], [sc])
                    tt(kb.dve, S(9), S(3), S(6), ALU.mult, [sc], [sc])
                    tt(kb.dve, S(8), S(11), S(9), ALU.add, [sc], [sc])
                    for c2 in range(4):
                        p = pp.get()
                        for k in range(4):
                            kb.op(kb.pe, lambda: nc.tensor.matmul(p[:, :], lhsT=wglu[:, k, c2 * 128:(c2 + 1) * 128], rhs=gb[:, k, :], start=(k == 0), stop=(k == 3)),
                                  reads=[wglu, gb], writes=[p])
                        sgt = tmp[3]
                        kb.op(kb.act, lambda: nc.scalar.activation(out=sgt[:, :], in_=p[:, :], func=AF.Sigmoid, bias=bglu[:, c2:c2 + 1]), reads=[p, pt], writes=[sgt])
                        pp.put(p)
                        yo = yco[c2 % 2]
                        tt(kb.pool, yo[:, :], sgt[:, :], g32[:, c2, :], ALU.mult, [sgt, g32], [yo])
                        kb.dma(kb.pool, mixT[c2][:, b * 512:(b + 1) * 512], yo[:, :], reads=[yo])
            kb.barrier()

        def phase_o1b(li, j):
            with ExitStack() as ph:
                pp = PsPool(kb, ph, 8, f"o1b{li}_")
                wf = kb.sb(ph, f"gwf{li}", [128, KC, 768], BF16)
                wuq = kb.sb(ph, f"gwuq{li}", [128, 3, 1024], BF16)
                wuk = kb.sb(ph, f"gwuk{li}", [128, 2, 1024], BF16)
                wuv = kb.sb(ph, f"gwuv{li}", [128, 2, 512], BF16)
                h32s = [kb.sb(ph, f"gh32_{li}_{i}", [128, KC, 512], F32) for i in range(2)]
                hn = kb.sb(ph, f"ghn{li}", [128, KC, 512], BF16)
                sq = kb.sb(ph, f"gsq{li}", [128, KC, 512], BF16)
                rstd = kb.sb(ph, f"grstd{li}", [128, 512], F32)
                rs2 = kb.sb(ph, f"grs2{li}", [128, 512], F32)
                nrm = kb.sb(ph, f"gnrm{li}", [128, 8], F32)
                c32 = kb.sb(ph, f"gc32{li}", [128, 5, 512], F32)
                cn = kb.sb(ph, f"gcn{li}", [128, 5, 512], BF16)
                css = [kb.sb(ph, f"gcs{li}_{i}", [128, 512], F32) for i in range(2)]
                sns = [kb.sb(ph, f"gsn{li}_{i}", [128, 512], F32) for i in range(2)]
                qbt = [kb.sb(ph, f"gqb{li}_{i}", [128, 512], BF16) for i in range(2)]
                tA = [kb.sb(ph, f"gtA{li}_{i}", [128, 512], F32) for i in range(2)]
                tB = [kb.sb(ph, f"gtB{li}_{i}", [128, 512], F32) for i in range(2)]
                qr = [kb.sb(ph, f"gqr{li}_{i}", [128, 512], BF16) for i in range(3)]
                krot = kb.sb(ph, f"gkr{li}", [128, 512], BF16)
                vt = [kb.sb(ph, f"gvt{li}_{i}", [128, 512], BF16) for i in range(2)]
                cast_load(wf, od_wf[j].rearrange("(k p) n -> p k n", p=128), KC, 768, h32s)
                cast_load(wuq, od_wuq[j].rearrange("(k p) n -> p k n", p=128), 3, 1024, h32s)
                cast_load(wuk, od_wuk[j].rearrange("(k p) n -> p k n", p=128), 2, 1024, h32s)
                cast_load(wuv, od_wuv[j].rearrange("(k p) n -> p k n", p=128), 2, 512, h32s)
                kb.dma(kb.sp, nrm[:, 0:5], od_nrm[j], writes=[nrm])
                load_h(h32s[0], 0)
                n = 0
                for b in range(NB):
                    h32 = h32s[b % 2]
                    cs, sn = css[b % 2], sns[b % 2]
                    kb.dma(kb.sp, cs[:, :], tabs[2][:, b * 512:(b + 1) * 512], writes=[cs])
                    kb.dma(kb.sp, sn[:, :], tabs[3][:, b * 512:(b + 1) * 512], writes=[sn])
                    if b + 1 < NB:
                        load_h(h32s[(b + 1) % 2], b + 1)
                    rmsnorm_fm(pp, h32, nf[:, (5 + li) * KC:(6 + li) * KC], hn, sq, rstd)
                    for c in range(5):
                        p = pp.get()
                        for k in range(KC):
                            kb.op(kb.pe, lambda: nc.tensor.matmul(p[:, :], lhsT=wf[:, k, c * 128:(c + 1) * 128], rhs=hn[:, k, :], start=(k == 0), stop=(k == KC - 1)),
                                  reads=[wf, hn], writes=[p])
                        copy_on(kb.dve, c32[:, c, :], p[:, :], [p], [c32])
                        kb.op(kb.act, lambda: nc.scalar.activation(out=sq[:, c, :], in_=p[:, :], func=AF.Square), reads=[p], writes=[sq])
                        pp.put(p)
                    for (c0, c1, rs_) in ((0, 3, rstd), (3, 5, rs2)):
                        p = pp.get()
                        for c in range(c0, c1):
                            kb.op(kb.pe, lambda: nc.tensor.matmul(p[:, :], lhsT=onesb[:, :], rhs=sq[:, c, :], start=(c == c0), stop=(c == c1 - 1)), reads=[onesb, sq], writes=[p])
                        kb.op(kb.act, lambda: nc.scalar.activation(out=rs_[:, :], in_=p[:, :], func=AF.Sqrt, scale=1.0 / (128 * (c1 - c0)), bias=epsc[:, 0:1]), reads=[p, epsc], writes=[rs_])
                        pp.put(p)
                        kb.op(kb.dve, lambda: nc.vector.reciprocal(out=rs_[:, :], in_=rs_[:, :]), reads=[rs_], writes=[rs_])
                        for c in range(c0, c1):
                            kb.op(kb.dve, lambda: nc.vector.scalar_tensor_tensor(out=cn[:, c, :], in0=c32[:, c, :], scalar=nrm[:, c:c + 1], in1=rs_[:, :], op0=ALU.mult, op1=ALU.mult),
                                  reads=[c32, nrm, rs_], writes=[cn])
                    p = pp.get()
                    for k in range(KC):
                        kb.op(kb.pe, lambda: nc.tensor.matmul(p[:, :], lhsT=wf[:, k, 640:768], rhs=hn[:, k, :], start=(k == 0), stop=(k == KC - 1)), reads=[wf, hn], writes=[p])
                    rope_chunk(pp, p, psw32, cs, sn, qbt[n % 2], tA[n % 2], tB[n % 2], krot[:, :], krot)
                    n += 1
                    for h in range(8):
                        p = pp.get()
                        for c in range(3):
                            kb.op(kb.pe, lambda: nc.tensor.matmul(p[:, :], lhsT=wuq[:, c, h * 128:(h + 1) * 128], rhs=cn[:, c, :], start=(c == 0), stop=(c == 2)), reads=[wuq, cn], writes=[p])
                        o = qr[n % 3]
                        rope_chunk(pp, p, psw32, cs, sn, qbt[n % 2], tA[n % 2], tB[n % 2], o[:, :], o)
                        n += 1
                        kb.dma(kb.pool, qT[h][:, b * 512:(b + 1) * 512], o[:, :], reads=[o])
                    for h in range(8):
                        p = pp.get()
                        for c in range(2):
                            kb.op(kb.pe, lambda: nc.tensor.matmul(p[:, :], lhsT=wuk[:, c, h * 128:(h + 1) * 128], rhs=cn[:, 3 + c, :], start=(c == 0), stop=(c == 1)), reads=[wuk, cn], writes=[p])
                        o = qr[n % 3]
                        n += 1
                        copy_on(kb.act, o[0:64, :], p[0:64, :], [p], [o])
                        pp.put(p)
                        copy_on(kb.dve, o[64:128, :], krot[64:128, :], [krot], [o])
                        kb.dma(kb.pool, kT[h][:, b * 512:(b + 1) * 512], o[:, :], reads=[o])
                    for i in range(4):
                        p = pp.get()
                        for c in range(2):
                            kb.op(kb.pe, lambda: nc.tensor.matmul(p[:, :], lhsT=cn[:, 3 + c, i * 128:(i + 1) * 128], rhs=wuv[:, c, :], start=(c == 0), stop=(c == 1)), reads=[wuv, cn], writes=[p])
                        v = vt[i % 2]
                        copy_on(kb.act if i % 2 else kb.dve, v[:, :], p[:, :], [p], [v])
                        pp.put(p)
                        kb.dma(kb.pool, vtm[b * 512 + i * 128:b * 512 + (i + 1) * 128, 0:512], v[:, :], reads=[v])
            kb.barrier()

        def phase_e1b(li, j):
            with ExitStack() as ph:
                pp = PsPool(kb, ph, 8, f"e1b{li}_")
                wqk = kb.sb(ph, f"wqk{li}", [128, KC, 2048], BF16)
                wv = kb.sb(ph, f"wv{li}", [128, KC, 1024], BF16)
                h32s = [kb.sb(ph, f"eh32_{li}_{i}", [128, KC, 512], F32) for i in range(2)]
                hn = kb.sb(ph, f"ehn{li}", [128, KC, 512], BF16)
                sq = kb.sb(ph, f"esq{li}", [128, KC, 512], BF16)
                rstd = kb.sb(ph, f"erstd{li}", [128, 512], F32)
                css = [kb.sb(ph, f"ecs{li}_{i}", [128, 512], F32) for i in range(2)]
                sns = [kb.sb(ph, f"esn{li}_{i}", [128, 512], F32) for i in range(2)]
                qbt = [kb.sb(ph, f"eqb{li}_{i}", [128, 512], BF16) for i in range(2)]
                tA = [kb.sb(ph, f"etA{li}_{i}", [128, 512], F32) for i in range(2)]
                tB = [kb.sb(ph, f"etB{li}_{i}", [128, 512], F32) for i in range(2)]
                qr = [kb.sb(ph, f"eqr{li}_{i}", [128, 512], BF16) for i in range(3)]
                vt = [kb.sb(ph, f"evt{li}_{i}", [128, 1024], BF16) for i in range(2)]
                cast_load(wqk, ev_wqk[j].rearrange("(k p) n -> p k n", p=128), KC, 2048, h32s)
                cast_load(wv, ev_wv[j].rearrange("(k p) n -> p k n", p=128), KC, 1024, h32s)
                load_h(h32s[0], 0)
                n = 0
                for b in range(NB):
                    h32 = h32s[b % 2]
                    cs, sn = css[b % 2], sns[b % 2]
                    kb.dma(kb.sp, cs[:, :], tabs[0][:, b * 512:(b + 1) * 512], writes=[cs])
                    kb.dma(kb.sp, sn[:, :], tabs[1][:, b * 512:(b + 1) * 512], writes=[sn])
                    if b + 1 < NB:
                        load_h(h32s[(b + 1) % 2], b + 1)
                    rmsnorm_fm(pp, h32, nf[:, (5 + li) * KC:(6 + li) * KC], hn, sq, rstd)
                    for c in range(16 if cfg.get('dbg', 3) >= 2 else 0):
                        p = pp.get()
                        for k in range(KC):
                            kb.op(kb.pe, lambda: nc.tensor.matmul(p[:, :], lhsT=wqk[:, k, c * 128:(c + 1) * 128], rhs=hn[:, k, :],
                                                                  start=(k == 0), stop=(k == KC - 1)), reads=[wqk, hn], writes=[p])
                        o = qr[n % 3]
                        if cfg.get('dbg2', 0) == 1:
                            copy_on(kb.act, o[:, :], p[:, :], [p], [o])
                            pp.put(p)
                        else:
                            rope_chunk(pp, p, psw64, cs, sn, qbt[n % 2], tA[n % 2], tB[n % 2], o[:, :], o)
                        n += 1
                        dst = (qT if c < 8 else kT)[c % 8]
                        kb.dma(kb.pool, dst[:, b * 512:(b + 1) * 512], o[:, :], reads=[o])
                    for i in range(4 if cfg.get('dbg', 3) >= 3 else 0):
                        v = vt[i % 2]
                        for half in range(2):
                            p = pp.get()
                            for k in range(KC):
                                kb.op(kb.pe, lambda: nc.tensor.matmul(p[:, :], lhsT=hn[:, k, i * 128:(i + 1) * 128], rhs=wv[:, k, half * 512:(half + 1) * 512],
                                                                      start=(k == 0), stop=(k == KC - 1)), reads=[wv, hn], writes=[p])
                            copy_on(kb.act if half else kb.dve, v[:, half * 512:(half + 1) * 512], p[:, :], [p], [v])
                            pp.put(p)
                        kb.dma(kb.pool, vtm[b * 512 + i * 128:b * 512 + (i + 1) * 128, :], v[:, :], reads=[v])
            kb.barrier()

        def phase_att(li, j, kind):
            diff = kind == "diff"
            dv = 128 if diff else 64
            nmap = 2 if diff else 1
            rows = 64 if diff else 128
            scale = 64 ** -0.5 if diff else 96 ** -0.5
            NKT = SEQ // 128
            NQB = SEQ // 512
            lam_init = 0.8 - 0.6 * float(np.exp(-0.3 * li))
            with ExitStack() as ph:
                sp_ = [kb.ps(ph, f"as{li}_{i}", [128, 512], F32) for i in range(2)]
                ob = [kb.ps(ph, f"ao{li}_{i}", [128, 512], F32) for i in range(4)]
                tp = [kb.ps(ph, f"at{li}_{i}", [128, 512], F32) for i in range(2)]
                qts = [kb.sb(ph, f"aq{li}_{i}", [128, SEQ], BF16) for i in range(2)]
                kts = [kb.sb(ph, f"ak{li}_{i}", [128, SEQ], BF16) for i in range(2)]
                vts = [kb.sb(ph, f"av{li}_{i}", [128, NKT, dv + 1], BF16) for i in range(2)]
                pts = [kb.sb(ph, f"ap{li}_{i}", [128, 512], BF16) for i in range(3)]
                o1 = [kb.sb(ph, f"ao1{li}_{i}", [128, 128], F32) for i in range(4)]
                of = [kb.sb(ph, f"aof{li}_{i}", [128, 128], F32) for i in range(2)]
                obf = [kb.sb(ph, f"aob{li}_{i}", [128, 128], BF16) for i in range(2)]
                junk = kb.sb(ph, f"ajk{li}", [128, 128], F32)
                junk2 = kb.sb(ph, f"ajk2{li}", [128, 128], F32)
                sm = [kb.sb(ph, f"asm{li}_{i}", [128, 8], F32) for i in range(4)]
                ybt = [kb.sb(ph, f"ayb{li}_{i}", [128, 512], BF16) for i in range(2)]
                for v in vts:
                    kb.op(kb.dve, lambda: nc.vector.memset(v[:, :, dv:dv + 1], 1.0), writes=[v])
                if diff:
                    row = kb.sb(ph, f"arow{li}", [128, EVROW], F32)
                    kb.dma(kb.sp, row[:, :], ev_row[j], writes=[row])
                    lw = kb.sb(ph, f"alw{li}", [128, 128], F32)
                    lam = kb.sb(ph, f"alam{li}", [128, 8], F32)
                    srow = kb.sb(ph, f"asrow{li}", [128, 128], F32)
                    L0 = 48 + 1024
                    kb.op(kb.dve, lambda: nc.vector.tensor_tensor(out=lw[:, 0:64], in0=row[:, L0:L0 + 64], in1=row[:, L0 + 64:L0 + 128], op=ALU.mult), reads=[row], writes=[lw])
                    kb.op(kb.dve, lambda: nc.vector.tensor_tensor(out=lw[:, 64:128], in0=row[:, L0 + 128:L0 + 192], in1=row[:, L0 + 192:L0 + 256], op=ALU.mult), reads=[row, lw], writes=[lw])
                    kb.op(kb.dve, lambda: nc.vector.reduce_sum(out=lam[:, 0:1], in_=lw[:, 0:64], axis=AX.X), reads=[lw], writes=[lam])
                    kb.op(kb.dve, lambda: nc.vector.reduce_sum(out=lam[:, 1:2], in_=lw[:, 64:128], axis=AX.X), reads=[lw, lam], writes=[lam])
                    kb.op(kb.act, lambda: nc.scalar.activation(out=lam[:, 2:4], in_=lam[:, 0:2], func=AF.Exp), reads=[lam], writes=[lam])
                    kb.op(kb.dve, lambda: nc.vector.scalar_tensor_tensor(out=lam[:, 4:5], in0=lam[:, 3:4], scalar=-lam_init, in1=lam[:, 2:3], op0=ALU.add, op1=ALU.subtract),
                          reads=[lam], writes=[lam])
                    nlam = lam[:, 4:5]
                    kb.op(kb.dve, lambda: nc.vector.tensor_scalar(out=srow[:, :], in0=row[:, L0 + 256:L0 + 384], scalar1=float(1.0 - lam_init), scalar2=0.0, op0=ALU.mult, op1=ALU.add),
                          reads=[row], writes=[srow])
                nh = 8
                it = 0
                npt = 0
                for s_ in range(NSEQ):
                    base = s_ * SEQ
                    for h in range(nh):
                        qt, kt_, vt = qts[it % 2], kts[it % 2], vts[it % 2]
                        it += 1
                        kb.dma(kb.sp, qt[:, :], qT[h][:, base:base + SEQ], writes=[qt])
                        kb.dma(kb.sp, kt_[:, :], kT[h][:, base:base + SEQ], writes=[kt_])
                        vsrc = vtm[base:base + SEQ, h * dv:(h + 1) * dv] if diff else vtm[base:base + SEQ, h * dv:(h + 1) * dv]
                        vsrc3 = vsrc.rearrange("(kt p) e -> p kt e", p=128)
                        for k0 in range(0, NKT, 8):
                            k1 = min(NKT, k0 + 8)
                            kb.dma(kb.sp, vt[:, k0:k1, 0:dv], vsrc3[:, k0:k1, :], writes=[vt])
                        for qb in range(NQB):
                            for mp in range(nmap):
                                r0 = mp * rows
                                nkt = 4 * qb + 4
                                def s_mm(kt):
                                    c0 = max(0, kt - 4 * qb)
                                    sp = sp_[kt % 2]
                                    kb.op(kb.pe, lambda: nc.tensor.matmul(sp[:, c0 * 128:512], lhsT=kt_[r0:r0 + rows, kt * 128:(kt + 1) * 128],
                                                                          rhs=qt[r0:r0 + rows, qb * 512 + c0 * 128:(qb + 1) * 512], start=True, stop=True),
                                          reads=[kt_, qt], writes=[sp])
                                s_mm(0)
                                for kt in range(nkt):
                                    if kt + 1 < nkt:
                                        s_mm(kt + 1)
                                    c0 = max(0, kt - 4 * qb)
                                    sp = sp_[kt % 2]
                                    pt = pts[npt % 3]
                                    npt += 1
                                    kb.op(kb.act, lambda: nc.scalar.activation(out=pt[:, c0 * 128:512], in_=sp[:, c0 * 128:512], func=AF.Exp, scale=float(scale)),
                                          reads=[sp], writes=[pt])
                                    if kt >= 4 * qb:
                                        ii = kt - 4 * qb
                                        kb.op(kb.pool, lambda: nc.gpsimd.memset(pt[64:128, ii * 128:ii * 128 + 64], 0.0), writes=[pt])
                                    for ii in range(c0, 4):
                                        kb.op(kb.pe, lambda: nc.tensor.matmul(ob[ii][:, 0:dv + 1], lhsT=pt[:, ii * 128:(ii + 1) * 128], rhs=vt[:, kt, :],
                                                                              start=(kt == 0), stop=(kt == 4 * qb + ii)), reads=[pt, vt], writes=[ob[ii]])
                                for ii in range(4):
                                    O = ob[ii]
                                    smi = sm[ii]
                                    kb.op(kb.dve, lambda: nc.vector.reciprocal(out=smi[:, mp:mp + 1], in_=O[:, dv:dv + 1]), reads=[O], writes=[smi])
                                    if diff and mp == 0:
                                        kb.op(kb.dve, lambda: nc.vector.tensor_scalar(out=o1[ii][:, :], in0=O[:, 0:dv], scalar1=smi[:, 0:1], scalar2=0.0, op0=ALU.mult, op1=ALU.add),
                                              reads=[O, smi], writes=[o1[ii]])
                                        continue
                                    tpb = tp[(it + qb) % 2]
                                    tpv = tpb[:, :].bitcast(BF16)
                                    if diff:
                                        f, fb = of[ii % 2], obf[ii % 2]
                                        kb.op(kb.dve, lambda: nc.vector.tensor_tensor(out=smi[:, 2:3], in0=smi[:, 1:2], in1=nlam, op=ALU.mult), reads=[smi, lam], writes=[smi])
                                        kb.op(kb.dve, lambda: nc.vector.tensor_scalar(out=junk[:, :], in0=O[:, 0:dv], scalar1=smi[:, 2:3], scalar2=0.0, op0=ALU.mult, op1=ALU.add),
                                              reads=[O, smi], writes=[junk])
                                        kb.op(kb.dve, lambda: nc.vector.tensor_tensor(out=f[:, :], in0=junk[:, :], in1=o1[ii][:, :], op=ALU.add),
                                              reads=[junk, o1[ii]], writes=[f])
                                        kb.op(kb.act, lambda: nc.scalar.activation(out=junk2[:, :], in_=f[:, :], func=AF.Square, accum_out=smi[:, 3:4]),
                                              reads=[f], writes=[junk2, smi])
                                        kb.op(kb.act, lambda: nc.scalar.activation(out=smi[:, 4:5], in_=smi[:, 3:4], func=AF.Sqrt, scale=1.0 / 128, bias=epsc[:, 0:1]),
                                              reads=[smi, epsc], writes=[smi])
                                        kb.op(kb.dve, lambda: nc.vector.reciprocal(out=smi[:, 5:6], in_=smi[:, 4:5]), reads=[smi], writes=[smi])
                                        kb.op(kb.dve, lambda: nc.vector.scalar_tensor_tensor(out=fb[:, :], in0=f[:, :], scalar=smi[:, 5:6], in1=srow[:, :],
                                                                                             op0=ALU.mult, op1=ALU.mult), reads=[f, smi, srow], writes=[fb])
                                        kb.op(kb.pe, lambda: nc.tensor.transpose(out=tpv[:, ii * 128:(ii + 1) * 128], in_=fb[:, :], identity=identb[:, :]),
                                              reads=[fb, identb], writes=[tpb])
                                    else:
                                        fb = obf[ii % 2]
                                        kb.op(kb.dve, lambda: nc.vector.tensor_scalar(out=fb[:, 0:dv], in0=O[:, 0:dv], scalar1=smi[:, 0:1], scalar2=0.0, op0=ALU.mult, op1=ALU.add),
                                              reads=[O, smi], writes=[fb])
                                        po = (h % 2) * 64
                                        kb.op(kb.pe, lambda: nc.tensor.transpose(out=tpv[po:po + 64, ii * 128:(ii + 1) * 128], in_=fb[:, 0:dv], identity=identb[:, :]),
                                              reads=[fb, identb], writes=[tpb])
                                if mp == nmap - 1:
                                    tpb = tp[(it + qb) % 2]
                                    tpv = tpb[:, :].bitcast(BF16)
                                    y = ybt[qb % 2]
                                    if diff:
                                        copy_on(kb.act, y[:, :], tpv[:, 0:512], [tpb], [y])
                                        kb.dma(kb.pool, mixT[8 + h][:, base + qb * 512:base + (qb + 1) * 512], y[:, :], reads=[y])
                                    else:
                                        po = (h % 2) * 64
                                        copy_on(kb.act, y[po:po + 64, :], tpv[po:po + 64, 0:512], [tpb], [y])
                                        kb.dma(kb.pool, mixT[4 + h // 2][po:po + 64, base + qb * 512:base + (qb + 1) * 512], y[po:po + 64, :], reads=[y])
            kb.barrier()

        def phase_op(li, w_src, lo, hi):
            nk = hi - lo
            with ExitStack() as ph:
                pp = PsPool(kb, ph, 4, f"op{li}_")
                wo = kb.sb(ph, f"wo{li}", [128, nk, D], BF16)
                h32s = [kb.sb(ph, f"oh32_{li}_{i}", [128, KC, 512], F32) for i in range(2)]
                mxs = [kb.sb(ph, f"omx{li}_{i}", [128, nk, 512], BF16) for i in range(2)]
                cast_load(wo, w_src.rearrange("(k p) n -> p k n", p=128)[:, lo:hi, :], nk, D, h32s)

                def loads(b):
                    load_h(h32s[b % 2], b)
                    kb.dma(kb.sp, mxs[b % 2][:, :, :], mixT[lo:hi, :, b * 512:(b + 1) * 512].rearrange("c p t -> p c t"), writes=[mxs[b % 2]])
                loads(0)
                for b in range(NB):
                    h32, mx = h32s[b % 2], mxs[b % 2]
                    if b + 1 < NB:
                        loads(b + 1)
                    for oc in range(KC):
                        p = pp.get()
                        for k in range(nk):
                            kb.op(kb.pe, lambda: nc.tensor.matmul(p[:, :], lhsT=wo[:, k, oc * 128:(oc + 1) * 128], rhs=mx[:, k, :],
                                                                  start=(k == 0), stop=(k == nk - 1)), reads=[wo, mx], writes=[p])
                        kb.op(kb.dve, lambda: nc.vector.tensor_tensor(out=h32[:, oc, :], in0=h32[:, oc, :], in1=p[:, :], op=ALU.add),
                              reads=[h32, p], writes=[h32])
                        pp.put(p)
                    kb.dma(kb.pool, hT_t[:, :, b * 512:(b + 1) * 512].rearrange("c p t -> p c t"), h32[:, :, :], reads=[h32], writes=[hT[b]])
            kb.barrier()

        def phase_ff(li, last):
            with ExitStack() as ph:
                pp = PsPool(kb, ph, 8, f"ff{li}_")
                wg = kb.sb(ph, f"wg{li}", [128, KC, FH], BF16)
                wu = kb.sb(ph, f"wu{li}", [128, KC, FH], BF16)
                wd = kb.sb(ph, f"wd{li}", [128, HC, D], BF16)
                actb = kb.sb(ph, f"actb{li}", [128, HC, 512], BF16)
                h32s = [kb.sb(ph, f"h32_{li}_{i}", [128, KC, 512], F32) for i in range(2)]
                hn = kb.sb(ph, f"hn{li}", [128, KC, 512], BF16)
                rstd = kb.sb(ph, f"rstd{li}", [128, 512], F32)
                sg = [kb.sb(ph, f"sg{li}_{i}", [128, 512], F32) for i in range(2)]
                stg = [Buf(h32s[i].t[:, :, :].rearrange("p c t -> p (c t)"), f"stg{i}") for i in range(2)]
                stg = [h32s[0], h32s[1]]

                def stage_ap(b, ncol):
                    return b.t[:, :, :].rearrange("p c t -> p (c t)")[:, 0:ncol]

                rot = 0
                for (w, src, kcn, n) in ((wg, ffn_g[li].rearrange("(k p) n -> p k n", p=128), KC, FH),
                                         (wu, ffn_u[li].rearrange("(k p) n -> p k n", p=128), KC, FH),
                                         (wd, ffn_d[li].rearrange("(k p) n -> p k n", p=128), HC, D)):
                    per = max(1, 4096 // n)
                    for k0 in range(0, kcn, per):
                        k1 = min(kcn, k0 + per)
                        st = stg[rot % 2]
                        sap = stage_ap(st, (k1 - k0) * n).rearrange("p (k n) -> p k n", n=n)
                        kb.dma(kb.sp if rot % 2 == 0 else kb.pool, sap, src[:, k0:k1, :], writes=[st])
                        if rot % 2:
                            kb.op(kb.act, lambda: nc.scalar.copy(out=w[:, k0:k1, :], in_=sap), reads=[st], writes=[w])
                        else:
                            kb.op(kb.dve, lambda: nc.vector.tensor_copy(out=w[:, k0:k1, :], in_=sap), reads=[st], writes=[w])
                        rot += 1

                load_h(h32s[0], 0)
                for b in range(NB):
                    h32 = h32s[b % 2]
                    if b + 1 < NB:
                        load_h(h32s[(b + 1) % 2], b + 1)
                    rmsnorm_fm(pp, h32, nf[:, li * KC:(li + 1) * KC], hn, actb, rstd)
                    for hc in range(HC):
                        pg, pu = pp.get(), pp.get()
                        for k in range(KC):
                            kb.op(kb.pe, lambda: nc.tensor.matmul(pg[:, :], lhsT=wg[:, k, hc * 128:(hc + 1) * 128], rhs=hn[:, k, :],
                                                                  start=(k == 0), stop=(k == KC - 1)), reads=[wg, hn], writes=[pg])
                        for k in range(KC):
                            kb.op(kb.pe, lambda: nc.tensor.matmul(pu[:, :], lhsT=wu[:, k, hc * 128:(hc + 1) * 128], rhs=hn[:, k, :],
                                                                  start=(k == 0), stop=(k == KC - 1)), reads=[wu, hn], writes=[pu])
                        s = sg[hc % 2]
                        kb.op(kb.act, lambda: nc.scalar.activation(out=s[:, :], in_=pg[:, :], func=AF.Silu), reads=[pg], writes=[s])
                        pp.put(pg)
                        kb.op(kb.dve, lambda: nc.vector.tensor_tensor(out=actb[:, hc, :], in0=s[:, :], in1=pu[:, :], op=ALU.mult),
                              reads=[s, pu], writes=[actb])
                        pp.put(pu)
                    for oc in range(KC):
                        p = pp.get()
                        for k in range(HC):
                            kb.op(kb.pe, lambda: nc.tensor.matmul(p[:, :], lhsT=wd[:, k, oc * 128:(oc + 1) * 128], rhs=actb[:, k, :],
                                                                  start=(k == 0), stop=(k == HC - 1)), reads=[wd, actb], writes=[p])
                        kb.op(kb.dve, lambda: nc.vector.tensor_tensor(out=h32[:, oc, :], in0=h32[:, oc, :], in1=p[:, :], op=ALU.add),
                              reads=[h32, p], writes=[h32])
                        pp.put(p)
                    if not last:
                        kb.dma(kb.pool, hT_t[:, :, b * 512:(b + 1) * 512].rearrange("c p t -> p c t"), h32[:, :, :],
                               reads=[h32], writes=[hT[b]])
                    else:
                        rmsnorm_fm(pp, h32, nf[:, 4 * KC:5 * KC], hn, actb, rstd)
                        for c in range(KC):
                            kb.op(kb.dve, lambda: nc.vector.scalar_tensor_tensor(out=h32[:, c, :], in0=h32[:, c, :], scalar=nf[:, 4 * KC + c:4 * KC + c + 1],
                                                                                 in1=rstd[:, :], op0=ALU.mult, op1=ALU.mult),
                                  reads=[h32, rstd, nf], writes=[h32])
                        for i in range(4):
                            for half in range(2):
                                p = pp.get()
                                for cc in range(4):
                                    c = half * 4 + cc
                                    kb.op(kb.pe, lambda: nc.tensor.transpose(out=p[:, cc * 128:(cc + 1) * 128], in_=h32[:, c, i * 128:(i + 1) * 128],
                                                                             identity=ident32), reads=[h32, cst], writes=[p])
                                o = sg[half]
                                kb.op(kb.act if half else kb.dve,
                                      (lambda: nc.scalar.copy(out=o[:, :], in_=p[:, :])) if half else (lambda: nc.vector.tensor_copy(out=o[:, :], in_=p[:, :])),
                                      reads=[p], writes=[o])
                                pp.put(p)
                                kb.dma(kb.pool, out[b * 512 + i * 128: b * 512 + (i + 1) * 128, half * 512:(half + 1) * 512], o[:, :],
                                       reads=[o], writes=[outb])
            kb.barrier()

        for li in layers:
            j = li // 2
            upto = cfg.get("upto", "all")
            if li % 2 == 0:
                if cfg.get("ssd", True):
                    phase_e1a(li, j)
                    if not cfg.get("att", True):
                        phase_op(li, ev_wout[j], 0, 8)
                if cfg.get("att", True):
                    if upto in ("e1b", "att", "op", "all"):
                        phase_e1b(li, j)
                    if upto in ("att", "op", "all"):
                        phase_att(li, j, "diff")
                    lo = 0 if cfg.get("ssd", True) else 8
                    if upto in ("op", "all"):
                        phase_op(li, ev_wout[j], lo, 16)
            if li % 2 == 1:
                s5, mla = cfg.get("s5", True), cfg.get("mla", True)
                if s5:
                    phase_o1a(li, j)
                if mla:
                    phase_o1b(li, j)
                    phase_att(li, j, "mla")
                if s5 or mla:
                    phase_op(li, od_wout[j], 0 if s5 else 4, 8 if mla else 4)
            if cfg.get("ffn", True):
                phase_ff(li, li == layers[-1])

        kb.barrier()
    return nc, kb


def host_consts():
    c = np.zeros((128, 1024), np.float32)
    c[:, 0:128] = np.eye(128, dtype=np.float32)
    p = np.arange(128)
    th = np.float32(10000.0)
    c[:, 128] = th ** (-(p % 32).astype(np.float32) * np.float32(2.0 / 64))
    f32 = th ** (-((p - 64) % 16).astype(np.float32) * np.float32(2.0 / 32))
    c[:, 129] = np.where((p >= 64) & (p < 96), f32, 0.0)
    for m in range(128):
        if (m % 64) < 32:
            c[m + 32, 256 + m] = -1.0
        else:
            c[m - 32, 256 + m] = 1.0
    for m in range(64, 80):
        c[m + 16, 384 + m] = -1.0
    for m in range(80, 96):
        c[m - 16, 384 + m] = 1.0
    t = np.arange(128)
    c[:, 512:640] = (t[:, None] <= t[None, :]).astype(np.float32)
    c[:, 640:768] = np.where(t[:, None] > t[None, :], -30000.0, 0.0)
    return c


def fm_cols(v, nch):
    return np.ascontiguousarray(np.asarray(v, np.float32).reshape(nch, 128).T)


def rep(v):
    v = np.asarray(v, np.float32).reshape(1, -1)
    return np.ascontiguousarray(np.broadcast_to(v, (128, v.shape[1])))


CFG = dict(T=8192, SEQ=4096, layers=[0, 1, 2, 3])


def make_in_maps(inputs, cfg, ncores):
    T = cfg["T"]
    f = lambda k: np.asarray(inputs[k], np.float32)
    x = f("x").reshape(-1, D)
    pos = np.asarray(inputs["positions"], np.int32).reshape(-1)
    nfn = np.concatenate([fm_cols(inputs["norm_ffn"][i], KC) for i in range(4)], axis=1)
    nmx = np.concatenate([fm_cols(inputs["norm_mix"][i], KC) for i in range(4)], axis=1)
    wi = f("ev_w_in")
    shared = {
        "consts": host_consts(),
        "norm_ffn": np.ascontiguousarray(nfn),
        "norm_mix": np.ascontiguousarray(nmx),
        "norm_final": fm_cols(inputs["norm_final"], KC),
        "ffn_gate": f("ffn_gate"), "ffn_up": f("ffn_up"), "ffn_down": f("ffn_down"),
        "ev_wqk": np.ascontiguousarray(wi[:, :, 2576:4624]),
        "ev_wv": np.ascontiguousarray(wi[:, :, 4624:5648]),
        "ev_row": np.stack([np.concatenate([rep(inputs["ev_dt_bias"][j]), rep(inputs["ev_a_log"][j]), rep(inputs["ev_d_skip"][j]),
                                            rep(inputs["ev_gate_norm"][j]), rep(np.asarray(inputs["ev_lambdas"][j]).reshape(-1)),
                                            rep(inputs["ev_subln"][j])], axis=1) for j in range(2)]),
        "ev_wout": f("ev_w_out"),
        "ev_wxz": np.ascontiguousarray(np.concatenate([wi[:, :, 1024:2560], wi[:, :, 0:1024], wi[:, :, 2560:2576]], axis=2)),
        "ev_conv": np.stack([np.concatenate([np.ascontiguousarray(np.asarray(inputs["ev_conv_w"][j], np.float32).T).reshape(12, 128, 4).transpose(1, 0, 2).reshape(128, 48),
                                             fm_cols(inputs["ev_conv_b"][j], 12)], axis=1) for j in range(2)]),
    }
    wo = f("od_w_in")
    krpad = np.zeros((2, D, 128), np.float32)
    krpad[:, :, 64:96] = wo[:, :, 1152:1184]
    shared["iota"] = np.ascontiguousarray(np.broadcast_to(np.arange(1, 513, dtype=np.float32)[None, :], (128, 512)))
    shared["od_wu"] = np.ascontiguousarray(wo[:, :, 0:512])
    shared["od_wf"] = np.ascontiguousarray(np.concatenate([wo[:, :, 512:1152], krpad], axis=2))
    def pt_layout(v):
        return np.ascontiguousarray(np.asarray(v, np.float32).reshape(16, 128).T)
    od_pt, od_rowp, od_bt, od_ct, od_nrm, wuq_p, wuk_p, wuv_p = [], [], [], [], [], [], [], []
    for j in range(2):
        lre, lim = f("od_lam_re")[j], f("od_lam_im")[j]
        stp = np.repeat(f("od_log_step")[j][:, None], 64, axis=1)
        od_pt.append(np.concatenate([pt_layout(lre), pt_layout(lim), pt_layout(stp), fm_cols(inputs["od_d_skip"][j], 4), fm_cols(inputs["od_b_glu"][j], 4),
                                     np.zeros((128, 8), np.float32)], axis=1))
        od_rowp.append(np.concatenate([rep(lre.reshape(-1)), rep(lim.reshape(-1)), rep(stp.reshape(-1))], axis=1))
        bts, cts = [], []
        for src in (f("od_b_re")[j], f("od_b_im")[j]):
            bt = np.zeros((128, 16, 128), np.float32)
            for g in range(32):
                k, r = g // 2, (g // 2) % 4
                rows = slice(32 * r + (g % 2) * 16, 32 * r + (g % 2) * 16 + 16)
                bt[rows, k, (g % 2) * 64:(g % 2) * 64 + 64] = src[g].T
            bts.append(bt)
        for src in (f("od_c_re")[j], f("od_c_im")[j]):
            ct = np.zeros((128, 16, 128), np.float32)
            for g in range(32):
                k = g // 2
                ct[(g % 2) * 64:(g % 2) * 64 + 64, k, (g % 8) * 16:(g % 8) * 16 + 16] = src[g].T
            cts.append(ct)
        od_bt.append(np.stack(bts)); od_ct.append(np.stack(cts))
        od_nrm.append(np.concatenate([fm_cols(inputs["od_q_norm"][j], 3), fm_cols(inputs["od_kv_norm"][j], 2)], axis=1))
        uq = f("od_w_uq")[j].reshape(384, 8, 96)
        uqp = np.zeros((384, 8, 128), np.float32); uqp[:, :, 0:96] = uq
        ukv = f("od_w_ukv")[j].reshape(256, 8, 128)
        ukp = np.zeros((256, 8, 128), np.float32); ukp[:, :, 0:64] = ukv[:, :, 0:64]
        wuq_p.append(uqp.reshape(384, 1024)); wuk_p.append(ukp.reshape(256, 1024))
        wuv_p.append(np.ascontiguousarray(ukv[:, :, 64:128]).reshape(256, 512))
    shared.update({"od_pt": np.stack(od_pt), "od_rowp": np.stack(od_rowp), "od_bt": np.stack(od_bt), "od_ct": np.stack(od_ct),
                   "od_nrm": np.stack(od_nrm), "od_wuq": np.stack(wuq_p), "od_wuk": np.stack(wuk_p), "od_wuv": np.stack(wuv_p),
                   "od_wglu": f("od_w_glu"), "od_wout": f("od_w_out")})
    maps = []
    for c in range(ncores):
        m = dict(shared)
        m["x"] = np.ascontiguousarray(x[c * T:(c + 1) * T])
        m["posb"] = np.ascontiguousarray(np.broadcast_to(pos[None, c * T:(c + 1) * T], (128, T)))
        maps.append(m)
    return maps


def kernel(**inputs):
    cfg = CFG
    nc, kb = build(cfg)
    maps = make_in_maps(inputs, cfg, NCORES)
    res = run_bass_kernel_spmd(nc, maps, core_ids=list(range(NCORES)))
    outs = [np.asarray(r["out"], np.float32) for r in res.results]
    B, S = inputs["x"].shape[0], inputs["x"].shape[1]
    return np.concatenate(outs, axis=0).reshape(B, S, D)
```
